# Optimizing a Trainium2 kernel written in Bass

```python
import math
import jax, jax.numpy as jnp
from jax import lax
import numpy as np

D_MODEL = 2048
BATCH = 8
SEQ = 2048
DEPTH = 4
DEC_BATCH = 16
DEC_SEQ = 64
PAST_LEN = 4096

CHUNK = 64
N_MIXERS = 3
N_SSM_LAYERS = (DEPTH + 2) // 3
N_FOX_LAYERS = (DEPTH + 1) // 3
N_BAND_LAYERS = DEPTH // 3
D_FF = 5632
EPS = 1e-6
SSM_GROUP = 16
SSM_GROUPS = D_MODEL // SSM_GROUP
SSM_STATE = 64
N_HEADS = 16
HEAD_DIM = D_MODEL // N_HEADS
ATTN_SCALE = HEAD_DIM ** -0.5
Q_BLOCK = 128
FORGET_BIAS_INIT = 3.0
BAND_PREV = 8
BAND_WINDOW = BAND_PREV * CHUNK
REL_CLIP = 256
N_REL = 2 * REL_CLIP + 1
NEG_INF = -1e30

kernel_name = 'hybrid_s5_fox_chunkband_streaming_step'


def rmsnorm(x, g):
    xf = x.astype(jnp.float32)
    y = xf * lax.rsqrt(jnp.mean(xf * xf, axis=-1, keepdims=True) + EPS)
    return (y * g.astype(jnp.float32)).astype(x.dtype)


def swiglu(x, w_in, w_out):
    g, u = jnp.split(x @ w_in, 2, axis=-1)
    return (jax.nn.silu(g) * u) @ w_out


def _split_heads(t):
    return t.reshape(t.shape[0], t.shape[1], N_HEADS, HEAD_DIM)


def ssm_discretize(a_re, a_im, log_step, b_re, b_im):
    f32 = jnp.float32
    a_re, a_im = a_re.astype(f32), a_im.astype(f32)
    dt = jnp.exp(log_step.astype(f32))[:, None]
    mag = jnp.exp(a_re * dt)
    ab_re, ab_im = mag * jnp.cos(a_im * dt), mag * jnp.sin(a_im * dt)
    den = a_re * a_re + a_im * a_im
    z_re = ((ab_re - 1.0) * a_re + ab_im * a_im) / den
    z_im = (ab_im * a_re - (ab_re - 1.0) * a_im) / den
    b_re, b_im = b_re.astype(f32), b_im.astype(f32)
    bb_re = z_re[..., None] * b_re - z_im[..., None] * b_im
    bb_im = z_re[..., None] * b_im + z_im[..., None] * b_re
    return dt, a_re, a_im, ab_re, ab_im, bb_re, bb_im


def _ssm_combine(e1, e2):
    a1r, a1i, b1r, b1i = e1
    a2r, a2i, b2r, b2i = e2
    return (a1r * a2r - a1i * a2i, a1r * a2i + a1i * a2r,
            a2r * b1r - a2i * b1i + b2r, a2r * b1i + a2i * b1r + b2i)


def ssm_mixer(h, s0_re, s0_im, disc, c_re, c_im, d_skip, w_glu):
    dt, a_re, a_im, ab_re, ab_im, bb_re, bb_im = disc
    B, L, _ = h.shape
    f32 = jnp.float32
    hf = h.astype(f32)
    u = hf.reshape(B, L, SSM_GROUPS, SSM_GROUP)
    bu_re = jnp.einsum('blgc,gpc->blgp', u, bb_re)
    bu_im = jnp.einsum('blgc,gpc->blgp', u, bb_im)
    shp = bu_re.shape
    elems = (jnp.broadcast_to(ab_re, shp), jnp.broadcast_to(ab_im, shp), bu_re, bu_im)
    _, _, s_re, s_im = lax.associative_scan(_ssm_combine, elems, axis=1)
    if s0_re is not None:
        t = jnp.arange(1, L + 1, dtype=f32)[:, None, None]
        mag = jnp.exp(a_re * dt * t)
        ang = a_im * dt * t
        p_re, p_im = mag * jnp.cos(ang), mag * jnp.sin(ang)
        s0r = s0_re.astype(f32)[:, None]
        s0i = s0_im.astype(f32)[:, None]
        s_re = s_re + p_re * s0r - p_im * s0i
        s_im = s_im + p_re * s0i + p_im * s0r
    y = (jnp.einsum('blgp,gcp->blgc', s_re, c_re.astype(f32))
         - jnp.einsum('blgp,gcp->blgc', s_im, c_im.astype(f32)))
    y = y.reshape(B, L, D_MODEL) + d_skip.astype(f32) * hf
    g = jax.nn.gelu(y).astype(h.dtype)
    val, gate = jnp.split(g @ w_glu, 2, axis=-1)
    return val * jax.nn.sigmoid(gate), s_re[:, -1], s_im[:, -1]


def fox_project(h, w_in, b_f):
    proj = h @ w_in
    q = _split_heads(proj[..., :D_MODEL])
    k = _split_heads(proj[..., D_MODEL:2 * D_MODEL])
    v = _split_heads(proj[..., 2 * D_MODEL:3 * D_MODEL])
    logf = jax.nn.log_sigmoid(proj[..., 3 * D_MODEL:].astype(jnp.float32) + b_f.astype(jnp.float32))
    return q, k, v, logf


def fox_prompt(h, w_in, b_f, w_out):
    B, L, _ = h.shape
    q, k, v, logf = fox_project(h, w_in, b_f)
    c = jnp.cumsum(logf, axis=1).transpose(0, 2, 1)
    nb = L // Q_BLOCK
    q_blocks = q.reshape(B, nb, Q_BLOCK, N_HEADS, HEAD_DIM).transpose(1, 0, 2, 3, 4)
    k_pos = jnp.arange(L)

    def attend_block(args):
        n, q_n = args
        start = n * Q_BLOCK
        q_pos = start + jnp.arange(Q_BLOCK)
        c_q = lax.dynamic_slice_in_dim(c, start, Q_BLOCK, axis=2)
        s = jnp.einsum('bqhd,bkhd->bhqk', q_n, k, preferred_element_type=jnp.float32) * ATTN_SCALE
        s = s + c_q[..., :, None] - c[..., None, :]
        s = jnp.where(k_pos[None, :] <= q_pos[:, None], s, NEG_INF)
        p = jax.nn.softmax(s, axis=-1).astype(v.dtype)
        return jnp.einsum('bhqk,bkhd->bqhd', p, v)

    o = lax.map(attend_block, (jnp.arange(nb), q_blocks))
    o = o.transpose(1, 0, 2, 3, 4).reshape(B, L, D_MODEL)
    return o @ w_out, k, v, logf


def fox_sample(h, k_cache, v_cache, logf_cache, w_in, b_f, w_out):
    B, L, _ = h.shape
    P = k_cache.shape[1]
    q, k, v, logf = fox_project(h, w_in, b_f)
    k_all = jnp.concatenate([k_cache.astype(k.dtype), k], axis=1)
    v_all = jnp.concatenate([v_cache.astype(v.dtype), v], axis=1)
    c = jnp.cumsum(jnp.concatenate([logf_cache.astype(jnp.float32), logf], axis=1), axis=1)
    c = c.transpose(0, 2, 1)
    s = jnp.einsum('bqhd,bkhd->bhqk', q, k_all, preferred_element_type=jnp.float32) * ATTN_SCALE
    s = s + c[..., P:, None] - c[..., None, :]
    mask = jnp.arange(P + L)[None, :] <= (P + jnp.arange(L))[:, None]
    s = jnp.where(mask, s, NEG_INF)
    p = jax.nn.softmax(s, axis=-1).astype(v.dtype)
    o = jnp.einsum('bhqk,bkhd->bqhd', p, v_all).reshape(B, L, D_MODEL)
    return o @ w_out, k, v, logf


def band_project(h, w_in):
    proj = h @ w_in
    return (_split_heads(proj[..., :D_MODEL]), _split_heads(proj[..., D_MODEL:2 * D_MODEL]),
            _split_heads(proj[..., 2 * D_MODEL:]))


def rel_bias_block(rel_bias, q_pos, k_pos):
    idx = jnp.clip(q_pos[:, None] - k_pos[None, :], -REL_CLIP, REL_CLIP) + REL_CLIP
    return rel_bias.astype(jnp.float32)[:, idx]


def band_prompt(h, w_in, rel_bias, w_out):
    B, L, _ = h.shape
    nc = L // CHUNK
    q, k, v = band_project(h, w_in)
    shp = (B, nc, CHUNK, N_HEADS, HEAD_DIM)
    qc = q.reshape(shp)
    pad = jnp.zeros((B, BAND_PREV, CHUNK, N_HEADS, HEAD_DIM), k.dtype)
    kp = jnp.concatenate([pad, k.reshape(shp)], axis=1)
    vp = jnp.concatenate([pad, v.reshape(shp)], axis=1)
    band_idx = jnp.arange(nc)[:, None] + jnp.arange(BAND_PREV + 1)[None, :]
    n_band = (BAND_PREV + 1) * CHUNK
    kb = kp[:, band_idx].reshape(B, nc, n_band, N_HEADS, HEAD_DIM)
    vb = vp[:, band_idx].reshape(B, nc, n_band, N_HEADS, HEAD_DIM)
    s = jnp.einsum('bnqhd,bnkhd->bnhqk', qc, kb, preferred_element_type=jnp.float32) * ATTN_SCALE
    bias = rel_bias_block(rel_bias, jnp.arange(CHUNK), jnp.arange(n_band) - BAND_PREV * CHUNK)
    valid = jnp.repeat(band_idx >= BAND_PREV, CHUNK, axis=1)
    s = jnp.where(valid[None, :, None, None, :], s + bias[None, None], NEG_INF)
    p = jax.nn.softmax(s, axis=-1).astype(v.dtype)
    o = jnp.einsum('bnhqk,bnkhd->bnqhd', p, vb).reshape(B, L, D_MODEL)
    keep = min(BAND_WINDOW, L)
    return o @ w_out, k[:, L - keep:], v[:, L - keep:]


def band_sample(h, k_cache, v_cache, w_in, rel_bias, w_out):
    B, L, _ = h.shape
    Lc = k_cache.shape[1]
    q, k, v = band_project(h, w_in)
    k_all = jnp.concatenate([k_cache.astype(k.dtype), k], axis=1)
    v_all = jnp.concatenate([v_cache.astype(v.dtype), v], axis=1)
    s = jnp.einsum('bqhd,bkhd->bhqk', q, k_all, preferred_element_type=jnp.float32) * ATTN_SCALE
    s = s + rel_bias_block(rel_bias, Lc + jnp.arange(L), jnp.arange(Lc + L))[None]
    p = jax.nn.softmax(s, axis=-1).astype(v.dtype)
    o = jnp.einsum('bhqk,bkhd->bqhd', p, v_all).reshape(B, L, D_MODEL)
    return o @ w_out, k, v


def setup_inputs(seed: int = 0) -> dict:
    key = jax.random.key(seed)
    ks = iter(jax.random.split(key, 40))
    f32 = jnp.float32

    def nrm(shape, std):
        return std * jax.random.normal(next(ks), shape, f32)

    D, H, HD = D_MODEL, N_HEADS, HEAD_DIM
    band_rows = min(BAND_WINDOW, PAST_LEN)
    inp = {}
    inp['x_prompt'] = nrm((BATCH, SEQ, D), 1.0)
    inp['x_sample'] = nrm((DEC_BATCH, DEC_SEQ, D), 1.0)
    inp['state_ssm_re'] = nrm((N_SSM_LAYERS, DEC_BATCH, SSM_GROUPS, SSM_STATE), 0.3)
    inp['state_ssm_im'] = nrm((N_SSM_LAYERS, DEC_BATCH, SSM_GROUPS, SSM_STATE), 0.3)
    inp['cache_fox_k'] = nrm((N_FOX_LAYERS, DEC_BATCH, PAST_LEN, H, HD), 1.0)
    inp['cache_fox_v'] = nrm((N_FOX_LAYERS, DEC_BATCH, PAST_LEN, H, HD), 1.0)
    inp['cache_fox_logf'] = jax.nn.log_sigmoid(FORGET_BIAS_INIT + nrm((N_FOX_LAYERS, DEC_BATCH, PAST_LEN, H), 1.0))
    inp['cache_band_k'] = nrm((N_BAND_LAYERS, DEC_BATCH, band_rows, H, HD), 1.0)
    inp['cache_band_v'] = nrm((N_BAND_LAYERS, DEC_BATCH, band_rows, H, HD), 1.0)
    inp['ffn1_norm'] = 1.0 + nrm((DEPTH, D), 0.05)
    inp['ffn1_w_in'] = nrm((DEPTH, D, 2 * D_FF), D ** -0.5)
    inp['ffn1_w_out'] = nrm((DEPTH, D_FF, D), D_FF ** -0.5)
    inp['mix_norm'] = 1.0 + nrm((DEPTH, D), 0.05)
    inp['ffn2_norm'] = 1.0 + nrm((DEPTH, D), 0.05)
    inp['ffn2_w_in'] = nrm((DEPTH, D, 2 * D_FF), D ** -0.5)
    inp['ffn2_w_out'] = nrm((DEPTH, D_FF, D), D_FF ** -0.5)
    inp['ssm_a_re'] = -0.5 + nrm((N_SSM_LAYERS, SSM_GROUPS, SSM_STATE), 0.01)
    inp['ssm_a_im'] = math.pi * jnp.arange(SSM_STATE, dtype=f32) + nrm((N_SSM_LAYERS, SSM_GROUPS, SSM_STATE), 0.01)
    inp['ssm_log_step'] = jax.random.uniform(next(ks), (N_SSM_LAYERS, SSM_GROUPS), f32,
                                             math.log(1e-3), math.log(1e-1))
    inp['ssm_b_re'] = nrm((N_SSM_LAYERS, SSM_GROUPS, SSM_STATE, SSM_GROUP), SSM_GROUP ** -0.5)
    inp['ssm_b_im'] = nrm((N_SSM_LAYERS, SSM_GROUPS, SSM_STATE, SSM_GROUP), SSM_GROUP ** -0.5)
    inp['ssm_c_re'] = nrm((N_SSM_LAYERS, SSM_GROUPS, SSM_GROUP, SSM_STATE), 0.5)
    inp['ssm_c_im'] = nrm((N_SSM_LAYERS, SSM_GROUPS, SSM_GROUP, SSM_STATE), 0.5)
    inp['ssm_d'] = nrm((N_SSM_LAYERS, D), 1.0)
    inp['ssm_w_glu'] = nrm((N_SSM_LAYERS, D, 2 * D), D ** -0.5)
    inp['fox_w_in'] = nrm((N_FOX_LAYERS, D, 3 * D + H), D ** -0.5)
    inp['fox_b_f'] = FORGET_BIAS_INIT + nrm((N_FOX_LAYERS, H), 0.5)
    inp['fox_w_out'] = nrm((N_FOX_LAYERS, D, D), D ** -0.5)
    inp['band_w_in'] = nrm((N_BAND_LAYERS, D, 3 * D), D ** -0.5)
    inp['band_rel_bias'] = nrm((N_BAND_LAYERS, H, N_REL), 0.5)
    inp['band_w_out'] = nrm((N_BAND_LAYERS, D, D), D ** -0.5)
    inp['final_norm'] = 1.0 + nrm((D,), 0.05)
    return inp


def reference(x_prompt, x_sample, state_ssm_re, state_ssm_im, cache_fox_k, cache_fox_v, cache_fox_logf,
              cache_band_k, cache_band_v, ffn1_norm, ffn1_w_in, ffn1_w_out, mix_norm, ffn2_norm, ffn2_w_in,
              ffn2_w_out, ssm_a_re, ssm_a_im, ssm_log_step, ssm_b_re, ssm_b_im, ssm_c_re, ssm_c_im, ssm_d,
              ssm_w_glu, fox_w_in, fox_b_f, fox_w_out, band_w_in, band_rel_bias, band_w_out, final_norm):
    xp, xs = x_prompt, x_sample
    p_ssm_re, p_ssm_im, s_ssm_re, s_ssm_im = [], [], [], []
    p_fox_k, p_fox_v, p_fox_logf, s_fox_k, s_fox_v, s_fox_logf = [], [], [], [], [], []
    p_band_k, p_band_v, s_band_k, s_band_v = [], [], [], []
    for i in range(DEPTH):
        kind, j = i % N_MIXERS, i // N_MIXERS
        xp = xp + 0.5 * swiglu(rmsnorm(xp, ffn1_norm[i]), ffn1_w_in[i], ffn1_w_out[i])
        xs = xs + 0.5 * swiglu(rmsnorm(xs, ffn1_norm[i]), ffn1_w_in[i], ffn1_w_out[i])
        hp, hs = rmsnorm(xp, mix_norm[i]), rmsnorm(xs, mix_norm[i])
        if kind == 0:
            disc = ssm_discretize(ssm_a_re[j], ssm_a_im[j], ssm_log_step[j], ssm_b_re[j], ssm_b_im[j])
            mp, pr, pi = ssm_mixer(hp, None, None, disc, ssm_c_re[j], ssm_c_im[j], ssm_d[j], ssm_w_glu[j])
            ms, sr, si = ssm_mixer(hs, state_ssm_re[j], state_ssm_im[j], disc,
                                   ssm_c_re[j], ssm_c_im[j], ssm_d[j], ssm_w_glu[j])
            p_ssm_re.append(pr); p_ssm_im.append(pi); s_ssm_re.append(sr); s_ssm_im.append(si)
        elif kind == 1:
            mp, kp, vp, lp = fox_prompt(hp, fox_w_in[j], fox_b_f[j], fox_w_out[j])
            ms, kn, vn, ln = fox_sample(hs, cache_fox_k[j], cache_fox_v[j], cache_fox_logf[j],
                                        fox_w_in[j], fox_b_f[j], fox_w_out[j])
            p_fox_k.append(kp); p_fox_v.append(vp); p_fox_logf.append(lp)
            s_fox_k.append(kn); s_fox_v.append(vn); s_fox_logf.append(ln)
        else:
            mp, kp, vp = band_prompt(hp, band_w_in[j], band_rel_bias[j], band_w_out[j])
            ms, kn, vn = band_sample(hs, cache_band_k[j], cache_band_v[j],
                                     band_w_in[j], band_rel_bias[j], band_w_out[j])
            p_band_k.append(kp); p_band_v.append(vp); s_band_k.append(kn); s_band_v.append(vn)
        xp = xp + mp
        xs = xs + ms
        xp = xp + 0.5 * swiglu(rmsnorm(xp, ffn2_norm[i]), ffn2_w_in[i], ffn2_w_out[i])
        xs = xs + 0.5 * swiglu(rmsnorm(xs, ffn2_norm[i]), ffn2_w_in[i], ffn2_w_out[i])
    y_prompt = rmsnorm(xp, final_norm)
    y_sample = rmsnorm(xs, final_norm)
    return (y_prompt, y_sample,
            jnp.stack(p_ssm_re), jnp.stack(p_ssm_im),
            jnp.stack(p_fox_k), jnp.stack(p_fox_v), jnp.stack(p_fox_logf),
            jnp.stack(p_band_k), jnp.stack(p_band_v),
            jnp.stack(s_ssm_re), jnp.stack(s_ssm_im),
            jnp.stack(s_fox_k), jnp.stack(s_fox_v), jnp.stack(s_fox_logf),
            jnp.stack(s_band_k), jnp.stack(s_band_v))
```

```python
import os
import numpy as np
from contextlib import ExitStack
import concourse.bass as bass
import concourse.mybir as mybir
from concourse.bass_utils import run_bass_kernel_spmd

F32 = mybir.dt.float32
BF16 = mybir.dt.bfloat16
AF = mybir.ActivationFunctionType
ALU = mybir.AluOpType

P = 128
D = 2048
DC = 16
DFF = 5632
FC = 44
T = 2176
NCH = 17
DEPTH = 4
EPS = 1e-6
NCORES = 8
TILES_ALL = [(0, 512), (512, 512), (1024, 512), (1536, 512), (2048, 128)]
HALF_TILES = [[(0, 512), (512, 512), (1024, 64)], [(1088, 512), (1600, 512), (2112, 64)]]
HALF_W = 1088

ENG4 = ('pe', 'act', 'dve', 'pool')
NSETS = 5
SET_LIMIT = 20000
NDSEM = 64


class Buf:
    __slots__ = ('name', 'last_w', 'readers', 'dsem')

    def __init__(self, name):
        self.name = name
        self.last_w = None
        self.readers = []
        self.dsem = None


class Op:
    __slots__ = ('eng', 'fn', 'cdeps', 'dwaits', 'dma_sem', 'need_inc', 'inc_val', 'idx', 'is_dma')


class Sched:
    def __init__(self, nc, es):
        self.nc = nc
        self.esems = {e: [es.enter_context(nc.semaphore(f"es_{e}_{k}")) for k in range(NSETS)] for e in ENG4}
        self.eset = 0
        self.ecount = {e: 0 for e in ENG4}
        self.dpool = [[es.enter_context(nc.semaphore(f"ds_{k}")), 0] for k in range(NDSEM)]
        self.dnext = 0
        self.engobj = {'pe': nc.tensor, 'act': nc.scalar, 'dve': nc.vector, 'pool': nc.gpsimd, 'sp': nc.sync}
        self.begin()

    def begin(self):
        self.ops = []
        self.nops = {e: 0 for e in ENG4}
        self.lastop = {e: None for e in ENG4}
        self.waited = {e: {} for e in ('pe', 'act', 'dve', 'pool', 'sp')}
        self.used_dsems = []
        if max(self.ecount.values()) > SET_LIMIT:
            self.eset += 1
            self.ecount = {e: 0 for e in ENG4}

    def _deps(self, eng, reads, writes):
        prods = []
        for b in reads:
            if b.last_w is not None:
                prods.append(b.last_w)
        for b in writes:
            if b.last_w is not None:
                prods.append(b.last_w)
            prods.extend(b.readers)
        cbest = {}
        dwaits = {}
        w = self.waited[eng]
        for p in prods:
            if p.is_dma:
                ent = p.dma_sem
                val = ent[1] * 16
                if w.get(id(ent), 0) < val:
                    dwaits[id(ent)] = (ent, val)
            else:
                if p.eng == eng and eng == 'pe':
                    continue
                if w.get(p.eng, -1) >= p.idx:
                    continue
                if p.eng not in cbest or cbest[p.eng].idx < p.idx:
                    cbest[p.eng] = p
        for e, p in cbest.items():
            w[e] = p.idx
        for k, (ent, val) in dwaits.items():
            w[k] = val
        return list(cbest.values()), list(dwaits.values())

    def _post(self, o, reads, writes):
        for b in reads:
            b.readers.append(o)
        for b in writes:
            b.last_w = o
            b.readers = []
        self.ops.append(o)

    def op(self, eng, fn, reads=(), writes=()):
        o = Op()
        o.eng = eng
        o.fn = fn
        o.is_dma = False
        o.dma_sem = None
        o.need_inc = False
        o.inc_val = 0
        o.cdeps, o.dwaits = self._deps(eng, reads, writes)
        o.idx = self.nops[eng]
        self.nops[eng] += 1
        self.lastop[eng] = o
        self._post(o, reads, writes)
        return o

    def dma(self, out, in_, sbuf, reads=(), writes=(), q='sp'):
        if sbuf.dsem is None:
            sbuf.dsem = self.dpool[self.dnext % NDSEM]
            self.dnext += 1
            self.used_dsems.append(sbuf.dsem)
        o = Op()
        o.eng = q
        o.fn = lambda e, out=out, in_=in_: e.dma_start(out=out, in_=in_)
        o.is_dma = True
        o.need_inc = False
        o.inc_val = 0
        o.cdeps, o.dwaits = self._deps(q, reads, writes)
        o.dma_sem = sbuf.dsem
        sbuf.dsem[1] += 1
        o.idx = -1
        self._post(o, reads, writes)
        return o

    def end(self):
        lasts = [o for o in self.lastop.values() if o is not None]
        dents = list({id(e): e for e in self.used_dsems}.values())
        for eng in ('pe', 'act', 'dve', 'pool', 'sp'):
            o = Op()
            o.eng = eng
            o.fn = None
            o.is_dma = False
            o.dma_sem = None
            o.need_inc = False
            o.inc_val = 0
            o.idx = -2
            w = self.waited[eng]
            o.cdeps = [p for p in lasts if p.eng != eng and w.get(p.eng, -1) < p.idx]
            o.dwaits = [(ent, ent[1] * 16) for ent in dents if w.get(id(ent), 0) < ent[1] * 16]
            self.ops.append(o)
        for o in self.ops:
            for p in o.cdeps:
                p.need_inc = True
        for o in self.ops:
            if o.need_inc:
                self.ecount[o.eng] += 1
                o.inc_val = self.ecount[o.eng]
        for o in self.ops:
            e = self.engobj[o.eng]
            for p in o.cdeps:
                e.wait_ge(self.esems[p.eng][self.eset], p.inc_val)
            for ent, val in o.dwaits:
                e.wait_ge(ent[0], val)
            if o.fn is None:
                continue
            ins = o.fn(e)
            if o.is_dma:
                ins.then_inc(o.dma_sem[0], 16)
            elif o.need_inc:
                ins.then_inc(self.esems[o.eng][self.eset], 1)
        self.begin()


class Ctx:
    pass


_UID = [0]


def _sb(nc, es, name, shape, dt):
    _UID[0] += 1
    return es.enter_context(nc.sbuf_tensor(f"s{_UID[0]}_{name}", shape, dt))


def _ps(nc, es, name, dt=F32, w=512):
    _UID[0] += 1
    return es.enter_context(nc.psum_tensor(f"p{_UID[0]}_{name}", [P, w], dt))


def phase_load(C):
    nc, S = C.nc, C.S
    with ExitStack() as es:
        ident = _sb(nc, es, "ident", [P, P], F32)
        b_ident = Buf("ident")
        xin = [_sb(nc, es, f"xin{i}", [P, D], F32) for i in range(2)]
        b_xin = [Buf(f"xin{i}") for i in range(2)]
        stg = [_sb(nc, es, f"lstg{i}", [P, DC, P], F32) for i in range(2)]
        b_stg = [Buf(f"lstg{i}") for i in range(2)]
        pst = [_ps(nc, es, f"lps{i}") for i in range(4)]
        b_pst = [Buf(f"lps{i}") for i in range(4)]
        S.dma(ident[:], C.ident[:, :], b_ident, writes=[b_ident])
        k = 0
        for ch in range(NCH):
            src = C.xp[ch * P:(ch + 1) * P, :] if ch < 16 else C.xs[:, :]
            xi, bxi = xin[ch % 2], b_xin[ch % 2]
            st, bst = stg[ch % 2], b_stg[ch % 2]
            S.dma(xi[:], src, bxi, writes=[bxi])
            for g in range(4):
                ps, bps = pst[k % 4], b_pst[k % 4]
                k += 1
                for j in range(4):
                    dc = g * 4 + j
                    S.op('pe', lambda e, ps=ps, j=j, xi=xi, dc=dc: e.transpose(
                        ps[:, j * P:(j + 1) * P], xi[:, dc * P:(dc + 1) * P], ident[:]),
                        reads=[bxi, b_ident], writes=[bps])
                eng = 'dve' if g % 2 == 0 else 'act'
                if eng == 'dve':
                    S.op('dve', lambda e, ps=ps, st=st, g=g: e.tensor_copy(
                        st[:, g * 4:(g + 1) * 4, :], ps[:].rearrange("p (a b) -> p a b", a=4)),
                        reads=[bps], writes=[bst])
                else:
                    S.op('act', lambda e, ps=ps, st=st, g=g: e.copy(
                        st[:, g * 4:(g + 1) * 4, :], ps[:].rearrange("p (a b) -> p a b", a=4)),
                        reads=[bps], writes=[bst])
            S.dma(C.XT[:, :, ch * P:(ch + 1) * P].rearrange("c p t -> p c t"), st[:], bst, reads=[bst])
        S.end()


def rms_tile(C, x_sl, b_x, w, sq, b_sq, ps_ss, b_ss, ones_bf, b_ones, tmp, b_tmp, rstd_sl, b_rstd):
    S = C.S
    for dc in range(DC):
        s, bs = sq[dc % 2], b_sq[dc % 2]
        S.op('act', lambda e, s=s, dc=dc: e.activation(s[:, :w], x_sl(dc), AF.Square),
             reads=[b_x(dc) if callable(b_x) else b_x], writes=[bs])
        S.op('pe', lambda e, s=s, dc=dc: e.matmul(ps_ss[:, :w], ones_bf[:], s[:, :w],
                                                 start=(dc == 0), stop=(dc == DC - 1)),
             reads=[bs, b_ones], writes=[b_ss])
    S.op('dve', lambda e: e.tensor_scalar(tmp[:, :w], ps_ss[:, :w], 1.0 / D, EPS, ALU.mult, ALU.add),
         reads=[b_ss], writes=[b_tmp])
    S.op('act', lambda e: e.activation(tmp[:, :w], tmp[:, :w], AF.Sqrt), reads=[b_tmp], writes=[b_tmp])
    S.op('dve', lambda e: e.reciprocal(rstd_sl, tmp[:, :w]), reads=[b_tmp], writes=[b_rstd])


def phase_ffn(C, fi, ni, half):
    nc, S = C.nc, C.S
    tiles = HALF_TILES[half]
    t0 = half * HALF_W
    FB = 2
    NB = FC // FB
    with ExitStack() as es:
        acc = _sb(nc, es, "acc", [P, DC, HALF_W], F32)
        xn = _sb(nc, es, "xn", [P, DC, HALF_W], BF16)
        rstd = _sb(nc, es, "rstd", [P, HALF_W], F32)
        gv = _sb(nc, es, "gv", [P, DC], F32)
        ones_bf = _sb(nc, es, "ones_bf", [P, P], BF16)
        ones_f = _sb(nc, es, "ones_f", [P, P], F32)
        sq = [_sb(nc, es, f"sq{i}", [P, 512], BF16) for i in range(2)]
        tmp = _sb(nc, es, "tmp", [P, 512], F32)
        sg = [_sb(nc, es, f"sg{i}", [P, 512], F32) for i in range(2)]
        NSTG = 3
        stg = [_sb(nc, es, f"stg{i}", [P, D], F32) for i in range(NSTG)]
        wgb = [_sb(nc, es, f"wgb{i}", [P, D], BF16) for i in range(2)]
        wub = [_sb(nc, es, f"wub{i}", [P, D], BF16) for i in range(2)]
        wob = [_sb(nc, es, f"wob{i}", [P, D], BF16) for i in range(2 * FB)]
        hb = _sb(nc, es, "hb", [P, 2, FB, HALF_W], BF16)
        psg = [_ps(nc, es, f"psg{i}") for i in range(2)]
        psu = [_ps(nc, es, f"psu{i}") for i in range(2)]
        pso = [_ps(nc, es, f"pso{i}") for i in range(2)]
        pss = _ps(nc, es, "pss")

        b_acc = [[Buf(f"acc{d}_{t}") for t in range(3)] for d in range(DC)]
        b_xn = [Buf(f"xn{t}") for t in range(3)]
        b_rstd = [Buf(f"rstd{t}") for t in range(3)]
        b_gv, b_ones, b_onesf, b_tmp, b_pss = Buf("gv"), Buf("ones"), Buf("onesf"), Buf("tmp"), Buf("pss")
        b_sq = [Buf("sq0"), Buf("sq1")]
        b_sg = [Buf("sg0"), Buf("sg1")]
        b_stg = [Buf(f"stg{i}") for i in range(NSTG)]
        b_wgb = [Buf(f"wgb{i}") for i in range(2)]
        b_wub = [Buf(f"wub{i}") for i in range(2)]
        b_wob = [Buf(f"wob{i}") for i in range(2 * FB)]
        b_h = [[[Buf(f"h{a}{c}{t}") for t in range(3)] for c in range(FB)] for a in range(2)]
        b_psg = [Buf("psg0"), Buf("psg1")]
        b_psu = [Buf("psu0"), Buf("psu1")]
        b_pso = [Buf("pso0"), Buf("pso1")]
        b_accall = [b for row in b_acc for b in row]

        S.dma(gv[:], C.nrm[ni], b_gv, writes=[b_gv])
        S.dma(ones_f[:], C.ones[:, :], b_onesf, writes=[b_onesf])
        S.op('dve', lambda e: e.tensor_copy(ones_bf[:], ones_f[:]), reads=[b_onesf], writes=[b_ones])
        for dc in range(DC):
            S.dma(acc[:, dc, :], C.XT[dc, :, t0:t0 + HALF_W], b_acc[dc][0], writes=b_acc[dc])

        items = []
        for b in range(NB):
            for cl in range(FB):
                c = b * FB + cl
                items.append(('g', c, C.win[fi, c, 0], wgb[c % 2], b_wgb[c % 2]))
                items.append(('u', c, C.win[fi, c, 1], wub[c % 2], b_wub[c % 2]))
            for cl in range(FB):
                c = b * FB + cl
                slot = (b % 2) * FB + cl
                items.append(('o', c, C.wout[fi, c], wob[slot], b_wob[slot]))
        state = {'n': 0}

        def fetch_upto(n):
            while state['n'] < min(n, len(items)):
                i = state['n']
                _, _, src, dst, bdst = items[i]
                st, bst = stg[i % NSTG], b_stg[i % NSTG]
                S.dma(st[:], src, bst, writes=[bst])
                S.op('pool', lambda e, dst=dst, st=st: e.tensor_copy(dst[:], st[:]), reads=[bst], writes=[bdst])
                state['n'] += 1

        fetch_upto(NSTG)

        for ti, (ts, w) in enumerate(tiles):
            lo = ts - t0
            rms_tile(C, lambda dc, lo=lo, w=w: acc[:, dc, lo:lo + w], (lambda dc, ti=ti: b_acc[dc][ti]), w, sq, b_sq, pss, b_pss,
                     ones_bf, b_ones, tmp, b_tmp, rstd[:, lo:lo + w], b_rstd[ti])
            for dc in range(DC):
                S.op('dve', lambda e, dc=dc, lo=lo, w=w: e.scalar_tensor_tensor(
                    xn[:, dc, lo:lo + w], acc[:, dc, lo:lo + w], gv[:, dc:dc + 1], rstd[:, lo:lo + w],
                    ALU.mult, ALU.mult),
                    reads=[b_acc[dc][ti], b_gv, b_rstd[ti]], writes=[b_xn[ti]])

        k = 0
        ko = 0
        consumed = 0
        for b in range(NB):
            par = b % 2
            for cl in range(FB):
                c = b * FB + cl
                consumed += 2
                fetch_upto(consumed + NSTG)
                wg, bwg, wu, bwu = wgb[c % 2], b_wgb[c % 2], wub[c % 2], b_wub[c % 2]
                for ti, (ts, w) in enumerate(tiles):
                    lo = ts - t0
                    pg, bpg, pu, bpu = psg[k % 2], b_psg[k % 2], psu[k % 2], b_psu[k % 2]
                    sgt, bsg = sg[k % 2], b_sg[k % 2]
                    k += 1
                    for dc in range(DC):
                        S.op('pe', lambda e, pg=pg, wg=wg, dc=dc, lo=lo, w=w: e.matmul(
                            pg[:, :w], wg[:, dc * P:(dc + 1) * P], xn[:, dc, lo:lo + w],
                            start=(dc == 0), stop=(dc == DC - 1)), reads=[bwg, b_xn[ti]], writes=[bpg])
                    for dc in range(DC):
                        S.op('pe', lambda e, pu=pu, wu=wu, dc=dc, lo=lo, w=w: e.matmul(
                            pu[:, :w], wu[:, dc * P:(dc + 1) * P], xn[:, dc, lo:lo + w],
                            start=(dc == 0), stop=(dc == DC - 1)), reads=[bwu, b_xn[ti]], writes=[bpu])
                    S.op('act', lambda e, sgt=sgt, pg=pg, w=w: e.activation(sgt[:, :w], pg[:, :w], AF.Silu),
                         reads=[bpg], writes=[bsg])
                    S.op('dve', lambda e, sgt=sgt, pu=pu, par=par, cl=cl, lo=lo, w=w: e.tensor_tensor(
                        hb[:, par, cl, lo:lo + w], sgt[:, :w], pu[:, :w], ALU.mult),
                        reads=[bsg, bpu], writes=[b_h[par][cl][ti]])
            consumed += FB
            fetch_upto(consumed + NSTG)
            for d in range(DC):
                for ti, (ts, w) in enumerate(tiles):
                    lo = ts - t0
                    po, bpo = pso[ko % 2], b_pso[ko % 2]
                    ko += 1
                    for cl in range(FB):
                        slot = par * FB + cl
                        S.op('pe', lambda e, po=po, slot=slot, d=d, par=par, cl=cl, lo=lo, w=w: e.matmul(
                            po[:, :w], wob[slot][:, d * P:(d + 1) * P], hb[:, par, cl, lo:lo + w],
                            start=(cl == 0), stop=(cl == FB - 1)),
                            reads=[b_wob[slot], b_h[par][cl][ti]], writes=[bpo])
                    S.op('dve', lambda e, po=po, d=d, lo=lo, w=w: e.scalar_tensor_tensor(
                        acc[:, d, lo:lo + w], po[:, :w], 0.5, acc[:, d, lo:lo + w], ALU.mult, ALU.add),
                        reads=[bpo, b_acc[d][ti]], writes=[b_acc[d][ti]])
        for dc in range(DC):
            S.dma(C.XT[dc, :, t0:t0 + HALF_W], acc[:, dc, :], b_acc[dc][0], reads=b_acc[dc])
        S.end()


ATTN_SCALE = 128 ** -0.5
NEG = -30000.0


def load_xn_all(C, es, ni, xn, b_xn):
    nc, S = C.nc, C.S
    x32 = _sb(nc, es, "x32", [P, DC, 512], F32)
    b_x32 = Buf("x32")
    rstd = _sb(nc, es, "rstdm", [P, 512], F32)
    b_rstd = Buf("rstdm")
    gv = _sb(nc, es, "gvm", [P, DC], F32)
    b_gv = Buf("gvm")
    ones_f = _sb(nc, es, "ones_fm", [P, P], F32)
    ones_bf = _sb(nc, es, "ones_bfm", [P, P], BF16)
    b_onesf, b_ones = Buf("onesf"), Buf("ones")
    sq = [_sb(nc, es, f"sqm{i}", [P, 512], BF16) for i in range(2)]
    b_sq = [Buf("sqm0"), Buf("sqm1")]
    tmp = _sb(nc, es, "tmpm", [P, 512], F32)
    b_tmp = Buf("tmpm")
    pss = _ps(nc, es, "pssm")
    b_pss = Buf("pssm")
    S.dma(gv[:], C.nrm[ni], b_gv, writes=[b_gv])
    S.dma(ones_f[:], C.ones[:, :], b_onesf, writes=[b_onesf])
    S.op('dve', lambda e: e.tensor_copy(ones_bf[:], ones_f[:]), reads=[b_onesf], writes=[b_ones])
    for ti, (ts, w) in enumerate(TILES_ALL):
        S.dma(x32[:, :, :w], C.XT[:, :, ts:ts + w].rearrange("c p t -> p c t"), b_x32, writes=[b_x32])
        rms_tile(C, lambda dc, w=w: x32[:, dc, :w], b_x32, w, sq, b_sq, pss, b_pss, ones_bf, b_ones,
                 tmp, b_tmp, rstd[:, :w], b_rstd)
        for dc in range(DC):
            S.op('dve', lambda e, dc=dc, ts=ts, w=w: e.scalar_tensor_tensor(
                xn[:, dc, ts:ts + w], x32[:, dc, :w], gv[:, dc:dc + 1], rstd[:, :w], ALU.mult, ALU.mult),
                reads=[b_x32, b_gv, b_rstd], writes=[b_xn[ti]])
    return x32, b_x32


class WStream:
    def __init__(self, C, es, srcs, nstg=3, nwb=2, tag="ws"):
        nc = C.nc
        self.C = C
        self.srcs = srcs
        self.nstg, self.nwb = nstg, nwb
        self.stg = [_sb(nc, es, f"{tag}stg{i}", [P, D], F32) for i in range(nstg)]
        self.b_stg = [Buf(f"{tag}stg{i}") for i in range(nstg)]
        self.wb = [_sb(nc, es, f"{tag}wb{i}", [P, D], BF16) for i in range(nwb)]
        self.b_wb = [Buf(f"{tag}wb{i}") for i in range(nwb)]
        self.n = 0

    def fetch_upto(self, n):
        S = self.C.S
        while self.n < min(n, len(self.srcs)):
            i = self.n
            st, bst = self.stg[i % self.nstg], self.b_stg[i % self.nstg]
            dst, bdst = self.wb[i % self.nwb], self.b_wb[i % self.nwb]
            S.dma(st[:], self.srcs[i], bst, writes=[bst])
            S.op('pool', lambda e, dst=dst, st=st: e.tensor_copy(dst[:], st[:]), reads=[bst], writes=[bdst])
            self.n += 1

    def get(self, i):
        self.fetch_upto(i + self.nwb)
        return self.wb[i % self.nwb], self.b_wb[i % self.nwb]


def phase_attn_proj(C, ni, wqkv, kout, vout, skout, svout, prow0, fox=False):
    nc, S = C.nc, C.S
    with ExitStack() as es:
        xn = _sb(nc, es, "xna", [P, DC, T], BF16)
        b_xn = [Buf(f"xna{t}") for t in range(5)]
        load_xn_all(C, es, ni, xn, b_xn)
        ident = _sb(nc, es, "identa", [P, P], F32)
        b_ident = Buf("identa")
        S.dma(ident[:], C.ident[:, :], b_ident, writes=[b_ident])
        ws = WStream(C, es, [wqkv[i] for i in range(48)])
        rowf = [_sb(nc, es, f"rowf{i}", [P, T], F32) for i in range(2)]
        b_rowf = [Buf(f"rowf{i}") for i in range(2)]
        rowb = [_sb(nc, es, f"rowb{i}", [P, T], BF16) for i in range(2)]
        b_rowb = [Buf(f"rowb{i}") for i in range(2)]
        tm = [_sb(nc, es, f"tm{i}", [P, NCH, P], F32) for i in range(2)]
        b_tm = [Buf(f"tm{i}") for i in range(2)]
        tmb = [_sb(nc, es, f"tmb{i}", [P, NCH, P], BF16) for i in range(2)]
        b_tmb = [Buf(f"tmb{i}") for i in range(2)]
        psm = [_ps(nc, es, f"psm{i}") for i in range(2)]
        b_psm = [Buf(f"psm{i}") for i in range(2)]
        pst = [_ps(nc, es, f"psta{i}") for i in range(2)]
        b_pst = [Buf(f"psta{i}") for i in range(2)]
        if fox:
            fox_logf(C, es, xn, b_xn)
        ws.fetch_upto(2)
        km = 0
        kt = 0
        for cc in range(int(os.environ.get('NCC', 48))):
            which, h = cc // 16, cc % 16
            wb, bwb = ws.get(cc)
            rf, brf = rowf[cc % 2], b_rowf[cc % 2]
            rb, brb = rowb[cc % 2], b_rowb[cc % 2]
            for ti, (ts, w) in enumerate(TILES_ALL):
                pm, bpm = psm[km % 2], b_psm[km % 2]
                km += 1
                for dc in range(DC):
                    S.op('pe', lambda e, pm=pm, wb=wb, dc=dc, ts=ts, w=w: e.matmul(
                        pm[:, :w], wb[:, dc * P:(dc + 1) * P], xn[:, dc, ts:ts + w],
                        start=(dc == 0), stop=(dc == DC - 1)), reads=[bwb, b_xn[ti]], writes=[bpm])
                if which == 0:
                    S.op('act', lambda e, pm=pm, rb=rb, ts=ts, w=w: e.mul(rb[:, ts:ts + w], pm[:, :w], ATTN_SCALE),
                         reads=[bpm], writes=[brb])
                else:
                    S.op('act', lambda e, pm=pm, rf=rf, ts=ts, w=w: e.copy(rf[:, ts:ts + w], pm[:, :w]),
                         reads=[bpm], writes=[brf])
                    if which == 1:
                        S.op('dve', lambda e, rf=rf, rb=rb, ts=ts, w=w: e.tensor_copy(rb[:, ts:ts + w], rf[:, ts:ts + w]),
                             reads=[brf], writes=[brb])
            if which == 0:
                S.dma(C.QT[h], rb[:], brb, reads=[brb])
                continue
            if which == 1:
                S.dma(C.KT[h], rb[:], brb, reads=[brb])
            t_, bt_ = tm[cc % 2], b_tm[cc % 2]
            for ch in range(NCH if not os.environ.get('SKIP_TR') else 0):
                pt, bpt = pst[kt % 2], b_pst[kt % 2]
                kt += 1
                S.op('pe', lambda e, pt=pt, rf=rf, ch=ch: e.transpose(pt[:, :P], rf[:, ch * P:(ch + 1) * P], ident[:]),
                     reads=[brf, b_ident], writes=[bpt])
                if ch % 2 == 0:
                    S.op('dve', lambda e, pt=pt, t_=t_, ch=ch: e.tensor_copy(t_[:, ch, :], pt[:, :P]),
                         reads=[bpt], writes=[bt_])
                else:
                    S.op('act', lambda e, pt=pt, t_=t_, ch=ch: e.copy(t_[:, ch, :], pt[:, :P]),
                         reads=[bpt], writes=[bt_])
            dst_p, dst_s = (kout, skout) if which == 1 else (vout, svout)
            c0 = prow0 // P
            if not os.environ.get('SKIP_OUT'):
                S.dma(dst_p[:, h, :].rearrange("(c p) d -> p c d", p=P), t_[:, c0:16, :], bt_, reads=[bt_])
                S.dma(dst_s[:, h, :], t_[:, 16, :], bt_, reads=[bt_])
            if which == 2:
                tb, btb = tmb[cc % 2], b_tmb[cc % 2]
                S.op('pool', lambda e, tb=tb, t_=t_: e.tensor_copy(tb[:], t_[:]), reads=[bt_], writes=[btb])
                S.dma(C.VTM[h].rearrange("c p d -> p c d"), tb[:], btb, reads=[btb])
        S.end()


def attn_tile(C, kT, b_kT, qT, b_qT, nq, vt, b_vt, bias_fn, R, first, last, out_fn):
    S = C.S
    i = R['k']
    R['k'] += 1
    ps, bps = R['pss'][i % 2], R['b_pss'][i % 2]
    sb_, bsb = R['sb'][i % 2], R['b_sb'][i % 2]
    pT, bpT = R['pT'][i % 2], R['b_pT'][i % 2]
    S.op('pe', lambda e: e.matmul(ps[:, :nq], kT, qT, start=True, stop=True), reads=[b_kT, b_qT], writes=[bps])
    bias_fn(ps, bps, sb_, bsb, pT, bpT)
    po, bpo, pl, bpl = R['po'], R['b_po'], R['pl'], R['b_pl']
    S.op('pe', lambda e: e.matmul(po[:, :nq], vt, pT[:, :nq], start=first, stop=last), reads=[b_vt, bpT], writes=[bpo])
    S.op('pe', lambda e: e.matmul(pl[:, :nq], R['ones'][:], pT[:, :nq], start=first, stop=last),
         reads=[R['b_ones'], bpT], writes=[bpl])
    if last:
        rc, brc = R['rc'], R['b_rc']
        S.op('dve', lambda e: e.reciprocal(rc[:, :nq], pl[:, :nq]), reads=[bpl], writes=[brc])
        out_fn(po, bpo, rc, brc)


def attn_resources(C, es, ones_bf, b_ones):
    nc = C.nc
    R = {'k': 0}
    R['pss'] = [_ps(nc, es, f"pssc{i}") for i in range(2)]
    R['b_pss'] = [Buf(f"pssc{i}") for i in range(2)]
    R['sb'] = [_sb(nc, es, f"sbs{i}", [P, P], F32) for i in range(2)]
    R['b_sb'] = [Buf(f"sbs{i}") for i in range(2)]
    R['pT'] = [_sb(nc, es, f"pT{i}", [P, P], BF16) for i in range(2)]
    R['b_pT'] = [Buf(f"pT{i}") for i in range(2)]
    R['po'] = _ps(nc, es, "po")
    R['b_po'] = Buf("po")
    R['pl'] = _ps(nc, es, "pl")
    R['b_pl'] = Buf("pl")
    R['rc'] = _sb(nc, es, "rc", [P, P], F32)
    R['b_rc'] = Buf("rc")
    R['ones'] = ones_bf
    R['b_ones'] = b_ones
    return R


def out_proj(C, es, OT, b_OT, wo):
    nc, S = C.nc, C.S
    ws = WStream(C, es, [wo[i] for i in range(DC)], tag="wo")
    xr = [_sb(nc, es, f"xr{i}", [P, 512], F32) for i in range(2)]
    b_xr = [Buf(f"xr{i}") for i in range(2)]
    pso = [_ps(nc, es, f"psop{i}") for i in range(2)]
    b_pso = [Buf(f"psop{i}") for i in range(2)]
    ws.fetch_upto(2)
    k = 0
    for d in range(DC):
        wb, bwb = ws.get(d)
        for ti, (ts, w) in enumerate(TILES_ALL):
            po, bpo = pso[k % 2], b_pso[k % 2]
            x, bx = xr[k % 2], b_xr[k % 2]
            k += 1
            S.dma(x[:, :w], C.XT[d, :, ts:ts + w], bx, writes=[bx])
            for h in range(16):
                S.op('pe', lambda e, po=po, wb=wb, h=h, ts=ts, w=w: e.matmul(
                    po[:, :w], wb[:, h * P:(h + 1) * P], OT[:, h, ts:ts + w], start=(h == 0), stop=(h == 15)),
                    reads=[bwb, b_OT], writes=[bpo])
            S.op('dve', lambda e, po=po, x=x, w=w: e.tensor_tensor(x[:, :w], x[:, :w], po[:, :w], ALU.add),
                 reads=[bpo, bx], writes=[bx])
            S.dma(C.XT[d, :, ts:ts + w], x[:, :w], bx, reads=[bx])


def fox_logf(C, es, xn, b_xn):
    nc, S = C.nc, C.S
    wf32 = _sb(nc, es, "wf32", [P, DC, 16], F32)
    wfb = _sb(nc, es, "wfb", [P, DC, 16], BF16)
    bfb = _sb(nc, es, "bfb", [P, NCH, 16], F32)
    z = _sb(nc, es, "zlf", [P, NCH, 16], F32)
    lf = _sb(nc, es, "lf", [P, NCH, 16], F32)
    plg = _ps(nc, es, "plg")
    b_wf32, b_wfb, b_bfb, b_z, b_lf, b_plg = Buf("wf32"), Buf("wfb"), Buf("bfb"), Buf("zlf"), Buf("lf"), Buf("plg")
    S.dma(wf32[:], C.wf_fox[:, :, :], b_wf32, writes=[b_wf32])
    S.dma(bfb[:], C.bf_fox[:, :, :], b_bfb, writes=[b_bfb])
    S.op('dve', lambda e: e.tensor_copy(wfb[:], wf32[:]), reads=[b_wf32], writes=[b_wfb])
    for ch in range(NCH):
        ti = min(ch // 4, 4)
        for dc in range(DC):
            S.op('pe', lambda e, ch=ch, dc=dc: e.matmul(plg[:, ch * 16:(ch + 1) * 16], xn[:, dc, ch * P:(ch + 1) * P],
                                                       wfb[:, dc, :], start=(dc == 0), stop=(dc == DC - 1)),
                 reads=[b_xn[ti], b_wfb], writes=[b_plg])
    zf = z[:].rearrange("p c h -> p (c h)")
    S.op('dve', lambda e: e.tensor_tensor(zf, plg[:, :NCH * 16], bfb[:].rearrange("p c h -> p (c h)"), ALU.add),
         reads=[b_plg, b_bfb], writes=[b_z])
    S.op('act', lambda e: e.activation(zf, zf, AF.Exp, scale=-1.0), reads=[b_z], writes=[b_z])
    S.op('dve', lambda e: e.tensor_scalar(zf, zf, 1.0, None, ALU.add), reads=[b_z], writes=[b_z])
    S.op('act', lambda e: e.activation(zf, zf, AF.Ln), reads=[b_z], writes=[b_z])
    S.op('dve', lambda e: e.tensor_scalar(lf[:].rearrange("p c h -> p (c h)"), zf, -1.0, None, ALU.mult),
         reads=[b_z], writes=[b_lf])
    S.dma(C.pfl[:, :].rearrange("(c p) h -> p c h", p=P), lf[:, 0:16, :], b_lf, reads=[b_lf])
    S.dma(C.sfl[:, :], lf[:, 16, :], b_lf, reads=[b_lf])
    S.dma(C.LFS[:, :, :], lf[:], b_lf, reads=[b_lf])


def prefix_chunks(C, es, tot, b_tot, n, tag):
    nc, S = C.nc, C.S
    a = _sb(nc, es, f"pxa{tag}", [P, n, 16], F32)
    b = _sb(nc, es, f"pxb{tag}", [P, n, 16], F32)
    b_a, b_b = Buf(f"pxa{tag}"), Buf(f"pxb{tag}")
    S.op('dve', lambda e: e.memset(a[:, 0, :], 0.0), writes=[b_a])
    S.op('dve', lambda e: e.tensor_copy(a[:, 1:n, :], tot[:, 0:n - 1, :]), reads=[b_tot, b_a], writes=[b_a])
    cur, bcur, oth, both = a, b_a, b, b_b
    sft = 1
    while sft < n:
        S.op('dve', lambda e, cur=cur, oth=oth, sft=sft: e.tensor_copy(oth[:, 0:sft, :], cur[:, 0:sft, :]),
             reads=[bcur], writes=[both])
        S.op('dve', lambda e, cur=cur, oth=oth, sft=sft: e.tensor_tensor(
            oth[:, sft:n, :], cur[:, sft:n, :], cur[:, 0:n - sft, :], ALU.add), reads=[bcur, both], writes=[both])
        cur, bcur, oth, both = oth, both, cur, bcur
        sft *= 2
    return cur, bcur


def phase_fox_attn(C):
    nc, S = C.nc, C.S
    with ExitStack() as eo, ExitStack() as es:
        OT = _sb(nc, eo, "OTf", [P, 16, T], BF16)
        BP = _sb(nc, eo, "BP", [P, 16, 16, 16], F32)
        BN = _sb(nc, eo, "BN", [P, 16], F32)
        BCt = [_sb(nc, eo, f"BC{s_}", [P, 32, 16], F32) for s_ in range(2)]
        ones_f = _sb(nc, eo, "ones_ffx", [P, P], F32)
        ones_bf = _sb(nc, eo, "ones_bffx", [P, P], BF16)
        ident = _sb(nc, eo, "identfx", [P, P], F32)
        mkb = _sb(nc, eo, "mkb", [P, 3, P], BF16)
        b_OT = Buf("OTf")
        utri = _sb(nc, es, "utri", [P, P], F32)
        utri2 = _sb(nc, es, "utri2", [P, P], F32)
        mk32 = _sb(nc, es, "mk32", [P, 3, P], F32)
        b_onesf, b_ones, b_ident, b_utri, b_utri2, b_mk32, b_mkb = (Buf("onesf"), Buf("ones"), Buf("ident"),
                                                                    Buf("utri"), Buf("utri2"), Buf("mk32"), Buf("mkb"))
        S.dma(ones_f[:], C.ones[:, :], b_onesf, writes=[b_onesf])
        S.dma(ident[:], C.ident[:, :], b_ident, writes=[b_ident])
        S.dma(utri[:], C.utri[0], b_utri, writes=[b_utri])
        S.dma(utri2[:], C.utri[1], b_utri2, writes=[b_utri2])
        S.dma(mk32[:], C.fmask[:, :, :], b_mk32, writes=[b_mk32])
        S.op('dve', lambda e: e.tensor_copy(ones_bf[:], ones_f[:]), reads=[b_onesf], writes=[b_ones])
        S.op('dve', lambda e: e.tensor_copy(mkb[:], mk32[:]), reads=[b_mk32], writes=[b_mkb])
        pcs = _ps(nc, es, "pcs")
        b_pcs = Buf("pcs")

        lf = _sb(nc, es, "lfa", [P, NCH, 16], F32)
        b_lf = Buf("lfa")
        S.dma(lf[:], C.LFS[:, :, :], b_lf, writes=[b_lf])
        inc = _sb(nc, es, "inc", [P, NCH, 16], F32)
        tot = _sb(nc, es, "tot", [P, 16, 16], F32)
        b_inc, b_tot = Buf("inc"), Buf("tot")
        lf2 = lf[:, 0:16, :].rearrange("p c h -> p (c h)")
        S.op('pe', lambda e: e.matmul(pcs[:, :256], utri[:], lf2, start=True, stop=True), reads=[b_utri, b_lf], writes=[b_pcs])
        S.op('dve', lambda e: e.tensor_copy(inc[:, 0:16, :].rearrange("p c h -> p (c h)"), pcs[:, :256]),
             reads=[b_pcs], writes=[b_inc])
        S.op('pe', lambda e: e.matmul(pcs[:, :16], utri2[:], lf[:, 16, :], start=True, stop=True),
             reads=[b_utri2, b_lf, b_inc], writes=[b_pcs])
        S.op('dve', lambda e: e.tensor_copy(inc[:, 16, :], pcs[:, :16]), reads=[b_pcs], writes=[b_inc])
        S.op('pe', lambda e: e.matmul(pcs[:, :256], ones_f[:], lf2, start=True, stop=True),
             reads=[b_onesf, b_lf, b_inc], writes=[b_pcs])
        S.op('dve', lambda e: e.tensor_copy(tot[:].rearrange("p c h -> p (c h)"), pcs[:, :256]), reads=[b_pcs], writes=[b_tot])
        E, b_E = prefix_chunks(C, es, tot, b_tot, 16, "p")
        incp = _sb(nc, es, "incp", [P, 16, 16], F32)
        b_incp = Buf("incp")
        S.op('dve', lambda e: e.tensor_tensor(incp[:], inc[:, 0:16, :], E[:], ALU.add), reads=[b_inc, b_E], writes=[b_incp])
        b_BP = Buf("BP")
        for n in range(16):
            for j in range(n + 1):
                S.op('dve', lambda e, n=n, j=j: e.tensor_tensor(BP[:, n, j, :], E[:, n, :], incp[:, j, :], ALU.subtract),
                     reads=[b_E, b_incp], writes=[b_BP])
        b_BN = Buf("BN")
        S.op('dve', lambda e: e.tensor_scalar(BN[:], inc[:, 16, :], -1.0, None, ALU.mult), reads=[b_inc], writes=[b_BN])
        BC = []
        b_BC = []
        for s_ in range(2):
            lc = _sb(nc, es, f"lc{s_}", [P, 32, 16], F32)
            b_lc = Buf(f"lc{s_}")
            S.dma(lc[:], C.cfl[s_].rearrange("(c p) h -> p c h", p=P), b_lc, writes=[b_lc])
            incc = _sb(nc, es, f"incc{s_}", [P, 32, 16], F32)
            totc = _sb(nc, es, f"totc{s_}", [P, 32, 16], F32)
            b_incc, b_totc = Buf(f"incc{s_}"), Buf(f"totc{s_}")
            lcf = lc[:].rearrange("p c h -> p (c h)")
            S.op('pe', lambda e, lcf=lcf: e.matmul(pcs[:, :512], utri[:], lcf, start=True, stop=True),
                 reads=[b_utri, b_lc, b_tot, b_inc], writes=[b_pcs])
            S.op('dve', lambda e, incc=incc: e.tensor_copy(incc[:].rearrange("p c h -> p (c h)"), pcs[:, :512]),
                 reads=[b_pcs], writes=[b_incc])
            S.op('pe', lambda e, lcf=lcf: e.matmul(pcs[:, :512], ones_f[:], lcf, start=True, stop=True),
                 reads=[b_onesf, b_lc, b_incc], writes=[b_pcs])
            S.op('dve', lambda e, totc=totc: e.tensor_copy(totc[:].rearrange("p c h -> p (c h)"), pcs[:, :512]),
                 reads=[b_pcs], writes=[b_totc])
            EC, b_EC = prefix_chunks(C, es, totc, b_totc, 32, f"c{s_}")
            tt = _sb(nc, es, f"ttot{s_}", [P, 16], F32)
            b_tt = Buf(f"ttot{s_}")
            S.op('dve', lambda e, tt=tt, EC=EC, totc=totc: e.tensor_tensor(tt[:], EC[:, 31, :], totc[:, 31, :], ALU.add),
                 reads=[b_EC, b_totc], writes=[b_tt])
            bc = BCt[s_]
            b_bc = Buf(f"BC{s_}")
            S.op('dve', lambda e, incc=incc, EC=EC: e.tensor_tensor(incc[:], incc[:], EC[:], ALU.add),
                 reads=[b_incc, b_EC], writes=[b_incc])
            for j in range(32):
                S.op('dve', lambda e, bc=bc, tt=tt, incc=incc, j=j: e.tensor_tensor(bc[:, j, :], tt[:], incc[:, j, :], ALU.subtract),
                     reads=[b_tt, b_incc], writes=[b_bc])
            BC.append(bc)
            b_BC.append(b_bc)

        S.end()
        es.close()
        for bb in [b_OT, b_ones, b_onesf, b_ident, b_mkb, b_BP, b_BN] + b_BC:
            bb.last_w, bb.readers, bb.dsem = None, [], None
        R = attn_resources(C, es, ones_bf, b_ones)
        qT = [_sb(nc, es, f"fqT{i}", [P, T], BF16) for i in range(2)]
        kT = [_sb(nc, es, f"fkT{i}", [P, T], BF16) for i in range(2)]
        vt = [_sb(nc, es, f"fvt{i}", [P, NCH, P], BF16) for i in range(2)]
        kc = [_sb(nc, es, f"fkc{i}", [P, 16, P], F32) for i in range(2)]
        vc = [_sb(nc, es, f"fvc{i}", [P, 16, P], F32) for i in range(2)]
        kcT = [_sb(nc, es, f"fkcT{i}", [P, 4096], BF16) for i in range(2)]
        vcb = [_sb(nc, es, f"fvcb{i}", [P, 32, P], BF16) for i in range(2)]
        nm = ['qT', 'kT', 'vt', 'kc', 'vc']
        B = {n_: [Buf(f"f{n_}{i}") for i in range(2)] for n_ in nm}
        b_kcT = [[Buf(f"kcT{i}_{hf}") for hf in range(2)] for i in range(2)]
        b_vcb = [[Buf(f"vcb{i}_{hf}") for hf in range(2)] for i in range(2)]
        ptr = [_ps(nc, es, f"ptrf{i}") for i in range(2)]
        b_ptr = [Buf(f"ptrf{i}") for i in range(2)]
        ktr = 0
        kld = 0
        for h in range(16):
            a = h % 2
            S.dma(qT[a][:], C.QT[h], B['qT'][a], writes=[B['qT'][a]])
            S.dma(kT[a][:], C.KT[h], B['kT'][a], writes=[B['kT'][a]])
            S.dma(vt[a][:], C.VTM[h].rearrange("c p d -> p c d"), B['vt'][a], writes=[B['vt'][a]])
            for n in range(16):
                for j in range(n + 1):
                    def bias_fn(ps, bps, sb_, bsb, pT, bpT, n=n, j=j, h=h):
                        S.op('act', lambda e: e.activation(pT[:], ps[:, :P], AF.Exp, bias=BP[:, n, j, h:h + 1]),
                             reads=[bps, b_BP], writes=[bpT])
                        if j == n:
                            S.op('pool', lambda e: e.tensor_tensor(pT[:], pT[:], mkb[:, 0, :], ALU.mult),
                                 reads=[bpT, b_mkb], writes=[bpT])

                    def out_fn(po, bpo, rc, brc, h=h, n=n):
                        S.op('dve', lambda e: e.tensor_tensor(OT[:, h, n * P:(n + 1) * P], po[:, :P], rc[:, :P], ALU.mult),
                             reads=[bpo, brc], writes=[b_OT])

                    attn_tile(C, kT[a][:, j * P:(j + 1) * P], B['kT'][a], qT[a][:, n * P:(n + 1) * P], B['qT'][a], P,
                              vt[a][:, j, :], B['vt'][a], bias_fn, R, j == 0, j == n, out_fn)
            for s_ in range(2):
                q0 = 2048 + 64 * s_
                for hf in range(2):
                    u = kld % 2
                    kld += 1
                    S.dma(kc[u][:], C.cfk[s_, hf * 2048:(hf + 1) * 2048, h, :].rearrange("(c p) d -> p c d", p=P),
                          B['kc'][u], writes=[B['kc'][u]])
                    S.dma(vc[u][:], C.cfv[s_, hf * 2048:(hf + 1) * 2048, h, :].rearrange("(c p) d -> p c d", p=P),
                          B['vc'][u], writes=[B['vc'][u]])
                    S.op('pool', lambda e, u=u, s_=s_, hf=hf: e.tensor_copy(vcb[s_][:, hf * 16:(hf + 1) * 16, :], vc[u][:]),
                         reads=[B['vc'][u]], writes=[b_vcb[s_][hf]])
                    for c in range(16):
                        pt, bpt = ptr[ktr % 2], b_ptr[ktr % 2]
                        ktr += 1
                        S.op('pe', lambda e, pt=pt, u=u, c=c: e.transpose(pt[:, :P], kc[u][:, c, :], ident[:]),
                             reads=[B['kc'][u], b_ident], writes=[bpt])
                        cg = hf * 16 + c
                        if c % 2 == 0:
                            S.op('act', lambda e, pt=pt, s_=s_, cg=cg: e.copy(kcT[s_][:, cg * P:(cg + 1) * P], pt[:, :P]),
                                 reads=[bpt], writes=[b_kcT[s_][hf]])
                        else:
                            S.op('dve', lambda e, pt=pt, s_=s_, cg=cg: e.tensor_copy(kcT[s_][:, cg * P:(cg + 1) * P], pt[:, :P]),
                                 reads=[bpt], writes=[b_kcT[s_][hf]])
                for j in range(33):
                    def bias_fn(ps, bps, sb_, bsb, pT, bpT, j=j, h=h, s_=s_):
                        if j < 32:
                            S.op('act', lambda e: e.activation(pT[:, :64], ps[:, :64], AF.Exp, bias=BC[s_][:, j, h:h + 1]),
                                 reads=[bps, b_BC[s_]], writes=[bpT])
                        else:
                            S.op('act', lambda e: e.activation(pT[:, :64], ps[:, :64], AF.Exp, bias=BN[:, h:h + 1]),
                                 reads=[bps, b_BN], writes=[bpT])
                            S.op('pool', lambda e: e.tensor_tensor(pT[:, :64], pT[:, :64], mkb[:, 1 + s_, 0:64], ALU.mult),
                                 reads=[bpT, b_mkb], writes=[bpT])

                    def out_fn(po, bpo, rc, brc, h=h, q0=q0):
                        S.op('dve', lambda e: e.tensor_tensor(OT[:, h, q0:q0 + 64], po[:, :64], rc[:, :64], ALU.mult),
                             reads=[bpo, brc], writes=[b_OT])

                    if j < 32:
                        kk, bkk = kcT[s_][:, j * P:(j + 1) * P], b_kcT[s_][j // 16]
                        vv, bvv = vcb[s_][:, j, :], b_vcb[s_][j // 16]
                    else:
                        kk, bkk = kT[a][:, 2048:2176], B['kT'][a]
                        vv, bvv = vt[a][:, 16, :], B['vt'][a]
                    attn_tile(C, kk, bkk, qT[a][:, q0:q0 + 64], B['qT'][a], 64, vv, bvv, bias_fn, R, j == 0, j == 32, out_fn)
        S.end()
        es.close()
        b_OT.last_w, b_OT.readers, b_OT.dsem = None, [], None
        out_proj(C, es, OT, b_OT, C.wo_fox)
        S.end()


import math
PI = math.pi
NCK = 34


def _sincos(C, x, b_x, n, tmp, b_tmp, itmp, s_out, b_s, c_out, b_c):
    S = C.S
    for off, dst, bd in ((0.0, s_out, b_s), (0.25, c_out, b_c)):
        S.op('dve', lambda e, off=off: e.tensor_scalar(tmp, x, 1.0 / (2 * PI), off, ALU.mult, ALU.add), reads=[b_x], writes=[b_tmp])
        S.op('dve', lambda e: e.tensor_copy(itmp, tmp), reads=[b_tmp], writes=[b_tmp])
        S.op('dve', lambda e, dst=dst: e.tensor_copy(dst, itmp), reads=[b_tmp], writes=[bd])
        S.op('dve', lambda e, dst=dst: e.tensor_tensor(tmp, tmp, dst, ALU.subtract), reads=[b_tmp, bd], writes=[b_tmp])
        S.op('dve', lambda e, dst=dst: e.tensor_scalar(dst, tmp, 0.5, None, ALU.is_gt), reads=[b_tmp], writes=[bd])
        S.op('dve', lambda e, dst=dst: e.tensor_tensor(tmp, tmp, dst, ALU.subtract), reads=[b_tmp, bd], writes=[b_tmp])
        S.op('dve', lambda e, dst=dst: e.tensor_scalar(dst, tmp, -0.5, None, ALU.is_lt), reads=[b_tmp], writes=[bd])
        S.op('dve', lambda e, dst=dst: e.tensor_tensor(tmp, tmp, dst, ALU.add), reads=[b_tmp, bd], writes=[b_tmp])
        S.op('act', lambda e, dst=dst: e.activation(dst, tmp, AF.Sin, scale=2 * PI), reads=[b_tmp], writes=[bd])


def phase_ssm_prep(C, j):
    nc, S = C.nc, C.S
    with ExitStack() as es:
        def t64(nm):
            return _sb(nc, es, nm, [P, 64], F32), Buf(nm)
        are, b_are = t64("are")
        aim, b_aim = t64("aim")
        ls, b_ls = t64("ls")
        S.dma(are[:], C.ssmp[j, 0], b_are, writes=[b_are])
        S.dma(aim[:], C.ssmp[j, 1], b_aim, writes=[b_aim])
        S.dma(ls[:], C.ssmp[j, 2], b_ls, writes=[b_ls])
        pos1, b_pos1 = t64("pos1")
        S.dma(pos1[:], C.pos1[:, :], b_pos1, writes=[b_pos1])
        br = _sb(nc, es, "br", [P, 64, 16], F32)
        bi = _sb(nc, es, "bi", [P, 64, 16], F32)
        cr = _sb(nc, es, "cr", [P, 64, 16], F32)
        ci = _sb(nc, es, "ci", [P, 64, 16], F32)
        b_br, b_bi, b_cr, b_ci = Buf("br"), Buf("bi"), Buf("cr"), Buf("ci")
        S.dma(br[:], C.ssmb[j, 0], b_br, writes=[b_br])
        S.dma(bi[:], C.ssmb[j, 1], b_bi, writes=[b_bi])
        S.dma(cr[:], C.ssmc[j, 0], b_cr, writes=[b_cr])
        S.dma(ci[:], C.ssmc[j, 1], b_ci, writes=[b_ci])
        ident = _sb(nc, es, "identp", [P, P], F32)
        b_ident = Buf("identp")
        S.dma(ident[:], C.ident[:, :], b_ident, writes=[b_ident])
        dt, b_dt = t64("dt")
        ar, b_ar = t64("ar")
        th, b_th = t64("th")
        mag, b_mag = t64("mag")
        sn, b_sn = t64("sn")
        cs, b_cs = t64("cs")
        tm_, b_tm = t64("tmq")
        abr, b_abr = t64("abr")
        abi, b_abi = t64("abi")
        den, b_den = t64("den")
        t1, b_t1 = t64("t1")
        t2, b_t2 = t64("t2")
        zr, b_zr = t64("zr")
        zi, b_zi = t64("zi")
        S.op('act', lambda e: e.activation(dt[:], ls[:], AF.Exp), reads=[b_ls], writes=[b_dt])
        S.op('dve', lambda e: e.tensor_tensor(ar[:], are[:], dt[:], ALU.mult), reads=[b_are, b_dt], writes=[b_ar])
        S.op('dve', lambda e: e.tensor_tensor(th[:], aim[:], dt[:], ALU.mult), reads=[b_aim, b_dt], writes=[b_th])
        S.op('act', lambda e: e.activation(mag[:], ar[:], AF.Exp), reads=[b_ar], writes=[b_mag])
        it64 = _sb(nc, es, "it64", [P, 64], mybir.dt.int32)
        _sincos(C, th[:], b_th, 64, tm_[:], b_tm, it64[:], sn[:], b_sn, cs[:], b_cs)
        S.op('dve', lambda e: e.tensor_tensor(abr[:], mag[:], cs[:], ALU.mult), reads=[b_mag, b_cs], writes=[b_abr])
        S.op('dve', lambda e: e.tensor_tensor(abi[:], mag[:], sn[:], ALU.mult), reads=[b_mag, b_sn], writes=[b_abi])
        S.op('dve', lambda e: e.tensor_tensor(den[:], are[:], are[:], ALU.mult), reads=[b_are], writes=[b_den])
        S.op('dve', lambda e: e.tensor_tensor(t1[:], aim[:], aim[:], ALU.mult), reads=[b_aim], writes=[b_t1])
        S.op('dve', lambda e: e.tensor_tensor(den[:], den[:], t1[:], ALU.add), reads=[b_den, b_t1], writes=[b_den])
        S.op('dve', lambda e: e.reciprocal(den[:], den[:]), reads=[b_den], writes=[b_den])
        S.op('dve', lambda e: e.tensor_scalar(abr[:], abr[:], -1.0, None, ALU.add), reads=[b_abr], writes=[b_abr])
        S.op('dve', lambda e: e.tensor_tensor(t1[:], abr[:], are[:], ALU.mult), reads=[b_abr, b_are, b_den], writes=[b_t1])
        S.op('dve', lambda e: e.tensor_tensor(t2[:], abi[:], aim[:], ALU.mult), reads=[b_abi, b_aim], writes=[b_t2])
        S.op('dve', lambda e: e.tensor_tensor(t1[:], t1[:], t2[:], ALU.add), reads=[b_t1, b_t2], writes=[b_t1])
        S.op('dve', lambda e: e.tensor_tensor(zr[:], t1[:], den[:], ALU.mult), reads=[b_t1, b_den], writes=[b_zr])
        S.op('dve', lambda e: e.tensor_tensor(t1[:], abi[:], are[:], ALU.mult), reads=[b_abi, b_are, b_zr], writes=[b_t1])
        S.op('dve', lambda e: e.tensor_tensor(t2[:], abr[:], aim[:], ALU.mult), reads=[b_abr, b_aim], writes=[b_t2])
        S.op('dve', lambda e: e.tensor_tensor(t1[:], t1[:], t2[:], ALU.subtract), reads=[b_t1, b_t2], writes=[b_t1])
        S.op('dve', lambda e: e.tensor_tensor(zi[:], t1[:], den[:], ALU.mult), reads=[b_t1, b_den], writes=[b_zi])
        bbr = _sb(nc, es, "bbr", [P, 64, 16], F32)
        bbi = _sb(nc, es, "bbi", [P, 64, 16], F32)
        tq = _sb(nc, es, "tq", [P, 64, 16], F32)
        b_bbr, b_bbi, b_tq = Buf("bbr"), Buf("bbi"), Buf("tq")
        zrb = zr[:].unsqueeze(2).to_broadcast([P, 64, 16])
        zib = zi[:].unsqueeze(2).to_broadcast([P, 64, 16])
        S.op('dve', lambda e: e.tensor_tensor(bbr[:], br[:], zrb, ALU.mult), reads=[b_br, b_zr], writes=[b_bbr])
        S.op('dve', lambda e: e.tensor_tensor(tq[:], bi[:], zib, ALU.mult), reads=[b_bi, b_zi], writes=[b_tq])
        S.op('dve', lambda e: e.tensor_tensor(bbr[:], bbr[:], tq[:], ALU.subtract), reads=[b_bbr, b_tq], writes=[b_bbr])
        S.op('dve', lambda e: e.tensor_tensor(bbi[:], bi[:], zrb, ALU.mult), reads=[b_bi, b_zr], writes=[b_bbi])
        S.op('dve', lambda e: e.tensor_tensor(tq[:], br[:], zib, ALU.mult), reads=[b_br, b_zi, b_bbr], writes=[b_tq])
        S.op('dve', lambda e: e.tensor_tensor(bbi[:], bbi[:], tq[:], ALU.add), reads=[b_bbi, b_tq], writes=[b_bbi])
        xb = [_sb(nc, es, f"xb{i}", [P, P], F32) for i in range(2)]
        b_xb = [Buf(f"xb{i}") for i in range(2)]
        bt = [_sb(nc, es, f"bt{i}", [P, 2, P], BF16) for i in range(2)]
        b_bt = [Buf(f"bt{i}") for i in range(2)]
        ct = [_sb(nc, es, f"ct{i}", [P, 3, P], BF16) for i in range(2)]
        b_ct = [Buf(f"ct{i}") for i in range(2)]
        ptp = [_ps(nc, es, f"ptp{i}") for i in range(2)]
        b_ptp = [Buf(f"ptp{i}") for i in range(2)]
        q = 0
        for k in range(64):
            r0 = (k % 4) * 32
            btk, bbt_ = bt[k % 2], b_bt[k % 2]
            ctk, bct_ = ct[k % 2], b_ct[k % 2]
            for a_, (src, bsrc) in enumerate(((bbr, b_bbr), (bbi, b_bbi))):
                x_, bx_ = xb[q % 2], b_xb[q % 2]
                pp, bpp = ptp[q % 2], b_ptp[q % 2]
                q += 1
                S.op('pool', lambda e, x_=x_: e.memset(x_[:], 0.0), writes=[bx_])
                S.op('pool', lambda e, x_=x_, src=src, k=k, r0=r0: e.tensor_copy(x_[0:64, r0:r0 + 16], src[0:64, k, :]),
                     reads=[bsrc, bx_], writes=[bx_])
                S.op('pool', lambda e, x_=x_, src=src, k=k, r0=r0: e.tensor_copy(x_[64:128, r0 + 16:r0 + 32], src[64:128, k, :]),
                     reads=[bsrc, bx_], writes=[bx_])
                S.op('pe', lambda e, pp=pp, x_=x_: e.transpose(pp[:, :P], x_[:], ident[:]), reads=[bx_, b_ident], writes=[bpp])
                S.op('act', lambda e, pp=pp, btk=btk, a_=a_: e.copy(btk[:, a_, :], pp[:, :P]), reads=[bpp], writes=[bbt_])
            S.dma(C.BBT[k].rearrange("a r s -> r a s"), btk[:], bbt_, reads=[bbt_])
            S.op('dve', lambda e, ctk=ctk: e.memset(ctk[:], 0.0), writes=[bct_])
            for a_, (src, bsrc, sc) in enumerate(((cr, b_cr, 1.0), (cr, b_cr, -1.0), (ci, b_ci, -1.0))):
                for gl in range(2):
                    S.op('dve', lambda e, ctk=ctk, a_=a_, src=src, sc=sc, gl=gl, k=k, r0=r0: e.tensor_scalar(
                        ctk[gl * 64:(gl + 1) * 64, a_, r0 + gl * 16:r0 + gl * 16 + 16], src[gl * 64:(gl + 1) * 64, k, :],
                        sc, None, ALU.mult), reads=[bsrc, bct_], writes=[bct_])
            S.dma(C.CTS[k].rearrange("a r s -> r a s"), ctk[:], bct_, reads=[bct_])
        def big(nm):
            return _sb(nc, es, nm, [P, 64, 64], F32), Buf(nm)
        ang, b_ang = big("ang")
        ea, b_ea = big("ea")
        sN, b_sN = big("sN")
        cN, b_cN = big("cN")
        tb, b_tb = big("tbig")
        tab = _sb(nc, es, "tab", [P, 64, 4, 64], F32)
        b_tab = Buf("tab")
        for k in range(64):
            S.op('dve', lambda e, k=k: e.tensor_scalar(ang[:, k, :], pos1[:], th[:, k:k + 1], None, ALU.mult),
                 reads=[b_pos1, b_th], writes=[b_ang])
            S.op('pool', lambda e, k=k: e.tensor_scalar(ea[:, k, :], pos1[:], ar[:, k:k + 1], None, ALU.mult),
                 reads=[b_pos1, b_ar], writes=[b_ea])
        fl = lambda t_: t_[:].rearrange("p k q -> p (k q)")
        itb = _sb(nc, es, "itb", [P, 4096], mybir.dt.int32)
        _sincos(C, fl(ang), b_ang, 4096, fl(tb), b_tb, itb[:], fl(sN), b_sN, fl(cN), b_cN)
        S.op('act', lambda e: e.activation(fl(tb), fl(ea), AF.Exp), reads=[b_ea, b_sN, b_cN], writes=[b_tb])
        S.op('dve', lambda e: e.tensor_tensor(tab[:, :, 0, :], tb[:], cN[:], ALU.mult), reads=[b_tb, b_cN], writes=[b_tab])
        S.op('dve', lambda e: e.tensor_tensor(tab[:, :, 1, :], tb[:], sN[:], ALU.mult), reads=[b_tb, b_sN], writes=[b_tab])
        S.op('act', lambda e: e.activation(fl(tb), fl(ea), AF.Exp, scale=-1.0), reads=[b_ea, b_tab], writes=[b_tb])
        S.op('dve', lambda e: e.tensor_tensor(tab[:, :, 2, :], tb[:], cN[:], ALU.mult), reads=[b_tb, b_cN], writes=[b_tab])
        S.op('dve', lambda e: e.scalar_tensor_tensor(tab[:, :, 3, :], tb[:], -1.0, sN[:], ALU.mult, ALU.mult),
             reads=[b_tb, b_sN], writes=[b_tab])
        S.dma(C.TAB[:, :, :, :], tab[:], b_tab, reads=[b_tab])
        ap_ = _sb(nc, es, "apow", [P, 64, 10], F32)
        b_ap = Buf("apow")
        S.op('dve', lambda e: e.tensor_copy(ap_[:, :, 0], tab[:, :, 0, 63]), reads=[b_tab], writes=[b_ap])
        S.op('dve', lambda e: e.tensor_copy(ap_[:, :, 1], tab[:, :, 1, 63]), reads=[b_tab, b_ap], writes=[b_ap])
        for i in range(1, 5):
            pr, pi_ = ap_[:, :, 2 * i - 2], ap_[:, :, 2 * i - 1]
            S.op('dve', lambda e, pr=pr: e.tensor_tensor(t1[:], pr, pr, ALU.mult), reads=[b_ap, b_zi], writes=[b_t1])
            S.op('dve', lambda e, pi_=pi_: e.tensor_tensor(t2[:], pi_, pi_, ALU.mult), reads=[b_ap], writes=[b_t2])
            S.op('dve', lambda e, i=i: e.tensor_tensor(ap_[:, :, 2 * i], t1[:], t2[:], ALU.subtract), reads=[b_t1, b_t2, b_ap], writes=[b_ap])
            S.op('dve', lambda e, pr=pr, pi_=pi_, i=i: e.scalar_tensor_tensor(ap_[:, :, 2 * i + 1], pr, 2.0, pi_, ALU.mult, ALU.mult),
                 reads=[b_ap], writes=[b_ap])
        S.dma(C.APOW[:, :, :], ap_[:], b_ap, reads=[b_ap])
        S.end()


def phase_ssm_scan(C, j, ni):
    nc, S = C.nc, C.S
    with ExitStack() as es:
        xn = _sb(nc, es, "xns", [P, DC, T], BF16)
        b_xn = [Buf(f"xns{t}") for t in range(5)]
        x32, b_x32 = load_xn_all(C, es, ni, xn, b_xn)
        x32f = x32[:].rearrange("p c t -> p (c t)")
        ident = _sb(nc, es, "idents", [P, P], F32)
        b_ident = Buf("idents")
        S.dma(ident[:], C.ident[:, :], b_ident, writes=[b_ident])
        rmask = _sb(nc, es, "rmask", [P, T], F32)
        b_rmask = Buf("rmask")
        S.dma(rmask[:], C.rmask[:, :], b_rmask, writes=[b_rmask])
        apw = _sb(nc, es, "apw", [P, 64, 10], F32)
        b_apw = Buf("apw")
        S.dma(apw[:], C.APOW[:, :, :], b_apw, writes=[b_apw])
        st0 = _sb(nc, es, "st0", [P, 2, 2, 64], F32)
        b_st0 = Buf("st0")
        S.dma(st0[:], C.ssms[j].rearrange("s a p k -> p s a k"), b_st0, writes=[b_st0])
        dv = _sb(nc, es, "dvs", [P, DC], F32)
        b_dv = Buf("dvs")
        S.dma(dv[:], C.ssmd[j], b_dv, writes=[b_dv])
        fin = _sb(nc, es, "fin", [P, 6, 64], F32)
        b_fin = Buf("fin")
        btk = [_sb(nc, es, f"sbt{i}", [P, 2, P], BF16) for i in range(2)]
        ctk = [_sb(nc, es, f"sct{i}", [P, 3, P], BF16) for i in range(2)]
        tbk = [_sb(nc, es, f"stb{i}", [P, 4, 64], F32) for i in range(2)]
        b_btk = [Buf(f"sbt{i}") for i in range(2)]
        b_ctk = [Buf(f"sct{i}") for i in range(2)]
        b_tbk = [Buf(f"stb{i}") for i in range(2)]
        bur = x32f[:, 0:T]
        bui = x32f[:, T:2 * T]
        b_bur = [Buf(f"bur{t}") for t in range(5)]
        b_bui = [Buf(f"bui{t}") for t in range(5)]
        wr = x32f[:, 2 * T:3 * T]
        wi = _sb(nc, es, "wi", [P, T], F32)
        b_wr, b_wi = Buf("wr"), Buf("wi")
        S.op('dve', lambda e: e.memset(wi[:, 0:1], 0.0), writes=[b_x32, b_wr, b_wi] + b_bur + b_bui)
        m1 = _sb(nc, es, "m1", [P, T], F32)
        m2 = _sb(nc, es, "m2", [P, T], F32)
        b_m1, b_m2 = Buf("m1"), Buf("m2")
        pr_ = [_sb(nc, es, f"prd{i}", [P, T], BF16) for i in range(4)]
        b_pr = [Buf(f"prd{i}") for i in range(4)]
        sm = _sb(nc, es, "sm", [P, 12, NCK], F32)
        b_sm = Buf("sm")
        psb = [_ps(nc, es, f"psb{i}") for i in range(2)]
        b_psb = [Buf(f"psb{i}") for i in range(2)]
        psy = [_ps(nc, es, f"psy{i}") for i in range(5)]
        b_psy = [Buf(f"psy{i}") for i in range(5)]
        xg = _sb(nc, es, "xg", [P, 512], F32)
        hx = _sb(nc, es, "hxg", [P, 512], F32)
        x2 = _sb(nc, es, "x2g", [P, 512], F32)
        b_xg, b_hx, b_x2 = Buf("xg"), Buf("hxg"), Buf("x2g")
        gb = [_sb(nc, es, f"gb{i}", [P, T], BF16) for i in range(2)]
        b_gb = [Buf(f"gb{i}") for i in range(2)]

        def v3(t_, lo, w):
            return t_[:, lo:lo + w].rearrange("p (n q) -> p n q", q=64)

        def tile_body(k):
            dc = k // 4
            a = k % 2
            S.dma(btk[a][:], C.BBT[k].rearrange("a r s -> r a s"), b_btk[a], writes=[b_btk[a]])
            S.dma(ctk[a][:], C.CTS[k].rearrange("a r s -> r a s"), b_ctk[a], writes=[b_ctk[a]])
            S.dma(tbk[a][:], C.TAB[:, k, :, :], b_tbk[a], writes=[b_tbk[a]])
            tpr, tpi, tnr, tni = (tbk[a][:, i, :] for i in range(4))
            for ti, (ts, w) in enumerate(TILES_ALL):
                for a_, (dst, bdst) in enumerate(((bur, b_bur), (bui, b_bui))):
                    pb, bpb = psb[a_], b_psb[a_]
                    S.op('pe', lambda e, pb=pb, a_=a_, ts=ts, w=w, a=a, dc=dc: e.matmul(
                        pb[:, :w], btk[a][:, a_, :], xn[:, dc, ts:ts + w], start=True, stop=True),
                        reads=[b_btk[a], b_xn[ti]], writes=[bpb])
                    S.op('act', lambda e, pb=pb, dst=dst, ts=ts, w=w: e.copy(dst[:, ts:ts + w], pb[:, :w]),
                         reads=[bpb], writes=[bdst[ti]])
            bc = lambda t_: t_.unsqueeze(1).to_broadcast([P, NCK, 64])
            A3 = lambda t_: t_[:].rearrange("p (n q) -> p n q", q=64)
            S.op('pool', lambda e: e.tensor_tensor(A3(m1), A3(bur), bc(tnr), ALU.mult), reads=b_bur + [b_tbk[a]], writes=[b_m1])
            S.op('pool', lambda e: e.tensor_tensor(A3(m2), A3(bui), bc(tni), ALU.mult), reads=b_bui + [b_tbk[a]], writes=[b_m2])
            S.op('dve', lambda e: e.tensor_tensor(wr[:], m1[:], m2[:], ALU.subtract), reads=[b_m1, b_m2], writes=[b_wr])
            S.op('pool', lambda e: e.tensor_tensor(A3(m1), A3(bur), bc(tni), ALU.mult), reads=b_bur + [b_tbk[a], b_wr], writes=[b_m1])
            S.op('pool', lambda e: e.tensor_tensor(A3(m2), A3(bui), bc(tnr), ALU.mult), reads=b_bui + [b_tbk[a], b_wr], writes=[b_m2])
            S.op('dve', lambda e: e.tensor_tensor(wi[:], m1[:], m2[:], ALU.add), reads=[b_m1, b_m2], writes=[b_wi])
            S.op('dve', lambda e: e.tensor_tensor_scan(wr[:], rmask[:], wr[:], 0.0, ALU.mult, ALU.add), reads=[b_rmask, b_wr], writes=[b_wr])
            S.op('dve', lambda e: e.tensor_tensor_scan(wi[:], rmask[:], wi[:], 0.0, ALU.mult, ALU.add), reads=[b_rmask, b_wi], writes=[b_wi])
            er = wr[:, 63:T:64]
            ei = wi[:, 63:T:64]
            Ar, Ai = apw[:, k, 0:1], apw[:, k, 1:2]
            X = lambda i, lo=0, hi=NCK: sm[:, i, lo:hi]
            S.op('dve', lambda e: e.tensor_scalar(X(2), ei, Ai, None, ALU.mult), reads=[b_wi, b_apw, b_fin], writes=[b_sm])
            S.op('dve', lambda e: e.scalar_tensor_tensor(X(0), er, Ar, X(2), ALU.mult, ALU.subtract), reads=[b_wr, b_apw, b_sm], writes=[b_sm])
            S.op('dve', lambda e: e.tensor_scalar(X(2), ei, Ar, None, ALU.mult), reads=[b_wi, b_apw, b_sm], writes=[b_sm])
            S.op('dve', lambda e: e.scalar_tensor_tensor(X(1), er, Ai, X(2), ALU.mult, ALU.add), reads=[b_wr, b_apw, b_sm], writes=[b_sm])
            cur = (0, 1)
            oth = (2, 3)
            sft = 1
            for i in range(5):
                Pr, Pi = apw[:, k, 2 * i:2 * i + 1], apw[:, k, 2 * i + 1:2 * i + 2]
                cr_, ci_ = cur
                or_, oi_ = oth
                n = 32
                S.op('dve', lambda e, cr_=cr_, or_=or_, sft=sft: e.tensor_copy(X(or_, 0, sft), X(cr_, 0, sft)), reads=[b_sm], writes=[b_sm])
                S.op('dve', lambda e, ci_=ci_, oi_=oi_, sft=sft: e.tensor_copy(X(oi_, 0, sft), X(ci_, 0, sft)), reads=[b_sm], writes=[b_sm])
                S.op('dve', lambda e, cr_=cr_, or_=or_, sft=sft, Pr=Pr: e.scalar_tensor_tensor(
                    X(or_, sft, n), X(cr_, 0, n - sft), Pr, X(cr_, sft, n), ALU.mult, ALU.add), reads=[b_sm, b_apw], writes=[b_sm])
                S.op('dve', lambda e, ci_=ci_, sft=sft, Pi=Pi: e.tensor_scalar(X(4, 0, n - sft), X(ci_, 0, n - sft), Pi, None, ALU.mult),
                     reads=[b_sm, b_apw], writes=[b_sm])
                S.op('dve', lambda e, or_=or_, sft=sft: e.tensor_tensor(X(or_, sft, n), X(or_, sft, n), X(4, 0, n - sft), ALU.subtract),
                     reads=[b_sm], writes=[b_sm])
                S.op('dve', lambda e, ci_=ci_, oi_=oi_, sft=sft, Pr=Pr: e.scalar_tensor_tensor(
                    X(oi_, sft, n), X(ci_, 0, n - sft), Pr, X(ci_, sft, n), ALU.mult, ALU.add), reads=[b_sm, b_apw], writes=[b_sm])
                S.op('dve', lambda e, cr_=cr_, oi_=oi_, sft=sft, Pi=Pi: e.scalar_tensor_tensor(
                    X(oi_, sft, n), X(cr_, 0, n - sft), Pi, X(oi_, sft, n), ALU.mult, ALU.add), reads=[b_sm, b_apw], writes=[b_sm])
                cur, oth = oth, cur
                sft *= 2
            sr, si = cur
            S.op('dve', lambda e: e.memset(sm[:, 6:8, 0:1], 0.0), reads=[b_sm], writes=[b_sm])
            S.op('dve', lambda e, sr=sr: e.tensor_copy(X(6, 1, 32), X(sr, 0, 31)), reads=[b_sm], writes=[b_sm])
            S.op('dve', lambda e, si=si: e.tensor_copy(X(7, 1, 32), X(si, 0, 31)), reads=[b_sm], writes=[b_sm])
            S.op('dve', lambda e, k=k: e.tensor_copy(sm[:, 6, 32:34], st0[:, :, 0, k]), reads=[b_sm, b_st0], writes=[b_sm])
            S.op('dve', lambda e, k=k: e.tensor_copy(sm[:, 7, 32:34], st0[:, :, 1, k]), reads=[b_sm, b_st0], writes=[b_sm])
            S.op('dve', lambda e, sr=sr, k=k: e.tensor_copy(fin[:, 0, k:k + 1], X(sr, 31, 32)), reads=[b_sm], writes=[b_fin])
            S.op('dve', lambda e, si=si, k=k: e.tensor_copy(fin[:, 1, k:k + 1], X(si, 31, 32)), reads=[b_sm, b_fin], writes=[b_fin])
            S.op('dve', lambda e: e.tensor_tensor(X(8, 32, 34), X(6, 32, 34), er[:, 32:34], ALU.add), reads=[b_sm, b_wr], writes=[b_sm])
            S.op('dve', lambda e: e.tensor_tensor(X(9, 32, 34), X(7, 32, 34), ei[:, 32:34], ALU.add), reads=[b_sm, b_wi], writes=[b_sm])
            S.op('dve', lambda e: e.tensor_scalar(X(10, 32, 34), X(9, 32, 34), Ai, None, ALU.mult), reads=[b_sm, b_apw], writes=[b_sm])
            S.op('dve', lambda e: e.scalar_tensor_tensor(X(11, 32, 34), X(8, 32, 34), Ar, X(10, 32, 34), ALU.mult, ALU.subtract),
                 reads=[b_sm, b_apw], writes=[b_sm])
            S.op('dve', lambda e, k=k: e.tensor_copy(fin[:, 2:6:2, k], X(11, 32, 34)), reads=[b_sm, b_fin], writes=[b_fin])
            S.op('dve', lambda e: e.tensor_scalar(X(10, 32, 34), X(9, 32, 34), Ar, None, ALU.mult), reads=[b_sm, b_apw, b_fin], writes=[b_sm])
            S.op('dve', lambda e: e.scalar_tensor_tensor(X(11, 32, 34), X(8, 32, 34), Ai, X(10, 32, 34), ALU.mult, ALU.add),
                 reads=[b_sm, b_apw], writes=[b_sm])
            S.op('dve', lambda e, k=k: e.tensor_copy(fin[:, 3:6:2, k], X(11, 32, 34)), reads=[b_sm, b_fin], writes=[b_fin])
            cb = lambda i: sm[:, i, :].unsqueeze(2).to_broadcast([P, NCK, 64])
            S.op('dve', lambda e: e.tensor_tensor(A3(wr), A3(wr), cb(6), ALU.add), reads=[b_wr, b_sm], writes=[b_wr])
            S.op('dve', lambda e: e.tensor_tensor(A3(wi), A3(wi), cb(7), ALU.add), reads=[b_wi, b_sm], writes=[b_wi])
            S.op('pool', lambda e: e.tensor_tensor(A3(pr_[0]), A3(wr), bc(tpr), ALU.mult), reads=[b_wr, b_tbk[a]], writes=[b_pr[0]])
            S.op('dve', lambda e: e.tensor_tensor(A3(pr_[1]), A3(wi), bc(tpi), ALU.mult), reads=[b_wi, b_tbk[a]], writes=[b_pr[1]])
            S.op('pool', lambda e: e.tensor_tensor(A3(pr_[2]), A3(wi), bc(tpr), ALU.mult), reads=[b_wi, b_tbk[a]], writes=[b_pr[2]])
            S.op('dve', lambda e: e.tensor_tensor(A3(pr_[3]), A3(wr), bc(tpi), ALU.mult), reads=[b_wr, b_tbk[a]], writes=[b_pr[3]])
            cmap = (0, 1, 2, 2)
            for ti, (ts, w) in enumerate(TILES_ALL):
                for q in range(4):
                    S.op('pe', lambda e, ti=ti, ts=ts, w=w, q=q, a=a, k=k: e.matmul(
                        psy[ti][:, :w], ctk[a][:, cmap[q], :], pr_[q][:, ts:ts + w],
                        start=(k % 4 == 0 and q == 0), stop=(k % 4 == 3 and q == 3)),
                        reads=[b_ctk[a], b_pr[q]], writes=[b_psy[ti]])
            if k % 4 != 3:
                return
            g_, bg_ = gb[dc % 2], b_gb[dc % 2]
            for ti, (ts, w) in enumerate(TILES_ALL):
                S.op('dve', lambda e, ti=ti, ts=ts, w=w, dc=dc: e.scalar_tensor_tensor(
                    xg[:, :w], xn[:, dc, ts:ts + w], dv[:, dc:dc + 1], psy[ti][:, :w], ALU.mult, ALU.add),
                    reads=[b_xn[ti], b_dv, b_psy[ti]], writes=[b_xg])
                S.op('act', lambda e, w=w: e.activation(x2[:, :w], xg[:, :w], AF.Square), reads=[b_xg], writes=[b_x2])
                S.op('act', lambda e, w=w: e.mul(hx[:, :w], xg[:, :w], 0.5), reads=[b_xg], writes=[b_hx])
                S.op('dve', lambda e, w=w: e.tensor_scalar(x2[:, :w], x2[:, :w], 0.044715, 1.0, ALU.mult, ALU.add), reads=[b_x2], writes=[b_x2])
                S.op('dve', lambda e, w=w: e.tensor_tensor(x2[:, :w], x2[:, :w], xg[:, :w], ALU.mult), reads=[b_x2, b_xg], writes=[b_x2])
                S.op('act', lambda e, w=w: e.activation(x2[:, :w], x2[:, :w], AF.Tanh, scale=0.7978845608028654), reads=[b_x2], writes=[b_x2])
                S.op('dve', lambda e, g_=g_, ts=ts, w=w: e.scalar_tensor_tensor(
                    g_[:, ts:ts + w], x2[:, :w], 1.0, hx[:, :w], ALU.add, ALU.mult), reads=[b_x2, b_hx], writes=[bg_])
            S.dma(C.GT[dc], g_[:], bg_, reads=[bg_])

        for k in range(64):
            tile_body(k)
        fo = _sb(nc, es, "fo", [P, 3, P], F32)
        b_fo = Buf("fo")
        for i in range(3):
            S.op('pe', lambda e, i=i: e.transpose(psb[i % 2][:, :P], fin[:, 2 * i:2 * i + 2, :].rearrange("p a k -> p (a k)"), ident[:]),
                 reads=[b_fin, b_ident], writes=[b_psb[i % 2]])
            S.op('dve', lambda e, i=i: e.tensor_copy(fo[:, i, :], psb[i % 2][:, :P]), reads=[b_psb[i % 2]], writes=[b_fo])
        S.dma(C.ssm_out[j].rearrange("i a k p -> (a k) i p"), fo[:], b_fo, reads=[b_fo])
        S.end()


def phase_ssm_glu(C, j):
    nc, S = C.nc, C.S
    with ExitStack() as es:
        g = _sb(nc, es, "gall", [P, DC, T], BF16)
        b_g = Buf("gall")
        S.dma(g[:], C.GT[:, :, :].rearrange("c p t -> p c t"), b_g, writes=[b_g])
        ws = WStream(C, es, [C.wglu[j, i // 2, i % 2] for i in range(32)], nstg=3, nwb=4, tag="wg")
        xr = [_sb(nc, es, f"xrg{i}", [P, 512], F32) for i in range(2)]
        b_xr = [Buf(f"xrg{i}") for i in range(2)]
        sgm = [_sb(nc, es, f"sgm{i}", [P, 512], F32) for i in range(2)]
        b_sgm = [Buf(f"sgm{i}") for i in range(2)]
        psv = [_ps(nc, es, f"psv{i}") for i in range(2)]
        psg = [_ps(nc, es, f"psgg{i}") for i in range(2)]
        b_psv = [Buf(f"psv{i}") for i in range(2)]
        b_psg = [Buf(f"psgg{i}") for i in range(2)]
        kk = 0
        for jc in range(DC):
            ws.fetch_upto(2 * jc + 4)
            wv, bwv = ws.wb[(2 * jc) % 4], ws.b_wb[(2 * jc) % 4]
            wg, bwg = ws.wb[(2 * jc + 1) % 4], ws.b_wb[(2 * jc + 1) % 4]
            for ti, (ts, w) in enumerate(TILES_ALL):
                pv, bpv, pg, bpg = psv[kk % 2], b_psv[kk % 2], psg[kk % 2], b_psg[kk % 2]
                x, bx = xr[kk % 2], b_xr[kk % 2]
                sg_, bsg_ = sgm[kk % 2], b_sgm[kk % 2]
                kk += 1
                S.dma(x[:, :w], C.XT[jc, :, ts:ts + w], bx, writes=[bx])
                for dc in range(DC):
                    S.op('pe', lambda e, pv=pv, wv=wv, dc=dc, ts=ts, w=w: e.matmul(
                        pv[:, :w], wv[:, dc * P:(dc + 1) * P], g[:, dc, ts:ts + w], start=(dc == 0), stop=(dc == DC - 1)),
                        reads=[bwv, b_g], writes=[bpv])
                for dc in range(DC):
                    S.op('pe', lambda e, pg=pg, wg=wg, dc=dc, ts=ts, w=w: e.matmul(
                        pg[:, :w], wg[:, dc * P:(dc + 1) * P], g[:, dc, ts:ts + w], start=(dc == 0), stop=(dc == DC - 1)),
                        reads=[bwg, b_g], writes=[bpg])
                S.op('act', lambda e, sg_=sg_, pg=pg, w=w: e.activation(sg_[:, :w], pg[:, :w], AF.Sigmoid), reads=[bpg], writes=[bsg_])
                S.op('dve', lambda e, sg_=sg_, pv=pv, w=w: e.tensor_tensor(sg_[:, :w], sg_[:, :w], pv[:, :w], ALU.mult),
                     reads=[bsg_, bpv], writes=[bsg_])
                S.op('dve', lambda e, sg_=sg_, x=x, w=w: e.tensor_tensor(x[:, :w], x[:, :w], sg_[:, :w], ALU.add),
                     reads=[bsg_, bx], writes=[bx])
                S.dma(C.XT[jc, :, ts:ts + w], x[:, :w], bx, reads=[bx])
        S.end()


def phase_band_attn(C):
    nc, S = C.nc, C.S
    with ExitStack() as es:
        OT = _sb(nc, es, "OT", [P, 16, T], BF16)
        b_OT = Buf("OT")
        ones_f = _sb(nc, es, "ones_fb", [P, P], F32)
        ones_bf = _sb(nc, es, "ones_bfb", [P, P], BF16)
        ident = _sb(nc, es, "identb", [P, P], F32)
        b_onesf, b_ones, b_ident = Buf("onesf"), Buf("ones"), Buf("ident")
        S.dma(ones_f[:], C.ones[:, :], b_onesf, writes=[b_onesf])
        S.dma(ident[:], C.ident[:, :], b_ident, writes=[b_ident])
        S.op('dve', lambda e: e.tensor_copy(ones_bf[:], ones_f[:]), reads=[b_onesf], writes=[b_ones])
        R = attn_resources(C, es, ones_bf, b_ones)
        qT = [_sb(nc, es, f"qT{i}", [P, T], BF16) for i in range(2)]
        kT = [_sb(nc, es, f"kT{i}", [P, T], BF16) for i in range(2)]
        vt = [_sb(nc, es, f"vt{i}", [P, NCH, P], BF16) for i in range(2)]
        bp = [_sb(nc, es, f"bp{i}", [P, 5, P], F32) for i in range(2)]
        bs = [_sb(nc, es, f"bs{i}", [P, 2, 5, 64], F32) for i in range(2)]
        kc = [_sb(nc, es, f"kc{i}", [P, 2, 4, P], F32) for i in range(2)]
        vc = [_sb(nc, es, f"vc{i}", [P, 2, 4, P], F32) for i in range(2)]
        kcT = [_sb(nc, es, f"kcT{i}", [P, 2, 512], BF16) for i in range(2)]
        vcb = [_sb(nc, es, f"vcb{i}", [P, 2, 4, P], BF16) for i in range(2)]
        nm = ['qT', 'kT', 'vt', 'bp', 'bs', 'kc', 'vc', 'kcT', 'vcb']
        B = {n: [Buf(f"{n}{i}") for i in range(2)] for n in nm}
        ptr = [_ps(nc, es, f"ptrb{i}") for i in range(2)]
        b_ptr = [Buf(f"ptrb{i}") for i in range(2)]
        ktr = 0
        for h in range(16):
            a = h % 2
            S.dma(qT[a][:], C.QT[h], B['qT'][a], writes=[B['qT'][a]])
            S.dma(kT[a][:], C.KT[h], B['kT'][a], writes=[B['kT'][a]])
            S.dma(vt[a][:], C.VTM[h].rearrange("c p d -> p c d"), B['vt'][a], writes=[B['vt'][a]])
            S.dma(bp[a][:], C.bbias_p[h].rearrange("t k q -> k t q"), B['bp'][a], writes=[B['bp'][a]])
            S.dma(bs[a][:], C.bbias_s[h].rearrange("s t k q -> k s t q"), B['bs'][a], writes=[B['bs'][a]])
            S.dma(kc[a][:], C.cbk[:, :, h, :].rearrange("s (c p) d -> p s c d", p=P), B['kc'][a], writes=[B['kc'][a]])
            S.dma(vc[a][:], C.cbv[:, :, h, :].rearrange("s (c p) d -> p s c d", p=P), B['vc'][a], writes=[B['vc'][a]])
            S.op('pool', lambda e, a=a: e.tensor_copy(vcb[a][:], vc[a][:]), reads=[B['vc'][a]], writes=[B['vcb'][a]])
            for s_ in range(2):
                for c in range(4):
                    pt, bpt = ptr[ktr % 2], b_ptr[ktr % 2]
                    ktr += 1
                    S.op('pe', lambda e, pt=pt, a=a, s_=s_, c=c: e.transpose(pt[:, :P], kc[a][:, s_, c, :], ident[:]),
                         reads=[B['kc'][a], b_ident], writes=[bpt])
                    S.op('act', lambda e, pt=pt, a=a, s_=s_, c=c: e.copy(kcT[a][:, s_, c * P:(c + 1) * P], pt[:, :P]),
                         reads=[bpt], writes=[B['kcT'][a]])
            for m in range(16):
                tl = [t for t in range(5) if m - 4 + t >= 0]
                for t in tl:
                    kt_ = m - 4 + t

                    def bias_fn(ps, bps, sb_, bsb, pT, bpT, a=a, t=t):
                        S.op('dve', lambda e: e.tensor_tensor(sb_[:], ps[:, :P], bp[a][:, t, :], ALU.add),
                             reads=[bps, B['bp'][a]], writes=[bsb])
                        S.op('act', lambda e: e.activation(pT[:], sb_[:], AF.Exp), reads=[bsb], writes=[bpT])

                    def out_fn(po, bpo, rc, brc, h=h, m=m):
                        S.op('dve', lambda e: e.tensor_tensor(OT[:, h, m * P:(m + 1) * P], po[:, :P], rc[:, :P], ALU.mult),
                             reads=[bpo, brc], writes=[b_OT])

                    attn_tile(C, kT[a][:, kt_ * P:(kt_ + 1) * P], B['kT'][a], qT[a][:, m * P:(m + 1) * P], B['qT'][a], P,
                              vt[a][:, kt_, :], B['vt'][a], bias_fn, R, t == tl[0], t == 4, out_fn)
            for s_ in range(2):
                q0 = 2048 + 64 * s_
                for t in range(5):
                    def bias_fn(ps, bps, sb_, bsb, pT, bpT, a=a, t=t, s_=s_):
                        S.op('dve', lambda e: e.tensor_tensor(sb_[:, :64], ps[:, :64], bs[a][:, s_, t, :], ALU.add),
                             reads=[bps, B['bs'][a]], writes=[bsb])
                        S.op('act', lambda e: e.activation(pT[:, :64], sb_[:, :64], AF.Exp), reads=[bsb], writes=[bpT])

                    def out_fn(po, bpo, rc, brc, h=h, q0=q0):
                        S.op('dve', lambda e: e.tensor_tensor(OT[:, h, q0:q0 + 64], po[:, :64], rc[:, :64], ALU.mult),
                             reads=[bpo, brc], writes=[b_OT])

                    if t < 4:
                        kk, bkk = kcT[a][:, s_, t * P:(t + 1) * P], B['kcT'][a]
                        vv, bvv = vcb[a][:, s_, t, :], B['vcb'][a]
                    else:
                        kk, bkk = kT[a][:, 2048:2176], B['kT'][a]
                        vv, bvv = vt[a][:, 16, :], B['vt'][a]
                    attn_tile(C, kk, bkk, qT[a][:, q0:q0 + 64], B['qT'][a], 64, vv, bvv, bias_fn, R, t == 0, t == 4, out_fn)
        out_proj(C, es, OT, b_OT, C.wo_band)
        S.end()


def phase_final(C, ni):
    nc, S = C.nc, C.S
    with ExitStack() as es:
        ident = _sb(nc, es, "identf", [P, P], F32)
        b_ident = Buf("identf")
        gv = _sb(nc, es, "gvf", [P, DC], F32)
        b_gv = Buf("gvf")
        ones_f = _sb(nc, es, "ones_ff", [P, P], F32)
        ones_bf = _sb(nc, es, "ones_bff", [P, P], BF16)
        b_onesf, b_ones = Buf("onesf"), Buf("ones")
        xt = [_sb(nc, es, f"fxt{i}", [P, DC, P], F32) for i in range(2)]
        b_xt = [Buf(f"fxt{i}") for i in range(2)]
        yn = [_sb(nc, es, f"fyn{i}", [P, DC, P], F32) for i in range(2)]
        b_yn = [Buf(f"fyn{i}") for i in range(2)]
        yo = [_sb(nc, es, f"fyo{i}", [P, D], F32) for i in range(2)]
        b_yo = [Buf(f"fyo{i}") for i in range(2)]
        sq = [_sb(nc, es, f"fsq{i}", [P, 512], BF16) for i in range(2)]
        b_sq = [Buf("fsq0"), Buf("fsq1")]
        tmp = _sb(nc, es, "ftmp", [P, 512], F32)
        b_tmp = Buf("ftmp")
        rstd = [_sb(nc, es, f"frstd{i}", [P, P], F32) for i in range(2)]
        b_rstd = [Buf("frstd0"), Buf("frstd1")]
        pss = _ps(nc, es, "fpss")
        b_pss = Buf("fpss")
        pst = [_ps(nc, es, f"fps{i}") for i in range(4)]
        b_pst = [Buf(f"fps{i}") for i in range(4)]
        S.dma(ident[:], C.ident[:, :], b_ident, writes=[b_ident])
        S.dma(gv[:], C.nrm[ni], b_gv, writes=[b_gv])
        S.dma(ones_f[:], C.ones[:, :], b_onesf, writes=[b_onesf])
        S.op('dve', lambda e: e.tensor_copy(ones_bf[:], ones_f[:]), reads=[b_onesf], writes=[b_ones])
        k = 0
        for ch in range(NCH):
            x, bx = xt[ch % 2], b_xt[ch % 2]
            y, by = yn[ch % 2], b_yn[ch % 2]
            o, bo = yo[ch % 2], b_yo[ch % 2]
            r, br = rstd[ch % 2], b_rstd[ch % 2]
            S.dma(x[:], C.XT[:, :, ch * P:(ch + 1) * P].rearrange("c p t -> p c t"), bx, writes=[bx])
            rms_tile(C, lambda dc, x=x: x[:, dc, :], bx, P, sq, b_sq, pss, b_pss, ones_bf, b_ones,
                     tmp, b_tmp, r[:], br)
            for dc in range(DC):
                S.op('dve', lambda e, y=y, x=x, r=r, dc=dc: e.scalar_tensor_tensor(
                    y[:, dc, :], x[:, dc, :], gv[:, dc:dc + 1], r[:], ALU.mult, ALU.mult),
                    reads=[bx, b_gv, br], writes=[by])
            for g in range(4):
                ps, bps = pst[k % 4], b_pst[k % 4]
                k += 1
                for j in range(4):
                    dc = g * 4 + j
                    S.op('pe', lambda e, ps=ps, j=j, y=y, dc=dc: e.transpose(
                        ps[:, j * P:(j + 1) * P], y[:, dc, :], ident[:]), reads=[by, b_ident], writes=[bps])
                if g % 2 == 0:
                    S.op('dve', lambda e, ps=ps, o=o, g=g: e.tensor_copy(o[:, g * 512:(g + 1) * 512], ps[:]),
                         reads=[bps], writes=[bo])
                else:
                    S.op('act', lambda e, ps=ps, o=o, g=g: e.copy(o[:, g * 512:(g + 1) * 512], ps[:]),
                         reads=[bps], writes=[bo])
            dst = C.yp[ch * P:(ch + 1) * P, :] if ch < 16 else C.ys[:, :]
            S.dma(dst, o[:], bo, reads=[bo])
        S.end()


def build_nc(cfg):
    nc = bass.Bass("TRN2", target_bir_lowering=False)
    C = Ctx()
    C.nc = nc
    C.xp = nc.dram_tensor("xp", [2048, D], F32, kind="ExternalInput").ap()
    C.xs = nc.dram_tensor("xs", [P, D], F32, kind="ExternalInput").ap()
    C.win = nc.dram_tensor("win", [2 * DEPTH, FC, 2, P, D], F32, kind="ExternalInput").ap()
    C.wout = nc.dram_tensor("wout", [2 * DEPTH, FC, P, D], F32, kind="ExternalInput").ap()
    C.nrm = nc.dram_tensor("nrm", [3 * DEPTH + 1, P, DC], F32, kind="ExternalInput").ap()
    C.ident = nc.dram_tensor("ident", [P, P], F32, kind="ExternalInput").ap()
    C.ones = nc.dram_tensor("ones", [P, P], F32, kind="ExternalInput").ap()
    C.yp = nc.dram_tensor("yp", [2048, D], F32, kind="ExternalOutput").ap()
    C.ys = nc.dram_tensor("ys", [P, D], F32, kind="ExternalOutput").ap()
    C.XT = nc.dram_tensor("XT", [DC, P, T], F32, kind="Internal").ap()
    C.QT = nc.dram_tensor("QT", [16, P, T], BF16, kind="Internal").ap()
    C.KT = nc.dram_tensor("KT", [16, P, T], BF16, kind="Internal").ap()
    C.VTM = nc.dram_tensor("VTM", [16, NCH, P, P], BF16, kind="Internal").ap()
    C.wqkv_band = nc.dram_tensor("wqkv_band", [48, P, D], F32, kind="ExternalInput").ap()
    C.wo_band = nc.dram_tensor("wo_band", [DC, P, D], F32, kind="ExternalInput").ap()
    C.bbias_p = nc.dram_tensor("bbias_p", [16, 5, P, P], F32, kind="ExternalInput").ap()
    C.bbias_s = nc.dram_tensor("bbias_s", [16, 2, 5, P, 64], F32, kind="ExternalInput").ap()
    C.cbk = nc.dram_tensor("cbk", [2, 512, 16, P], F32, kind="ExternalInput").ap()
    C.cbv = nc.dram_tensor("cbv", [2, 512, 16, P], F32, kind="ExternalInput").ap()
    C.wqkv_fox = nc.dram_tensor("wqkv_fox", [48, P, D], F32, kind="ExternalInput").ap()
    C.wo_fox = nc.dram_tensor("wo_fox", [DC, P, D], F32, kind="ExternalInput").ap()
    C.wf_fox = nc.dram_tensor("wf_fox", [P, DC, 16], F32, kind="ExternalInput").ap()
    C.bf_fox = nc.dram_tensor("bf_fox", [P, NCH, 16], F32, kind="ExternalInput").ap()
    C.utri = nc.dram_tensor("utri", [2, P, P], F32, kind="ExternalInput").ap()
    C.fmask = nc.dram_tensor("fmask", [P, 3, P], F32, kind="ExternalInput").ap()
    C.cfk = nc.dram_tensor("cfk", [2, 4096, 16, P], F32, kind="ExternalInput").ap()
    C.cfv = nc.dram_tensor("cfv", [2, 4096, 16, P], F32, kind="ExternalInput").ap()
    C.cfl = nc.dram_tensor("cfl", [2, 4096, 16], F32, kind="ExternalInput").ap()
    C.LFS = nc.dram_tensor("LFS", [P, NCH, 16], F32, kind="Internal").ap()
    C.pfk = nc.dram_tensor("pfk", [2048, 16, P], F32, kind="ExternalOutput").ap()
    C.pfv = nc.dram_tensor("pfv", [2048, 16, P], F32, kind="ExternalOutput").ap()
    C.pfl = nc.dram_tensor("pfl", [2048, 16], F32, kind="ExternalOutput").ap()
    C.sfk = nc.dram_tensor("sfk", [P, 16, P], F32, kind="ExternalOutput").ap()
    C.sfv = nc.dram_tensor("sfv", [P, 16, P], F32, kind="ExternalOutput").ap()
    C.sfl = nc.dram_tensor("sfl", [P, 16], F32, kind="ExternalOutput").ap()
    C.ssmp = nc.dram_tensor("ssmp", [2, 3, P, 64], F32, kind="ExternalInput").ap()
    C.ssmb = nc.dram_tensor("ssmb", [2, 2, P, 64, 16], F32, kind="ExternalInput").ap()
    C.ssmc = nc.dram_tensor("ssmc", [2, 2, P, 64, 16], F32, kind="ExternalInput").ap()
    C.ssmd = nc.dram_tensor("ssmd", [2, P, DC], F32, kind="ExternalInput").ap()
    C.ssms = nc.dram_tensor("ssms", [2, 2, 2, P, 64], F32, kind="ExternalInput").ap()
    C.wglu = nc.dram_tensor("wglu", [2, DC, 2, P, D], F32, kind="ExternalInput").ap()
    C.pos1 = nc.dram_tensor("pos1", [P, 64], F32, kind="ExternalInput").ap()
    C.rmask = nc.dram_tensor("rmask", [P, T], F32, kind="ExternalInput").ap()
    C.BBT = nc.dram_tensor("BBT", [64, 2, P, P], BF16, kind="Internal").ap()
    C.CTS = nc.dram_tensor("CTS", [64, 3, P, P], BF16, kind="Internal").ap()
    C.TAB = nc.dram_tensor("TAB", [P, 64, 4, 64], F32, kind="Internal").ap()
    C.APOW = nc.dram_tensor("APOW", [P, 64, 10], F32, kind="Internal").ap()
    C.GT = nc.dram_tensor("GT", [DC, P, T], BF16, kind="Internal").ap()
    C.ssm_out = nc.dram_tensor("ssm_out", [2, 3, 2, 64, P], F32, kind="ExternalOutput").ap()
    C.pbk = nc.dram_tensor("pbk", [512, 16, P], F32, kind="ExternalOutput").ap()
    C.pbv = nc.dram_tensor("pbv", [512, 16, P], F32, kind="ExternalOutput").ap()
    C.sbk = nc.dram_tensor("sbk", [P, 16, P], F32, kind="ExternalOutput").ap()
    C.sbv = nc.dram_tensor("sbv", [P, 16, P], F32, kind="ExternalOutput").ap()
    with ExitStack() as es:
        C.S = Sched(nc, es)
        phase_load(C)
        for i in range(cfg.get('depth', DEPTH)):
            if cfg.get('ffn', True):
                for half in range(2):
                    phase_ffn(C, 2 * i, 3 * i, half)
            if i % 3 == 0 and cfg.get('ssm', True):
                phase_ssm_prep(C, i // 3)
                if not cfg.get('prep_only', False):
                    phase_ssm_scan(C, i // 3, 3 * i + 1)
                    phase_ssm_glu(C, i // 3)
            if i % 3 == 1 and cfg.get('fox', True):
                phase_attn_proj(C, 3 * i + 1, C.wqkv_fox, C.pfk, C.pfv, C.sfk, C.sfv, 0, fox=True)
                if not cfg.get('proj_only', False):
                    phase_fox_attn(C)
            if i % 3 == 2 and cfg.get('band', True):
                phase_attn_proj(C, 3 * i + 1, C.wqkv_band, C.pbk, C.pbv, C.sbk, C.sbv, 1536)
                if not cfg.get('proj_only', False):
                    phase_band_attn(C)
            if cfg.get('ffn', True):
                for half in range(2):
                    phase_ffn(C, 2 * i + 1, 3 * i + 2, half)
        phase_final(C, 3 * DEPTH)
    return nc


def host_prep(inp):
    H = {}
    win = np.empty((2 * DEPTH, FC, 2, P, D), np.float32)
    wout = np.empty((2 * DEPTH, FC, P, D), np.float32)
    for i in range(DEPTH):
        for k, nm in enumerate(('ffn1', 'ffn2')):
            w = np.asarray(inp[nm + '_w_in'][i]).reshape(DC, P, 2, FC, P)
            win[2 * i + k] = w.transpose(3, 2, 1, 0, 4).reshape(FC, 2, P, D)
            wout[2 * i + k] = np.asarray(inp[nm + '_w_out'][i]).reshape(FC, P, D)
    H['win'] = win
    H['wout'] = wout
    nrm = np.empty((3 * DEPTH + 1, P, DC), np.float32)
    for i in range(DEPTH):
        nrm[3 * i] = np.asarray(inp['ffn1_norm'][i]).reshape(DC, P).T
        nrm[3 * i + 1] = np.asarray(inp['mix_norm'][i]).reshape(DC, P).T
        nrm[3 * i + 2] = np.asarray(inp['ffn2_norm'][i]).reshape(DC, P).T
    nrm[3 * DEPTH] = np.asarray(inp['final_norm']).reshape(DC, P).T
    H['nrm'] = nrm
    H['ident'] = np.eye(P, dtype=np.float32)
    wb = np.asarray(inp['band_w_in'][0]).reshape(DC, P, 48, P)
    H['wqkv_band'] = np.ascontiguousarray(wb.transpose(2, 1, 0, 3)).reshape(48, P, D)
    wo = np.asarray(inp['band_w_out'][0]).reshape(16, P, DC, P)
    H['wo_band'] = np.ascontiguousarray(wo.transpose(2, 1, 0, 3)).reshape(DC, P, D)
    def pk(a):
        return np.ascontiguousarray(np.asarray(a, np.float32).reshape(64, 2, 64).transpose(1, 2, 0)).reshape(P, 64)
    H['ssmp'] = np.stack([np.stack([pk(inp['ssm_a_re'][j]), pk(inp['ssm_a_im'][j]),
                                    pk(np.repeat(np.asarray(inp['ssm_log_step'][j])[:, None], 64, 1))], 0) for j in range(2)], 0)
    def pkb(b):
        return np.ascontiguousarray(np.asarray(b, np.float32).reshape(64, 2, 64, 16).transpose(1, 2, 0, 3)).reshape(P, 64, 16)
    def pkc(c_):
        return np.ascontiguousarray(np.asarray(c_, np.float32).reshape(64, 2, 16, 64).transpose(1, 3, 0, 2)).reshape(P, 64, 16)
    H['ssmb'] = np.stack([np.stack([pkb(inp['ssm_b_re'][j]), pkb(inp['ssm_b_im'][j])], 0) for j in range(2)], 0)
    H['ssmc'] = np.stack([np.stack([pkc(inp['ssm_c_re'][j]), pkc(inp['ssm_c_im'][j])], 0) for j in range(2)], 0)
    H['ssmd'] = np.stack([np.asarray(inp['ssm_d'][j], np.float32).reshape(DC, P).T for j in range(2)], 0).copy()
    wg = np.stack([np.asarray(inp['ssm_w_glu'][j]).reshape(DC, P, 2, DC, P).transpose(3, 2, 1, 0, 4).reshape(DC, 2, P, D)
                   for j in range(2)], 0)
    H['wglu'] = np.ascontiguousarray(wg)
    H['pos1'] = np.ascontiguousarray(np.broadcast_to(np.arange(1, 65, dtype=np.float32), (P, 64)))
    rm = np.ones((P, T), np.float32)
    rm[:, ::64] = 0.0
    H['rmask'] = rm
    H['_pk'] = pk
    wfx = np.asarray(inp['fox_w_in'][0])
    H['wqkv_fox'] = np.ascontiguousarray(wfx[:, :3 * D].reshape(DC, P, 48, P).transpose(2, 1, 0, 3)).reshape(48, P, D)
    H['wf_fox'] = np.ascontiguousarray(wfx[:, 3 * D:].reshape(DC, P, 16).transpose(1, 0, 2))
    wo = np.asarray(inp['fox_w_out'][0]).reshape(16, P, DC, P)
    H['wo_fox'] = np.ascontiguousarray(wo.transpose(2, 1, 0, 3)).reshape(DC, P, D)
    H['bf_fox'] = np.ascontiguousarray(np.broadcast_to(np.asarray(inp['fox_b_f'][0], np.float32), (P, NCH, 16)))
    ii = np.arange(P)
    u1 = (ii[:, None] <= ii[None, :]).astype(np.float32)
    u2 = u1 * ((ii[:, None] // 64) == (ii[None, :] // 64))
    H['utri'] = np.stack([u1, u2.astype(np.float32)], 0)
    fm = np.zeros((P, 3, P), np.float32)
    fm[:, 0, :] = u1
    for s_ in range(2):
        kl = ii[:, None] - 64 * s_
        fm[:, 1 + s_, :64] = ((kl >= 0) & (kl < 64) & (kl <= np.arange(64)[None, :])).astype(np.float32)
    H['fmask'] = fm
    rb = np.asarray(inp['band_rel_bias'][0])
    kk = np.arange(P)[:, None]
    bp = np.empty((16, 5, P, P), np.float32)
    for t in range(5):
        qq = np.arange(P)[None, :]
        idx = np.clip(512 - 128 * t + qq - kk, -256, 256) + 256
        v = rb[:, idx]
        if t == 0:
            v = np.where(((kk < 64) & (qq >= 64))[None], np.float32(NEG), v)
        if t == 4:
            v = np.where(((kk >= 64) & (qq < 64))[None], np.float32(NEG), v)
        bp[:, t] = v
    H['bbias_p'] = bp
    bs = np.empty((16, 2, 5, P, 64), np.float32)
    qq = np.arange(64)[None, :]
    for t in range(4):
        idx = np.clip(512 + qq - 128 * t - kk, -256, 256) + 256
        bs[:, 0, t] = rb[:, idx]
        bs[:, 1, t] = rb[:, idx]
    for s_ in range(2):
        kl = kk - 64 * s_
        idx = np.clip(qq - kl, -256, 256) + 256
        v = rb[:, idx]
        bs[:, s_, 4] = np.where(((kl < 0) | (kl >= 64))[None], np.float32(NEG), v)
    H['bbias_s'] = bs
    H['ones'] = np.ones((P, P), np.float32)
    return H


def core_inputs(inp, H, c):
    m = {k_: v for k_, v in H.items() if not k_.startswith('_')}
    pk = H['_pk']
    m['ssms'] = np.stack([np.stack([np.stack([pk(inp['state_ssm_re'][j, 2 * c + s_]), pk(inp['state_ssm_im'][j, 2 * c + s_])], 0)
                                    for s_ in range(2)], 0) for j in range(2)], 0)
    m['xp'] = np.ascontiguousarray(inp['x_prompt'][c])
    m['xs'] = np.ascontiguousarray(np.asarray(inp['x_sample'][2 * c:2 * c + 2]).reshape(P, D))
    m['cfk'] = np.ascontiguousarray(inp['cache_fox_k'][0, 2 * c:2 * c + 2])
    m['cfv'] = np.ascontiguousarray(inp['cache_fox_v'][0, 2 * c:2 * c + 2])
    m['cfl'] = np.ascontiguousarray(inp['cache_fox_logf'][0, 2 * c:2 * c + 2])
    m['cbk'] = np.ascontiguousarray(inp['cache_band_k'][0, 2 * c:2 * c + 2])
    m['cbv'] = np.ascontiguousarray(inp['cache_band_v'][0, 2 * c:2 * c + 2])
    return m


def kernel(**inp):
    nc = build_nc({})
    H = host_prep(inp)
    in_maps = [core_inputs(inp, H, c) for c in range(NCORES)]
    res = run_bass_kernel_spmd(nc, in_maps, core_ids=list(range(NCORES)))
    R_ = res.results
    f32 = np.float32
    yp = np.stack([r['yp'] for r in R_], 0).astype(f32, copy=False)
    ys = np.concatenate([r['ys'].reshape(2, 64, D) for r in R_], 0).astype(f32, copy=False)
    so = [np.asarray(r['ssm_out']) for r in R_]
    p_re = np.stack([np.stack([so[c][j, 0, 0].reshape(128, 64) for c in range(NCORES)], 0) for j in range(2)], 0)
    p_im = np.stack([np.stack([so[c][j, 0, 1].reshape(128, 64) for c in range(NCORES)], 0) for j in range(2)], 0)
    s_re = np.stack([np.stack([so[c][j, 1 + s_, 0].reshape(128, 64) for c in range(NCORES) for s_ in range(2)], 0) for j in range(2)], 0)
    s_im = np.stack([np.stack([so[c][j, 1 + s_, 1].reshape(128, 64) for c in range(NCORES) for s_ in range(2)], 0) for j in range(2)], 0)
    stk = lambda nm: np.stack([np.asarray(r[nm]) for r in R_], 0)[None]
    cat = lambda nm, tail: np.concatenate([np.asarray(r[nm]).reshape((2, 64) + tail) for r in R_], 0)[None]
    return (yp, ys, p_re.astype(f32), p_im.astype(f32),
            stk('pfk'), stk('pfv'), stk('pfl'), stk('pbk'), stk('pbv'),
            s_re.astype(f32), s_im.astype(f32),
            cat('sfk', (16, P)), cat('sfv', (16, P)), cat('sfl', (16,)),
            cat('sbk', (16, P)), cat('sbv', (16, P)))
```

```python
import os
import numpy as np
from contextlib import ExitStack
import concourse.bass as bass
import concourse.mybir as mybir
from concourse.bass_utils import run_bass_kernel_spmd

F32 = mybir.dt.float32
BF16 = mybir.dt.bfloat16
AF = mybir.ActivationFunctionType
ALU = mybir.AluOpType

P = 128
D = 2048
DC = 16
DFF = 5632
FC = 44
T = 2176
NCH = 17
DEPTH = 4
EPS = 1e-6
NCORES = 8
TILES_ALL = [(0, 512), (512, 512), (1024, 512), (1536, 512), (2048, 128)]
HALF_TILES = [[(0, 512), (512, 512), (1024, 64)], [(1088, 512), (1600, 512), (2112, 64)]]
HALF_W = 1088

ENG4 = ('pe', 'act', 'dve', 'pool')
NSETS = 5
SET_LIMIT = 20000
NDSEM = 64


class Buf:
    __slots__ = ('name', 'last_w', 'readers', 'dsem')

    def __init__(self, name):
        self.name = name
        self.last_w = None
        self.readers = []
        self.dsem = None


class Op:
    __slots__ = ('eng', 'fn', 'cdeps', 'dwaits', 'dma_sem', 'need_inc', 'inc_val', 'idx', 'is_dma')


class Sched:
    def __init__(self, nc, es):
        self.nc = nc
        self.esems = {e: [es.enter_context(nc.semaphore(f"es_{e}_{k}")) for k in range(NSETS)] for e in ENG4}
        self.eset = 0
        self.ecount = {e: 0 for e in ENG4}
        self.dpool = [[es.enter_context(nc.semaphore(f"ds_{k}")), 0] for k in range(NDSEM)]
        self.dnext = 0
        self.engobj = {'pe': nc.tensor, 'act': nc.scalar, 'dve': nc.vector, 'pool': nc.gpsimd, 'sp': nc.sync}
        self.begin()

    def begin(self):
        self.ops = []
        self.nops = {e: 0 for e in ENG4}
        self.lastop = {e: None for e in ENG4}
        self.waited = {e: {} for e in ('pe', 'act', 'dve', 'pool', 'sp')}
        self.used_dsems = []
        if max(self.ecount.values()) > SET_LIMIT:
            self.eset += 1
            self.ecount = {e: 0 for e in ENG4}

    def _deps(self, eng, reads, writes):
        prods = []
        for b in reads:
            if b.last_w is not None:
                prods.append(b.last_w)
        for b in writes:
            if b.last_w is not None:
                prods.append(b.last_w)
            prods.extend(b.readers)
        cbest = {}
        dwaits = {}
        w = self.waited[eng]
        for p in prods:
            if p.is_dma:
                ent = p.dma_sem
                val = ent[1] * 16
                if w.get(id(ent), 0) < val:
                    dwaits[id(ent)] = (ent, val)
            else:
                if p.eng == eng and eng == 'pe':
                    continue
                if w.get(p.eng, -1) >= p.idx:
                    continue
                if p.eng not in cbest or cbest[p.eng].idx < p.idx:
                    cbest[p.eng] = p
        for e, p in cbest.items():
            w[e] = p.idx
        for k, (ent, val) in dwaits.items():
            w[k] = val
        return list(cbest.values()), list(dwaits.values())

    def _post(self, o, reads, writes):
        for b in reads:
            b.readers.append(o)
        for b in writes:
            b.last_w = o
            b.readers = []
        self.ops.append(o)

    def op(self, eng, fn, reads=(), writes=()):
        o = Op()
        o.eng = eng
        o.fn = fn
        o.is_dma = False
        o.dma_sem = None
        o.need_inc = False
        o.inc_val = 0
        o.cdeps, o.dwaits = self._deps(eng, reads, writes)
        o.idx = self.nops[eng]
        self.nops[eng] += 1
        self.lastop[eng] = o
        self._post(o, reads, writes)
        return o

    def dma(self, out, in_, sbuf, reads=(), writes=(), q='sp'):
        if sbuf.dsem is None:
            sbuf.dsem = self.dpool[self.dnext % NDSEM]
            self.dnext += 1
            self.used_dsems.append(sbuf.dsem)
        o = Op()
        o.eng = q
        o.fn = lambda e, out=out, in_=in_: e.dma_start(out=out, in_=in_)
        o.is_dma = True
        o.need_inc = False
        o.inc_val = 0
        o.cdeps, o.dwaits = self._deps(q, reads, writes)
        o.dma_sem = sbuf.dsem
        sbuf.dsem[1] += 1
        o.idx = -1
        self._post(o, reads, writes)
        return o

    def end(self):
        lasts = [o for o in self.lastop.values() if o is not None]
        dents = list({id(e): e for e in self.used_dsems}.values())
        for eng in ('pe', 'act', 'dve', 'pool', 'sp'):
            o = Op()
            o.eng = eng
            o.fn = None
            o.is_dma = False
            o.dma_sem = None
            o.need_inc = False
            o.inc_val = 0
            o.idx = -2
            w = self.waited[eng]
            o.cdeps = [p for p in lasts if p.eng != eng and w.get(p.eng, -1) < p.idx]
            o.dwaits = [(ent, ent[1] * 16) for ent in dents if w.get(id(ent), 0) < ent[1] * 16]
            self.ops.append(o)
        for o in self.ops:
            for p in o.cdeps:
                p.need_inc = True
        for o in self.ops:
            if o.need_inc:
                self.ecount[o.eng] += 1
                o.inc_val = self.ecount[o.eng]
        for o in self.ops:
            e = self.engobj[o.eng]
            for p in o.cdeps:
                e.wait_ge(self.esems[p.eng][self.eset], p.inc_val)
            for ent, val in o.dwaits:
                e.wait_ge(ent[0], val)
            if o.fn is None:
                continue
            ins = o.fn(e)
            if o.is_dma:
                ins.then_inc(o.dma_sem[0], 16)
            elif o.need_inc:
                ins.then_inc(self.esems[o.eng][self.eset], 1)
        self.begin()


class Ctx:
    pass


_UID = [0]


def _sb(nc, es, name, shape, dt):
    _UID[0] += 1
    return es.enter_context(nc.sbuf_tensor(f"s{_UID[0]}_{name}", shape, dt))


def _ps(nc, es, name, dt=F32, w=512):
    _UID[0] += 1
    return es.enter_context(nc.psum_tensor(f"p{_UID[0]}_{name}", [P, w], dt))


def phase_load(C):
    nc, S = C.nc, C.S
    with ExitStack() as es:
        ident = _sb(nc, es, "ident", [P, P], F32)
        b_ident = Buf("ident")
        xin = [_sb(nc, es, f"xin{i}", [P, D], F32) for i in range(2)]
        b_xin = [Buf(f"xin{i}") for i in range(2)]
        stg = [_sb(nc, es, f"lstg{i}", [P, DC, P], F32) for i in range(2)]
        b_stg = [Buf(f"lstg{i}") for i in range(2)]
        pst = [_ps(nc, es, f"lps{i}") for i in range(4)]
        b_pst = [Buf(f"lps{i}") for i in range(4)]
        S.dma(ident[:], C.ident[:, :], b_ident, writes=[b_ident])
        k = 0
        for ch in range(NCH):
            src = C.xp[ch * P:(ch + 1) * P, :] if ch < 16 else C.xs[:, :]
            xi, bxi = xin[ch % 2], b_xin[ch % 2]
            st, bst = stg[ch % 2], b_stg[ch % 2]
            S.dma(xi[:], src, bxi, writes=[bxi])
            for g in range(4):
                ps, bps = pst[k % 4], b_pst[k % 4]
                k += 1
                for j in range(4):
                    dc = g * 4 + j
                    S.op('pe', lambda e, ps=ps, j=j, xi=xi, dc=dc: e.transpose(
                        ps[:, j * P:(j + 1) * P], xi[:, dc * P:(dc + 1) * P], ident[:]),
                        reads=[bxi, b_ident], writes=[bps])
                eng = 'dve' if g % 2 == 0 else 'act'
                if eng == 'dve':
                    S.op('dve', lambda e, ps=ps, st=st, g=g: e.tensor_copy(
                        st[:, g * 4:(g + 1) * 4, :], ps[:].rearrange("p (a b) -> p a b", a=4)),
                        reads=[bps], writes=[bst])
                else:
                    S.op('act', lambda e, ps=ps, st=st, g=g: e.copy(
                        st[:, g * 4:(g + 1) * 4, :], ps[:].rearrange("p (a b) -> p a b", a=4)),
                        reads=[bps], writes=[bst])
            S.dma(C.XT[:, :, ch * P:(ch + 1) * P].rearrange("c p t -> p c t"), st[:], bst, reads=[bst])
        S.end()


def rms_tile(C, x_sl, b_x, w, sq, b_sq, ps_ss, b_ss, ones_bf, b_ones, tmp, b_tmp, rstd_sl, b_rstd):
    S = C.S
    for dc in range(DC):
        s, bs = sq[dc % 2], b_sq[dc % 2]
        S.op('act', lambda e, s=s, dc=dc: e.activation(s[:, :w], x_sl(dc), AF.Square),
             reads=[b_x(dc) if callable(b_x) else b_x], writes=[bs])
        S.op('pe', lambda e, s=s, dc=dc: e.matmul(ps_ss[:, :w], ones_bf[:], s[:, :w],
                                                 start=(dc == 0), stop=(dc == DC - 1)),
             reads=[bs, b_ones], writes=[b_ss])
    S.op('dve', lambda e: e.tensor_scalar(tmp[:, :w], ps_ss[:, :w], 1.0 / D, EPS, ALU.mult, ALU.add),
         reads=[b_ss], writes=[b_tmp])
    S.op('act', lambda e: e.activation(tmp[:, :w], tmp[:, :w], AF.Sqrt), reads=[b_tmp], writes=[b_tmp])
    S.op('dve', lambda e: e.reciprocal(rstd_sl, tmp[:, :w]), reads=[b_tmp], writes=[b_rstd])


def phase_ffn(C, fi, ni, half):
    nc, S = C.nc, C.S
    tiles = HALF_TILES[half]
    t0 = half * HALF_W
    FB = 2
    NB = FC // FB
    with ExitStack() as es:
        acc = _sb(nc, es, "acc", [P, DC, HALF_W], F32)
        xn = _sb(nc, es, "xn", [P, DC, HALF_W], BF16)
        rstd = _sb(nc, es, "rstd", [P, HALF_W], F32)
        gv = _sb(nc, es, "gv", [P, DC], F32)
        ones_bf = _sb(nc, es, "ones_bf", [P, P], BF16)
        ones_f = _sb(nc, es, "ones_f", [P, P], F32)
        sq = [_sb(nc, es, f"sq{i}", [P, 512], BF16) for i in range(2)]
        tmp = _sb(nc, es, "tmp", [P, 512], F32)
        sg = [_sb(nc, es, f"sg{i}", [P, 512], F32) for i in range(2)]
        NSTG = 3
        stg = [_sb(nc, es, f"stg{i}", [P, D], F32) for i in range(NSTG)]
        wgb = [_sb(nc, es, f"wgb{i}", [P, D], BF16) for i in range(3)]
        wub = [_sb(nc, es, f"wub{i}", [P, D], BF16) for i in range(3)]
        wob = [_sb(nc, es, f"wob{i}", [P, D], BF16) for i in range(2 * FB)]
        hb = _sb(nc, es, "hb", [P, 2, FB, HALF_W], BF16)
        psg = [_ps(nc, es, f"psg{i}") for i in range(2)]
        psu = [_ps(nc, es, f"psu{i}") for i in range(2)]
        pso = [_ps(nc, es, f"pso{i}") for i in range(3)]
        pss = _ps(nc, es, "pss")

        b_acc = [[Buf(f"acc{d}_{t}") for t in range(3)] for d in range(DC)]
        b_xn = [Buf(f"xn{t}") for t in range(3)]
        b_rstd = [Buf(f"rstd{t}") for t in range(3)]
        b_gv, b_ones, b_onesf, b_tmp, b_pss = Buf("gv"), Buf("ones"), Buf("onesf"), Buf("tmp"), Buf("pss")
        b_sq = [Buf("sq0"), Buf("sq1")]
        b_sg = [Buf("sg0"), Buf("sg1")]
        b_stg = [Buf(f"stg{i}") for i in range(NSTG)]
        b_wgb = [Buf(f"wgb{i}") for i in range(3)]
        b_wub = [Buf(f"wub{i}") for i in range(3)]
        b_wob = [Buf(f"wob{i}") for i in range(2 * FB)]
        b_h = [[[Buf(f"h{a}{c}{t}") for t in range(3)] for c in range(FB)] for a in range(2)]
        b_psg = [Buf("psg0"), Buf("psg1")]
        b_psu = [Buf("psu0"), Buf("psu1")]
        b_pso = [Buf("pso0"), Buf("pso1"), Buf("pso2")]
        b_accall = [b for row in b_acc for b in row]

        S.dma(gv[:], C.nrm[ni], b_gv, writes=[b_gv])
        S.dma(ones_f[:], C.ones[:, :], b_onesf, writes=[b_onesf])
        S.op('dve', lambda e: e.tensor_copy(ones_bf[:], ones_f[:]), reads=[b_onesf], writes=[b_ones])
        for dc in range(DC):
            S.dma(acc[:, dc, :], C.XT[dc, :, t0:t0 + HALF_W], b_acc[dc][0], writes=b_acc[dc])

        items = []
        for b in range(NB + 1):
            for cl in range(FB):
                if b < NB:
                    c = b * FB + cl
                    items.append(('g', c, C.win[fi, c, 0], wgb[c % 3], b_wgb[c % 3]))
                    items.append(('u', c, C.win[fi, c, 1], wub[c % 3], b_wub[c % 3]))
                if cl == 0 and b >= 1:
                    for cl2 in range(FB):
                        c2 = (b - 1) * FB + cl2
                        slot = ((b - 1) % 2) * FB + cl2
                        items.append(('o', c2, C.wout[fi, c2], wob[slot], b_wob[slot]))
        state = {'n': 0}

        def fetch_upto(n):
            while state['n'] < min(n, len(items)):
                i = state['n']
                _, _, src, dst, bdst = items[i]
                st, bst = stg[i % NSTG], b_stg[i % NSTG]
                S.dma(st[:], src, bst, writes=[bst])
                S.op('pool', lambda e, dst=dst, st=st: e.tensor_copy(dst[:], st[:]), reads=[bst], writes=[bdst])
                state['n'] += 1

        fetch_upto(NSTG)

        for ti, (ts, w) in enumerate(tiles):
            lo = ts - t0
            rms_tile(C, lambda dc, lo=lo, w=w: acc[:, dc, lo:lo + w], (lambda dc, ti=ti: b_acc[dc][ti]), w, sq, b_sq, pss, b_pss,
                     ones_bf, b_ones, tmp, b_tmp, rstd[:, lo:lo + w], b_rstd[ti])
            for dc in range(DC):
                S.op('dve', lambda e, dc=dc, lo=lo, w=w: e.scalar_tensor_tensor(
                    xn[:, dc, lo:lo + w], acc[:, dc, lo:lo + w], gv[:, dc:dc + 1], rstd[:, lo:lo + w],
                    ALU.mult, ALU.mult),
                    reads=[b_acc[dc][ti], b_gv, b_rstd[ti]], writes=[b_xn[ti]])

        evt = [_sb(nc, es, f"evt{i}", [P, 512], F32) for i in range(2)]
        b_evt = [Buf(f"evt{i}") for i in range(2)]
        cnt = {'k': 0, 'ko': 0, 'ev': 0}

        def glu_unit(b, cl, ti):
            par = b % 2
            c = b * FB + cl
            ts, w = tiles[ti]
            lo = ts - t0
            wg, bwg, wu, bwu = wgb[c % 3], b_wgb[c % 3], wub[c % 3], b_wub[c % 3]
            k = cnt['k']
            cnt['k'] += 1
            pg, bpg, pu, bpu = psg[k % 2], b_psg[k % 2], psu[k % 2], b_psu[k % 2]
            sgt, bsg = sg[k % 2], b_sg[k % 2]
            for dc in range(DC):
                S.op('pe', lambda e, dc=dc: e.matmul(pg[:, :w], wg[:, dc * P:(dc + 1) * P], xn[:, dc, lo:lo + w],
                                                     start=(dc == 0), stop=(dc == DC - 1)), reads=[bwg, b_xn[ti]], writes=[bpg])
            for dc in range(DC):
                S.op('pe', lambda e, dc=dc: e.matmul(pu[:, :w], wu[:, dc * P:(dc + 1) * P], xn[:, dc, lo:lo + w],
                                                     start=(dc == 0), stop=(dc == DC - 1)), reads=[bwu, b_xn[ti]], writes=[bpu])
            S.op('act', lambda e: e.activation(sgt[:, :w], pg[:, :w], AF.Silu), reads=[bpg], writes=[bsg])
            S.op('dve', lambda e: e.tensor_tensor(hb[:, par, cl, lo:lo + w], sgt[:, :w], pu[:, :w], ALU.mult),
                 reads=[bsg, bpu], writes=[b_h[par][cl][ti]])

        def out_group(b, d, ti):
            par = b % 2
            ts, w = tiles[ti]
            lo = ts - t0
            ko = cnt['ko']
            cnt['ko'] += 1
            po, bpo = pso[ko % 3], b_pso[ko % 3]
            for cl in range(FB):
                slot = par * FB + cl
                S.op('pe', lambda e, cl=cl, slot=slot: e.matmul(
                    po[:, :w], wob[slot][:, d * P:(d + 1) * P], hb[:, par, cl, lo:lo + w],
                    start=(cl == 0), stop=(cl == FB - 1)), reads=[b_wob[slot], b_h[par][cl][ti]], writes=[bpo])
            if ko % 3 == 2 and w == 512:
                ev = cnt['ev']
                cnt['ev'] += 1
                et, bet = evt[ev % 2], b_evt[ev % 2]
                S.op('act', lambda e: e.mul(et[:, :w], po[:, :w], 0.5), reads=[bpo], writes=[bet])
                S.op('pool', lambda e: e.tensor_tensor(acc[:, d, lo:lo + w], acc[:, d, lo:lo + w], et[:, :w], ALU.add),
                     reads=[bet, b_acc[d][ti]], writes=[b_acc[d][ti]])
            else:
                S.op('dve', lambda e: e.scalar_tensor_tensor(
                    acc[:, d, lo:lo + w], po[:, :w], 0.5, acc[:, d, lo:lo + w], ALU.mult, ALU.add),
                    reads=[bpo, b_acc[d][ti]], writes=[b_acc[d][ti]])

        for b in range(NB + 1):
            units = [(cl, ti) for cl in range(FB) for ti in range(3)] if b < NB else []
            groups = [(d, ti) for d in range(DC) for ti in range(3)] if b >= 1 else []
            nu = max(len(units), 1)
            per = (len(groups) + nu - 1) // nu
            gi = 0
            for ui in range(nu):
                if units:
                    cl, ti = units[ui]
                    if ti == 0:
                        state['need'] = state.get('need', 0) + 2 + (FB if (cl == 0 and b >= 1) else 0)
                        fetch_upto(state['need'] + 4)
                    glu_unit(b, cl, ti)
                elif ui == 0:
                    fetch_upto(len(items))
                for _ in range(per):
                    if gi < len(groups):
                        out_group(b - 1, groups[gi][0], groups[gi][1])
                        gi += 1
        for dc in range(DC):
            S.dma(C.XT[dc, :, t0:t0 + HALF_W], acc[:, dc, :], b_acc[dc][0], reads=b_acc[dc])
        S.end()


ATTN_SCALE = 128 ** -0.5
NEG = -30000.0


def load_xn_all(C, es, ni, xn, b_xn):
    nc, S = C.nc, C.S
    x32 = _sb(nc, es, "x32", [P, DC, 512], F32)
    b_x32 = Buf("x32")
    rstd = _sb(nc, es, "rstdm", [P, 512], F32)
    b_rstd = Buf("rstdm")
    gv = _sb(nc, es, "gvm", [P, DC], F32)
    b_gv = Buf("gvm")
    ones_f = _sb(nc, es, "ones_fm", [P, P], F32)
    ones_bf = _sb(nc, es, "ones_bfm", [P, P], BF16)
    b_onesf, b_ones = Buf("onesf"), Buf("ones")
    sq = [_sb(nc, es, f"sqm{i}", [P, 512], BF16) for i in range(2)]
    b_sq = [Buf("sqm0"), Buf("sqm1")]
    tmp = _sb(nc, es, "tmpm", [P, 512], F32)
    b_tmp = Buf("tmpm")
    pss = _ps(nc, es, "pssm")
    b_pss = Buf("pssm")
    S.dma(gv[:], C.nrm[ni], b_gv, writes=[b_gv])
    S.dma(ones_f[:], C.ones[:, :], b_onesf, writes=[b_onesf])
    S.op('dve', lambda e: e.tensor_copy(ones_bf[:], ones_f[:]), reads=[b_onesf], writes=[b_ones])
    for ti, (ts, w) in enumerate(TILES_ALL):
        S.dma(x32[:, :, :w], C.XT[:, :, ts:ts + w].rearrange("c p t -> p c t"), b_x32, writes=[b_x32])
        rms_tile(C, lambda dc, w=w: x32[:, dc, :w], b_x32, w, sq, b_sq, pss, b_pss, ones_bf, b_ones,
                 tmp, b_tmp, rstd[:, :w], b_rstd)
        for dc in range(DC):
            S.op('dve', lambda e, dc=dc, ts=ts, w=w: e.scalar_tensor_tensor(
                xn[:, dc, ts:ts + w], x32[:, dc, :w], gv[:, dc:dc + 1], rstd[:, :w], ALU.mult, ALU.mult),
                reads=[b_x32, b_gv, b_rstd], writes=[b_xn[ti]])
    return x32, b_x32


class WStream:
    def __init__(self, C, es, srcs, nstg=3, nwb=2, tag="ws"):
        nc = C.nc
        self.C = C
        self.srcs = srcs
        self.nstg, self.nwb = nstg, nwb
        self.stg = [_sb(nc, es, f"{tag}stg{i}", [P, D], F32) for i in range(nstg)]
        self.b_stg = [Buf(f"{tag}stg{i}") for i in range(nstg)]
        self.wb = [_sb(nc, es, f"{tag}wb{i}", [P, D], BF16) for i in range(nwb)]
        self.b_wb = [Buf(f"{tag}wb{i}") for i in range(nwb)]
        self.n = 0

    def fetch_upto(self, n):
        S = self.C.S
        while self.n < min(n, len(self.srcs)):
            i = self.n
            st, bst = self.stg[i % self.nstg], self.b_stg[i % self.nstg]
            dst, bdst = self.wb[i % self.nwb], self.b_wb[i % self.nwb]
            S.dma(st[:], self.srcs[i], bst, writes=[bst])
            S.op('pool', lambda e, dst=dst, st=st: e.tensor_copy(dst[:], st[:]), reads=[bst], writes=[bdst])
            self.n += 1

    def get(self, i):
        self.fetch_upto(i + self.nwb)
        return self.wb[i % self.nwb], self.b_wb[i % self.nwb]


def phase_attn_proj(C, ni, wqkv, kout, vout, skout, svout, prow0, fox=False):
    nc, S = C.nc, C.S
    with ExitStack() as es:
        xn = _sb(nc, es, "xna", [P, DC, T], BF16)
        b_xn = [Buf(f"xna{t}") for t in range(5)]
        load_xn_all(C, es, ni, xn, b_xn)
        ident = _sb(nc, es, "identa", [P, P], F32)
        b_ident = Buf("identa")
        S.dma(ident[:], C.ident[:, :], b_ident, writes=[b_ident])
        ws = WStream(C, es, [wqkv[i] for i in range(48)])
        rowf = [_sb(nc, es, f"rowf{i}", [P, T], F32) for i in range(2)]
        b_rowf = [Buf(f"rowf{i}") for i in range(2)]
        rowb = [_sb(nc, es, f"rowb{i}", [P, T], BF16) for i in range(2)]
        b_rowb = [Buf(f"rowb{i}") for i in range(2)]
        tm = [_sb(nc, es, f"tm{i}", [P, NCH, P], F32) for i in range(2)]
        b_tm = [Buf(f"tm{i}") for i in range(2)]
        tmb = [_sb(nc, es, f"tmb{i}", [P, NCH, P], BF16) for i in range(2)]
        b_tmb = [Buf(f"tmb{i}") for i in range(2)]
        psm = [_ps(nc, es, f"psm{i}") for i in range(2)]
        b_psm = [Buf(f"psm{i}") for i in range(2)]
        pst = [_ps(nc, es, f"psta{i}") for i in range(2)]
        b_pst = [Buf(f"psta{i}") for i in range(2)]
        if fox:
            fox_logf(C, es, xn, b_xn)
        ws.fetch_upto(2)
        km = 0
        kt = 0
        for cc in range(int(os.environ.get('NCC', 48))):
            which, h = cc // 16, cc % 16
            wb, bwb = ws.get(cc)
            rf, brf = rowf[cc % 2], b_rowf[cc % 2]
            rb, brb = rowb[cc % 2], b_rowb[cc % 2]
            for ti, (ts, w) in enumerate(TILES_ALL):
                pm, bpm = psm[km % 2], b_psm[km % 2]
                km += 1
                for dc in range(DC):
                    S.op('pe', lambda e, pm=pm, wb=wb, dc=dc, ts=ts, w=w: e.matmul(
                        pm[:, :w], wb[:, dc * P:(dc + 1) * P], xn[:, dc, ts:ts + w],
                        start=(dc == 0), stop=(dc == DC - 1)), reads=[bwb, b_xn[ti]], writes=[bpm])
                if which == 0:
                    S.op('act', lambda e, pm=pm, rb=rb, ts=ts, w=w: e.mul(rb[:, ts:ts + w], pm[:, :w], ATTN_SCALE),
                         reads=[bpm], writes=[brb])
                else:
                    S.op('act', lambda e, pm=pm, rf=rf, ts=ts, w=w: e.copy(rf[:, ts:ts + w], pm[:, :w]),
                         reads=[bpm], writes=[brf])
                    if which == 1:
                        S.op('dve', lambda e, rf=rf, rb=rb, ts=ts, w=w: e.tensor_copy(rb[:, ts:ts + w], rf[:, ts:ts + w]),
                             reads=[brf], writes=[brb])
            if which == 0:
                S.dma(C.QT[h], rb[:], brb, reads=[brb])
                continue
            if which == 1:
                S.dma(C.KT[h], rb[:], brb, reads=[brb])
            t_, bt_ = tm[cc % 2], b_tm[cc % 2]
            for ch in range(NCH if not os.environ.get('SKIP_TR') else 0):
                pt, bpt = pst[kt % 2], b_pst[kt % 2]
                kt += 1
                S.op('pe', lambda e, pt=pt, rf=rf, ch=ch: e.transpose(pt[:, :P], rf[:, ch * P:(ch + 1) * P], ident[:]),
                     reads=[brf, b_ident], writes=[bpt])
                if ch % 2 == 0:
                    S.op('dve', lambda e, pt=pt, t_=t_, ch=ch: e.tensor_copy(t_[:, ch, :], pt[:, :P]),
                         reads=[bpt], writes=[bt_])
                else:
                    S.op('act', lambda e, pt=pt, t_=t_, ch=ch: e.copy(t_[:, ch, :], pt[:, :P]),
                         reads=[bpt], writes=[bt_])
            dst_p, dst_s = (kout, skout) if which == 1 else (vout, svout)
            c0 = prow0 // P
            if not os.environ.get('SKIP_OUT'):
                S.dma(dst_p[:, h, :].rearrange("(c p) d -> p c d", p=P), t_[:, c0:16, :], bt_, reads=[bt_])
                S.dma(dst_s[:, h, :], t_[:, 16, :], bt_, reads=[bt_])
            if which == 2:
                tb, btb = tmb[cc % 2], b_tmb[cc % 2]
                S.op('pool', lambda e, tb=tb, t_=t_: e.tensor_copy(tb[:], t_[:]), reads=[bt_], writes=[btb])
                S.dma(C.VTM[h].rearrange("c p d -> p c d"), tb[:], btb, reads=[btb])
        S.end()


def attn_tile(C, kT, b_kT, qT, b_qT, nq, vt, b_vt, bias_fn, R, first, last, out_fn):
    S = C.S
    i = R['k']
    R['k'] += 1
    ps, bps = R['pss'][i % 2], R['b_pss'][i % 2]
    sb_, bsb = R['sb'][i % 2], R['b_sb'][i % 2]
    pT, bpT = R['pT'][i % 2], R['b_pT'][i % 2]
    S.op('pe', lambda e: e.matmul(ps[:, :nq], kT, qT, start=True, stop=True), reads=[b_kT, b_qT], writes=[bps])
    bias_fn(ps, bps, sb_, bsb, pT, bpT)
    po, bpo, pl, bpl = R['po'], R['b_po'], R['pl'], R['b_pl']
    S.op('pe', lambda e: e.matmul(po[:, :nq], vt, pT[:, :nq], start=first, stop=last), reads=[b_vt, bpT], writes=[bpo])
    S.op('pe', lambda e: e.matmul(pl[:, :nq], R['ones'][:], pT[:, :nq], start=first, stop=last),
         reads=[R['b_ones'], bpT], writes=[bpl])
    if last:
        rc, brc = R['rc'], R['b_rc']
        S.op('dve', lambda e: e.reciprocal(rc[:, :nq], pl[:, :nq]), reads=[bpl], writes=[brc])
        out_fn(po, bpo, rc, brc)


def attn_resources(C, es, ones_bf, b_ones):
    nc = C.nc
    R = {'k': 0}
    R['pss'] = [_ps(nc, es, f"pssc{i}") for i in range(2)]
    R['b_pss'] = [Buf(f"pssc{i}") for i in range(2)]
    R['sb'] = [_sb(nc, es, f"sbs{i}", [P, P], F32) for i in range(2)]
    R['b_sb'] = [Buf(f"sbs{i}") for i in range(2)]
    R['pT'] = [_sb(nc, es, f"pT{i}", [P, P], BF16) for i in range(2)]
    R['b_pT'] = [Buf(f"pT{i}") for i in range(2)]
    R['po'] = _ps(nc, es, "po")
    R['b_po'] = Buf("po")
    R['pl'] = _ps(nc, es, "pl")
    R['b_pl'] = Buf("pl")
    R['rc'] = _sb(nc, es, "rc", [P, P], F32)
    R['b_rc'] = Buf("rc")
    R['ones'] = ones_bf
    R['b_ones'] = b_ones
    return R


def out_proj(C, es, OT, b_OT, wo):
    nc, S = C.nc, C.S
    ws = WStream(C, es, [wo[i] for i in range(DC)], tag="wo")
    xr = [_sb(nc, es, f"xr{i}", [P, 512], F32) for i in range(2)]
    b_xr = [Buf(f"xr{i}") for i in range(2)]
    pso = [_ps(nc, es, f"psop{i}") for i in range(2)]
    b_pso = [Buf(f"psop{i}") for i in range(2)]
    ws.fetch_upto(2)
    k = 0
    for d in range(DC):
        wb, bwb = ws.get(d)
        for ti, (ts, w) in enumerate(TILES_ALL):
            po, bpo = pso[k % 2], b_pso[k % 2]
            x, bx = xr[k % 2], b_xr[k % 2]
            k += 1
            S.dma(x[:, :w], C.XT[d, :, ts:ts + w], bx, writes=[bx])
            for h in range(16):
                S.op('pe', lambda e, po=po, wb=wb, h=h, ts=ts, w=w: e.matmul(
                    po[:, :w], wb[:, h * P:(h + 1) * P], OT[:, h, ts:ts + w], start=(h == 0), stop=(h == 15)),
                    reads=[bwb, b_OT], writes=[bpo])
            S.op('dve', lambda e, po=po, x=x, w=w: e.tensor_tensor(x[:, :w], x[:, :w], po[:, :w], ALU.add),
                 reads=[bpo, bx], writes=[bx])
            S.dma(C.XT[d, :, ts:ts + w], x[:, :w], bx, reads=[bx])


def fox_logf(C, es, xn, b_xn):
    nc, S = C.nc, C.S
    wf32 = _sb(nc, es, "wf32", [P, DC, 16], F32)
    wfb = _sb(nc, es, "wfb", [P, DC, 16], BF16)
    bfb = _sb(nc, es, "bfb", [P, NCH, 16], F32)
    z = _sb(nc, es, "zlf", [P, NCH, 16], F32)
    lf = _sb(nc, es, "lf", [P, NCH, 16], F32)
    plg = _ps(nc, es, "plg")
    b_wf32, b_wfb, b_bfb, b_z, b_lf, b_plg = Buf("wf32"), Buf("wfb"), Buf("bfb"), Buf("zlf"), Buf("lf"), Buf("plg")
    S.dma(wf32[:], C.wf_fox[:, :, :], b_wf32, writes=[b_wf32])
    S.dma(bfb[:], C.bf_fox[:, :, :], b_bfb, writes=[b_bfb])
    S.op('dve', lambda e: e.tensor_copy(wfb[:], wf32[:]), reads=[b_wf32], writes=[b_wfb])
    for ch in range(NCH):
        ti = min(ch // 4, 4)
        for dc in range(DC):
            S.op('pe', lambda e, ch=ch, dc=dc: e.matmul(plg[:, ch * 16:(ch + 1) * 16], xn[:, dc, ch * P:(ch + 1) * P],
                                                       wfb[:, dc, :], start=(dc == 0), stop=(dc == DC - 1)),
                 reads=[b_xn[ti], b_wfb], writes=[b_plg])
    zf = z[:].rearrange("p c h -> p (c h)")
    S.op('dve', lambda e: e.tensor_tensor(zf, plg[:, :NCH * 16], bfb[:].rearrange("p c h -> p (c h)"), ALU.add),
         reads=[b_plg, b_bfb], writes=[b_z])
    S.op('act', lambda e: e.activation(zf, zf, AF.Exp, scale=-1.0), reads=[b_z], writes=[b_z])
    S.op('dve', lambda e: e.tensor_scalar(zf, zf, 1.0, None, ALU.add), reads=[b_z], writes=[b_z])
    S.op('act', lambda e: e.activation(zf, zf, AF.Ln), reads=[b_z], writes=[b_z])
    S.op('dve', lambda e: e.tensor_scalar(lf[:].rearrange("p c h -> p (c h)"), zf, -1.0, None, ALU.mult),
         reads=[b_z], writes=[b_lf])
    S.dma(C.pfl[:, :].rearrange("(c p) h -> p c h", p=P), lf[:, 0:16, :], b_lf, reads=[b_lf])
    S.dma(C.sfl[:, :], lf[:, 16, :], b_lf, reads=[b_lf])
    S.dma(C.LFS[:, :, :], lf[:], b_lf, reads=[b_lf])


def prefix_chunks(C, es, tot, b_tot, n, tag):
    nc, S = C.nc, C.S
    a = _sb(nc, es, f"pxa{tag}", [P, n, 16], F32)
    b = _sb(nc, es, f"pxb{tag}", [P, n, 16], F32)
    b_a, b_b = Buf(f"pxa{tag}"), Buf(f"pxb{tag}")
    S.op('dve', lambda e: e.memset(a[:, 0, :], 0.0), writes=[b_a])
    S.op('dve', lambda e: e.tensor_copy(a[:, 1:n, :], tot[:, 0:n - 1, :]), reads=[b_tot, b_a], writes=[b_a])
    cur, bcur, oth, both = a, b_a, b, b_b
    sft = 1
    while sft < n:
        S.op('dve', lambda e, cur=cur, oth=oth, sft=sft: e.tensor_copy(oth[:, 0:sft, :], cur[:, 0:sft, :]),
             reads=[bcur], writes=[both])
        S.op('dve', lambda e, cur=cur, oth=oth, sft=sft: e.tensor_tensor(
            oth[:, sft:n, :], cur[:, sft:n, :], cur[:, 0:n - sft, :], ALU.add), reads=[bcur, both], writes=[both])
        cur, bcur, oth, both = oth, both, cur, bcur
        sft *= 2
    return cur, bcur


def phase_fox_attn(C):
    nc, S = C.nc, C.S
    with ExitStack() as eo, ExitStack() as es:
        OT = _sb(nc, eo, "OTf", [P, 16, T], BF16)
        BP = _sb(nc, eo, "BP", [P, 16, 16, 16], F32)
        BN = _sb(nc, eo, "BN", [P, 16], F32)
        BCt = [_sb(nc, eo, f"BC{s_}", [P, 32, 16], F32) for s_ in range(2)]
        ones_f = _sb(nc, eo, "ones_ffx", [P, P], F32)
        ones_bf = _sb(nc, eo, "ones_bffx", [P, P], BF16)
        ident = _sb(nc, eo, "identfx", [P, P], F32)
        mkb = _sb(nc, eo, "mkb", [P, 3, P], BF16)
        b_OT = Buf("OTf")
        utri = _sb(nc, es, "utri", [P, P], F32)
        utri2 = _sb(nc, es, "utri2", [P, P], F32)
        mk32 = _sb(nc, es, "mk32", [P, 3, P], F32)
        b_onesf, b_ones, b_ident, b_utri, b_utri2, b_mk32, b_mkb = (Buf("onesf"), Buf("ones"), Buf("ident"),
                                                                    Buf("utri"), Buf("utri2"), Buf("mk32"), Buf("mkb"))
        S.dma(ones_f[:], C.ones[:, :], b_onesf, writes=[b_onesf])
        S.dma(ident[:], C.ident[:, :], b_ident, writes=[b_ident])
        S.dma(utri[:], C.utri[0], b_utri, writes=[b_utri])
        S.dma(utri2[:], C.utri[1], b_utri2, writes=[b_utri2])
        S.dma(mk32[:], C.fmask[:, :, :], b_mk32, writes=[b_mk32])
        S.op('dve', lambda e: e.tensor_copy(ones_bf[:], ones_f[:]), reads=[b_onesf], writes=[b_ones])
        S.op('dve', lambda e: e.tensor_copy(mkb[:], mk32[:]), reads=[b_mk32], writes=[b_mkb])
        pcs = _ps(nc, es, "pcs")
        b_pcs = Buf("pcs")

        lf = _sb(nc, es, "lfa", [P, NCH, 16], F32)
        b_lf = Buf("lfa")
        S.dma(lf[:], C.LFS[:, :, :], b_lf, writes=[b_lf])
        inc = _sb(nc, es, "inc", [P, NCH, 16], F32)
        tot = _sb(nc, es, "tot", [P, 16, 16], F32)
        b_inc, b_tot = Buf("inc"), Buf("tot")
        lf2 = lf[:, 0:16, :].rearrange("p c h -> p (c h)")
        S.op('pe', lambda e: e.matmul(pcs[:, :256], utri[:], lf2, start=True, stop=True), reads=[b_utri, b_lf], writes=[b_pcs])
        S.op('dve', lambda e: e.tensor_copy(inc[:, 0:16, :].rearrange("p c h -> p (c h)"), pcs[:, :256]),
             reads=[b_pcs], writes=[b_inc])
        S.op('pe', lambda e: e.matmul(pcs[:, :16], utri2[:], lf[:, 16, :], start=True, stop=True),
             reads=[b_utri2, b_lf, b_inc], writes=[b_pcs])
        S.op('dve', lambda e: e.tensor_copy(inc[:, 16, :], pcs[:, :16]), reads=[b_pcs], writes=[b_inc])
        S.op('pe', lambda e: e.matmul(pcs[:, :256], ones_f[:], lf2, start=True, stop=True),
             reads=[b_onesf, b_lf, b_inc], writes=[b_pcs])
        S.op('dve', lambda e: e.tensor_copy(tot[:].rearrange("p c h -> p (c h)"), pcs[:, :256]), reads=[b_pcs], writes=[b_tot])
        E, b_E = prefix_chunks(C, es, tot, b_tot, 16, "p")
        incp = _sb(nc, es, "incp", [P, 16, 16], F32)
        b_incp = Buf("incp")
        S.op('dve', lambda e: e.tensor_tensor(incp[:], inc[:, 0:16, :], E[:], ALU.add), reads=[b_inc, b_E], writes=[b_incp])
        b_BP = Buf("BP")
        for n in range(16):
            for j in range(n + 1):
                S.op('dve', lambda e, n=n, j=j: e.tensor_tensor(BP[:, n, j, :], E[:, n, :], incp[:, j, :], ALU.subtract),
                     reads=[b_E, b_incp], writes=[b_BP])
        b_BN = Buf("BN")
        S.op('dve', lambda e: e.tensor_scalar(BN[:], inc[:, 16, :], -1.0, None, ALU.mult), reads=[b_inc], writes=[b_BN])
        BC = []
        b_BC = []
        for s_ in range(2):
            lc = _sb(nc, es, f"lc{s_}", [P, 32, 16], F32)
            b_lc = Buf(f"lc{s_}")
            S.dma(lc[:], C.cfl[s_].rearrange("(c p) h -> p c h", p=P), b_lc, writes=[b_lc])
            incc = _sb(nc, es, f"incc{s_}", [P, 32, 16], F32)
            totc = _sb(nc, es, f"totc{s_}", [P, 32, 16], F32)
            b_incc, b_totc = Buf(f"incc{s_}"), Buf(f"totc{s_}")
            lcf = lc[:].rearrange("p c h -> p (c h)")
            S.op('pe', lambda e, lcf=lcf: e.matmul(pcs[:, :512], utri[:], lcf, start=True, stop=True),
                 reads=[b_utri, b_lc, b_tot, b_inc], writes=[b_pcs])
            S.op('dve', lambda e, incc=incc: e.tensor_copy(incc[:].rearrange("p c h -> p (c h)"), pcs[:, :512]),
                 reads=[b_pcs], writes=[b_incc])
            S.op('pe', lambda e, lcf=lcf: e.matmul(pcs[:, :512], ones_f[:], lcf, start=True, stop=True),
                 reads=[b_onesf, b_lc, b_incc], writes=[b_pcs])
            S.op('dve', lambda e, totc=totc: e.tensor_copy(totc[:].rearrange("p c h -> p (c h)"), pcs[:, :512]),
                 reads=[b_pcs], writes=[b_totc])
            EC, b_EC = prefix_chunks(C, es, totc, b_totc, 32, f"c{s_}")
            tt = _sb(nc, es, f"ttot{s_}", [P, 16], F32)
            b_tt = Buf(f"ttot{s_}")
            S.op('dve', lambda e, tt=tt, EC=EC, totc=totc: e.tensor_tensor(tt[:], EC[:, 31, :], totc[:, 31, :], ALU.add),
                 reads=[b_EC, b_totc], writes=[b_tt])
            bc = BCt[s_]
            b_bc = Buf(f"BC{s_}")
            S.op('dve', lambda e, incc=incc, EC=EC: e.tensor_tensor(incc[:], incc[:], EC[:], ALU.add),
                 reads=[b_incc, b_EC], writes=[b_incc])
            for j in range(32):
                S.op('dve', lambda e, bc=bc, tt=tt, incc=incc, j=j: e.tensor_tensor(bc[:, j, :], tt[:], incc[:, j, :], ALU.subtract),
                     reads=[b_tt, b_incc], writes=[b_bc])
            BC.append(bc)
            b_BC.append(b_bc)

        S.end()
        es.close()
        for bb in [b_OT, b_ones, b_onesf, b_ident, b_mkb, b_BP, b_BN] + b_BC:
            bb.last_w, bb.readers, bb.dsem = None, [], None
        R = attn_resources(C, es, ones_bf, b_ones)
        qT = [_sb(nc, es, f"fqT{i}", [P, T], BF16) for i in range(2)]
        kT = [_sb(nc, es, f"fkT{i}", [P, T], BF16) for i in range(2)]
        vt = [_sb(nc, es, f"fvt{i}", [P, NCH, P], BF16) for i in range(2)]
        kc = [_sb(nc, es, f"fkc{i}", [P, 16, P], F32) for i in range(2)]
        vc = [_sb(nc, es, f"fvc{i}", [P, 16, P], F32) for i in range(2)]
        kcT = [_sb(nc, es, f"fkcT{i}", [P, 4096], BF16) for i in range(2)]
        vcb = [_sb(nc, es, f"fvcb{i}", [P, 32, P], BF16) for i in range(2)]
        nm = ['qT', 'kT', 'vt', 'kc', 'vc']
        B = {n_: [Buf(f"f{n_}{i}") for i in range(2)] for n_ in nm}
        b_kcT = [[Buf(f"kcT{i}_{hf}") for hf in range(2)] for i in range(2)]
        b_vcb = [[Buf(f"vcb{i}_{hf}") for hf in range(2)] for i in range(2)]
        ptr = [_ps(nc, es, f"ptrf{i}") for i in range(2)]
        b_ptr = [Buf(f"ptrf{i}") for i in range(2)]
        ktr = 0
        kld = 0
        for h in range(16):
            a = h % 2
            S.dma(qT[a][:], C.QT[h], B['qT'][a], writes=[B['qT'][a]])
            S.dma(kT[a][:], C.KT[h], B['kT'][a], writes=[B['kT'][a]])
            S.dma(vt[a][:], C.VTM[h].rearrange("c p d -> p c d"), B['vt'][a], writes=[B['vt'][a]])
            for n in range(16):
                for j in range(n + 1):
                    def bias_fn(ps, bps, sb_, bsb, pT, bpT, n=n, j=j, h=h):
                        S.op('act', lambda e: e.activation(pT[:], ps[:, :P], AF.Exp, bias=BP[:, n, j, h:h + 1]),
                             reads=[bps, b_BP], writes=[bpT])
                        if j == n:
                            S.op('pool', lambda e: e.tensor_tensor(pT[:], pT[:], mkb[:, 0, :], ALU.mult),
                                 reads=[bpT, b_mkb], writes=[bpT])

                    def out_fn(po, bpo, rc, brc, h=h, n=n):
                        S.op('dve', lambda e: e.tensor_tensor(OT[:, h, n * P:(n + 1) * P], po[:, :P], rc[:, :P], ALU.mult),
                             reads=[bpo, brc], writes=[b_OT])

                    attn_tile(C, kT[a][:, j * P:(j + 1) * P], B['kT'][a], qT[a][:, n * P:(n + 1) * P], B['qT'][a], P,
                              vt[a][:, j, :], B['vt'][a], bias_fn, R, j == 0, j == n, out_fn)
            for s_ in range(2):
                q0 = 2048 + 64 * s_
                for hf in range(2):
                    u = kld % 2
                    kld += 1
                    S.dma(kc[u][:], C.cfk[s_, hf * 2048:(hf + 1) * 2048, h, :].rearrange("(c p) d -> p c d", p=P),
                          B['kc'][u], writes=[B['kc'][u]])
                    S.dma(vc[u][:], C.cfv[s_, hf * 2048:(hf + 1) * 2048, h, :].rearrange("(c p) d -> p c d", p=P),
                          B['vc'][u], writes=[B['vc'][u]])
                    S.op('pool', lambda e, u=u, s_=s_, hf=hf: e.tensor_copy(vcb[s_][:, hf * 16:(hf + 1) * 16, :], vc[u][:]),
                         reads=[B['vc'][u]], writes=[b_vcb[s_][hf]])
                    for c in range(16):
                        pt, bpt = ptr[ktr % 2], b_ptr[ktr % 2]
                        ktr += 1
                        S.op('pe', lambda e, pt=pt, u=u, c=c: e.transpose(pt[:, :P], kc[u][:, c, :], ident[:]),
                             reads=[B['kc'][u], b_ident], writes=[bpt])
                        cg = hf * 16 + c
                        if c % 2 == 0:
                            S.op('act', lambda e, pt=pt, s_=s_, cg=cg: e.copy(kcT[s_][:, cg * P:(cg + 1) * P], pt[:, :P]),
                                 reads=[bpt], writes=[b_kcT[s_][hf]])
                        else:
                            S.op('dve', lambda e, pt=pt, s_=s_, cg=cg: e.tensor_copy(kcT[s_][:, cg * P:(cg + 1) * P], pt[:, :P]),
                                 reads=[bpt], writes=[b_kcT[s_][hf]])
                for j in range(33):
                    def bias_fn(ps, bps, sb_, bsb, pT, bpT, j=j, h=h, s_=s_):
                        if j < 32:
                            S.op('act', lambda e: e.activation(pT[:, :64], ps[:, :64], AF.Exp, bias=BC[s_][:, j, h:h + 1]),
                                 reads=[bps, b_BC[s_]], writes=[bpT])
                        else:
                            S.op('act', lambda e: e.activation(pT[:, :64], ps[:, :64], AF.Exp, bias=BN[:, h:h + 1]),
                                 reads=[bps, b_BN], writes=[bpT])
                            S.op('pool', lambda e: e.tensor_tensor(pT[:, :64], pT[:, :64], mkb[:, 1 + s_, 0:64], ALU.mult),
                                 reads=[bpT, b_mkb], writes=[bpT])

                    def out_fn(po, bpo, rc, brc, h=h, q0=q0):
                        S.op('dve', lambda e: e.tensor_tensor(OT[:, h, q0:q0 + 64], po[:, :64], rc[:, :64], ALU.mult),
                             reads=[bpo, brc], writes=[b_OT])

                    if j < 32:
                        kk, bkk = kcT[s_][:, j * P:(j + 1) * P], b_kcT[s_][j // 16]
                        vv, bvv = vcb[s_][:, j, :], b_vcb[s_][j // 16]
                    else:
                        kk, bkk = kT[a][:, 2048:2176], B['kT'][a]
                        vv, bvv = vt[a][:, 16, :], B['vt'][a]
                    attn_tile(C, kk, bkk, qT[a][:, q0:q0 + 64], B['qT'][a], 64, vv, bvv, bias_fn, R, j == 0, j == 32, out_fn)
        S.end()
        es.close()
        b_OT.last_w, b_OT.readers, b_OT.dsem = None, [], None
        out_proj(C, es, OT, b_OT, C.wo_fox)
        S.end()


import math
PI = math.pi
NCK = 34


def _sincos(C, x, b_x, n, tmp, b_tmp, itmp, s_out, b_s, c_out, b_c):
    S = C.S
    for off, dst, bd in ((0.0, s_out, b_s), (0.25, c_out, b_c)):
        S.op('dve', lambda e, off=off: e.tensor_scalar(tmp, x, 1.0 / (2 * PI), off, ALU.mult, ALU.add), reads=[b_x], writes=[b_tmp])
        S.op('dve', lambda e: e.tensor_copy(itmp, tmp), reads=[b_tmp], writes=[b_tmp])
        S.op('dve', lambda e, dst=dst: e.tensor_copy(dst, itmp), reads=[b_tmp], writes=[bd])
        S.op('dve', lambda e, dst=dst: e.tensor_tensor(tmp, tmp, dst, ALU.subtract), reads=[b_tmp, bd], writes=[b_tmp])
        S.op('dve', lambda e, dst=dst: e.tensor_scalar(dst, tmp, 0.5, None, ALU.is_gt), reads=[b_tmp], writes=[bd])
        S.op('dve', lambda e, dst=dst: e.tensor_tensor(tmp, tmp, dst, ALU.subtract), reads=[b_tmp, bd], writes=[b_tmp])
        S.op('dve', lambda e, dst=dst: e.tensor_scalar(dst, tmp, -0.5, None, ALU.is_lt), reads=[b_tmp], writes=[bd])
        S.op('dve', lambda e, dst=dst: e.tensor_tensor(tmp, tmp, dst, ALU.add), reads=[b_tmp, bd], writes=[b_tmp])
        S.op('act', lambda e, dst=dst: e.activation(dst, tmp, AF.Sin, scale=2 * PI), reads=[b_tmp], writes=[bd])


def phase_ssm_prep(C, j):
    nc, S = C.nc, C.S
    with ExitStack() as es:
        def t64(nm):
            return _sb(nc, es, nm, [P, 64], F32), Buf(nm)
        are, b_are = t64("are")
        aim, b_aim = t64("aim")
        ls, b_ls = t64("ls")
        S.dma(are[:], C.ssmp[j, 0], b_are, writes=[b_are])
        S.dma(aim[:], C.ssmp[j, 1], b_aim, writes=[b_aim])
        S.dma(ls[:], C.ssmp[j, 2], b_ls, writes=[b_ls])
        pos1, b_pos1 = t64("pos1")
        S.dma(pos1[:], C.pos1[:, :], b_pos1, writes=[b_pos1])
        br = _sb(nc, es, "br", [P, 64, 16], F32)
        bi = _sb(nc, es, "bi", [P, 64, 16], F32)
        cr = _sb(nc, es, "cr", [P, 64, 16], F32)
        ci = _sb(nc, es, "ci", [P, 64, 16], F32)
        b_br, b_bi, b_cr, b_ci = Buf("br"), Buf("bi"), Buf("cr"), Buf("ci")
        S.dma(br[:], C.ssmb[j, 0], b_br, writes=[b_br])
        S.dma(bi[:], C.ssmb[j, 1], b_bi, writes=[b_bi])
        S.dma(cr[:], C.ssmc[j, 0], b_cr, writes=[b_cr])
        S.dma(ci[:], C.ssmc[j, 1], b_ci, writes=[b_ci])
        ident = _sb(nc, es, "identp", [P, P], F32)
        b_ident = Buf("identp")
        S.dma(ident[:], C.ident[:, :], b_ident, writes=[b_ident])
        dt, b_dt = t64("dt")
        ar, b_ar = t64("ar")
        th, b_th = t64("th")
        mag, b_mag = t64("mag")
        sn, b_sn = t64("sn")
        cs, b_cs = t64("cs")
        tm_, b_tm = t64("tmq")
        abr, b_abr = t64("abr")
        abi, b_abi = t64("abi")
        den, b_den = t64("den")
        t1, b_t1 = t64("t1")
        t2, b_t2 = t64("t2")
        zr, b_zr = t64("zr")
        zi, b_zi = t64("zi")
        S.op('act', lambda e: e.activation(dt[:], ls[:], AF.Exp), reads=[b_ls], writes=[b_dt])
        S.op('dve', lambda e: e.tensor_tensor(ar[:], are[:], dt[:], ALU.mult), reads=[b_are, b_dt], writes=[b_ar])
        S.op('dve', lambda e: e.tensor_tensor(th[:], aim[:], dt[:], ALU.mult), reads=[b_aim, b_dt], writes=[b_th])
        S.op('act', lambda e: e.activation(mag[:], ar[:], AF.Exp), reads=[b_ar], writes=[b_mag])
        it64 = _sb(nc, es, "it64", [P, 64], mybir.dt.int32)
        _sincos(C, th[:], b_th, 64, tm_[:], b_tm, it64[:], sn[:], b_sn, cs[:], b_cs)
        S.op('dve', lambda e: e.tensor_tensor(abr[:], mag[:], cs[:], ALU.mult), reads=[b_mag, b_cs], writes=[b_abr])
        S.op('dve', lambda e: e.tensor_tensor(abi[:], mag[:], sn[:], ALU.mult), reads=[b_mag, b_sn], writes=[b_abi])
        S.op('dve', lambda e: e.tensor_tensor(den[:], are[:], are[:], ALU.mult), reads=[b_are], writes=[b_den])
        S.op('dve', lambda e: e.tensor_tensor(t1[:], aim[:], aim[:], ALU.mult), reads=[b_aim], writes=[b_t1])
        S.op('dve', lambda e: e.tensor_tensor(den[:], den[:], t1[:], ALU.add), reads=[b_den, b_t1], writes=[b_den])
        S.op('dve', lambda e: e.reciprocal(den[:], den[:]), reads=[b_den], writes=[b_den])
        S.op('dve', lambda e: e.tensor_scalar(abr[:], abr[:], -1.0, None, ALU.add), reads=[b_abr], writes=[b_abr])
        S.op('dve', lambda e: e.tensor_tensor(t1[:], abr[:], are[:], ALU.mult), reads=[b_abr, b_are, b_den], writes=[b_t1])
        S.op('dve', lambda e: e.tensor_tensor(t2[:], abi[:], aim[:], ALU.mult), reads=[b_abi, b_aim], writes=[b_t2])
        S.op('dve', lambda e: e.tensor_tensor(t1[:], t1[:], t2[:], ALU.add), reads=[b_t1, b_t2], writes=[b_t1])
        S.op('dve', lambda e: e.tensor_tensor(zr[:], t1[:], den[:], ALU.mult), reads=[b_t1, b_den], writes=[b_zr])
        S.op('dve', lambda e: e.tensor_tensor(t1[:], abi[:], are[:], ALU.mult), reads=[b_abi, b_are, b_zr], writes=[b_t1])
        S.op('dve', lambda e: e.tensor_tensor(t2[:], abr[:], aim[:], ALU.mult), reads=[b_abr, b_aim], writes=[b_t2])
        S.op('dve', lambda e: e.tensor_tensor(t1[:], t1[:], t2[:], ALU.subtract), reads=[b_t1, b_t2], writes=[b_t1])
        S.op('dve', lambda e: e.tensor_tensor(zi[:], t1[:], den[:], ALU.mult), reads=[b_t1, b_den], writes=[b_zi])
        bbr = _sb(nc, es, "bbr", [P, 64, 16], F32)
        bbi = _sb(nc, es, "bbi", [P, 64, 16], F32)
        tq = _sb(nc, es, "tq", [P, 64, 16], F32)
        b_bbr, b_bbi, b_tq = Buf("bbr"), Buf("bbi"), Buf("tq")
        zrb = zr[:].unsqueeze(2).to_broadcast([P, 64, 16])
        zib = zi[:].unsqueeze(2).to_broadcast([P, 64, 16])
        S.op('dve', lambda e: e.tensor_tensor(bbr[:], br[:], zrb, ALU.mult), reads=[b_br, b_zr], writes=[b_bbr])
        S.op('dve', lambda e: e.tensor_tensor(tq[:], bi[:], zib, ALU.mult), reads=[b_bi, b_zi], writes=[b_tq])
        S.op('dve', lambda e: e.tensor_tensor(bbr[:], bbr[:], tq[:], ALU.subtract), reads=[b_bbr, b_tq], writes=[b_bbr])
        S.op('dve', lambda e: e.tensor_tensor(bbi[:], bi[:], zrb, ALU.mult), reads=[b_bi, b_zr], writes=[b_bbi])
        S.op('dve', lambda e: e.tensor_tensor(tq[:], br[:], zib, ALU.mult), reads=[b_br, b_zi, b_bbr], writes=[b_tq])
        S.op('dve', lambda e: e.tensor_tensor(bbi[:], bbi[:], tq[:], ALU.add), reads=[b_bbi, b_tq], writes=[b_bbi])
        xb = [_sb(nc, es, f"xb{i}", [P, P], F32) for i in range(2)]
        b_xb = [Buf(f"xb{i}") for i in range(2)]
        bt = [_sb(nc, es, f"bt{i}", [P, 2, P], BF16) for i in range(2)]
        b_bt = [Buf(f"bt{i}") for i in range(2)]
        ct = [_sb(nc, es, f"ct{i}", [P, 3, P], BF16) for i in range(2)]
        b_ct = [Buf(f"ct{i}") for i in range(2)]
        ptp = [_ps(nc, es, f"ptp{i}") for i in range(2)]
        b_ptp = [Buf(f"ptp{i}") for i in range(2)]
        q = 0
        for k in range(64):
            r0 = (k % 4) * 32
            btk, bbt_ = bt[k % 2], b_bt[k % 2]
            ctk, bct_ = ct[k % 2], b_ct[k % 2]
            for a_, (src, bsrc) in enumerate(((bbr, b_bbr), (bbi, b_bbi))):
                x_, bx_ = xb[q % 2], b_xb[q % 2]
                pp, bpp = ptp[q % 2], b_ptp[q % 2]
                q += 1
                S.op('pool', lambda e, x_=x_: e.memset(x_[:], 0.0), writes=[bx_])
                S.op('pool', lambda e, x_=x_, src=src, k=k, r0=r0: e.tensor_copy(x_[0:64, r0:r0 + 16], src[0:64, k, :]),
                     reads=[bsrc, bx_], writes=[bx_])
                S.op('pool', lambda e, x_=x_, src=src, k=k, r0=r0: e.tensor_copy(x_[64:128, r0 + 16:r0 + 32], src[64:128, k, :]),
                     reads=[bsrc, bx_], writes=[bx_])
                S.op('pe', lambda e, pp=pp, x_=x_: e.transpose(pp[:, :P], x_[:], ident[:]), reads=[bx_, b_ident], writes=[bpp])
                S.op('act', lambda e, pp=pp, btk=btk, a_=a_: e.copy(btk[:, a_, :], pp[:, :P]), reads=[bpp], writes=[bbt_])
            S.dma(C.BBT[k].rearrange("a r s -> r a s"), btk[:], bbt_, reads=[bbt_])
            S.op('dve', lambda e, ctk=ctk: e.memset(ctk[:], 0.0), writes=[bct_])
            for a_, (src, bsrc, sc) in enumerate(((cr, b_cr, 1.0), (cr, b_cr, -1.0), (ci, b_ci, -1.0))):
                for gl in range(2):
                    S.op('dve', lambda e, ctk=ctk, a_=a_, src=src, sc=sc, gl=gl, k=k, r0=r0: e.tensor_scalar(
                        ctk[gl * 64:(gl + 1) * 64, a_, r0 + gl * 16:r0 + gl * 16 + 16], src[gl * 64:(gl + 1) * 64, k, :],
                        sc, None, ALU.mult), reads=[bsrc, bct_], writes=[bct_])
            S.dma(C.CTS[k].rearrange("a r s -> r a s"), ctk[:], bct_, reads=[bct_])
        def big(nm):
            return _sb(nc, es, nm, [P, 64, 64], F32), Buf(nm)
        ang, b_ang = big("ang")
        ea, b_ea = big("ea")
        sN, b_sN = big("sN")
        cN, b_cN = big("cN")
        tb, b_tb = big("tbig")
        tab = _sb(nc, es, "tab", [P, 64, 4, 64], F32)
        b_tab = Buf("tab")
        for k in range(64):
            S.op('dve', lambda e, k=k: e.tensor_scalar(ang[:, k, :], pos1[:], th[:, k:k + 1], None, ALU.mult),
                 reads=[b_pos1, b_th], writes=[b_ang])
            S.op('pool', lambda e, k=k: e.tensor_scalar(ea[:, k, :], pos1[:], ar[:, k:k + 1], None, ALU.mult),
                 reads=[b_pos1, b_ar], writes=[b_ea])
        fl = lambda t_: t_[:].rearrange("p k q -> p (k q)")
        itb = _sb(nc, es, "itb", [P, 4096], mybir.dt.int32)
        _sincos(C, fl(ang), b_ang, 4096, fl(tb), b_tb, itb[:], fl(sN), b_sN, fl(cN), b_cN)
        S.op('act', lambda e: e.activation(fl(tb), fl(ea), AF.Exp), reads=[b_ea, b_sN, b_cN], writes=[b_tb])
        S.op('dve', lambda e: e.tensor_tensor(tab[:, :, 0, :], tb[:], cN[:], ALU.mult), reads=[b_tb, b_cN], writes=[b_tab])
        S.op('dve', lambda e: e.tensor_tensor(tab[:, :, 1, :], tb[:], sN[:], ALU.mult), reads=[b_tb, b_sN], writes=[b_tab])
        S.op('act', lambda e: e.activation(fl(tb), fl(ea), AF.Exp, scale=-1.0), reads=[b_ea, b_tab], writes=[b_tb])
        S.op('dve', lambda e: e.tensor_tensor(tab[:, :, 2, :], tb[:], cN[:], ALU.mult), reads=[b_tb, b_cN], writes=[b_tab])
        S.op('dve', lambda e: e.scalar_tensor_tensor(tab[:, :, 3, :], tb[:], -1.0, sN[:], ALU.mult, ALU.mult),
             reads=[b_tb, b_sN], writes=[b_tab])
        S.dma(C.TAB[:, :, :, :], tab[:], b_tab, reads=[b_tab])
        ap_ = _sb(nc, es, "apow", [P, 64, 10], F32)
        b_ap = Buf("apow")
        S.op('dve', lambda e: e.tensor_copy(ap_[:, :, 0], tab[:, :, 0, 63]), reads=[b_tab], writes=[b_ap])
        S.op('dve', lambda e: e.tensor_copy(ap_[:, :, 1], tab[:, :, 1, 63]), reads=[b_tab, b_ap], writes=[b_ap])
        for i in range(1, 5):
            pr, pi_ = ap_[:, :, 2 * i - 2], ap_[:, :, 2 * i - 1]
            S.op('dve', lambda e, pr=pr: e.tensor_tensor(t1[:], pr, pr, ALU.mult), reads=[b_ap, b_zi], writes=[b_t1])
            S.op('dve', lambda e, pi_=pi_: e.tensor_tensor(t2[:], pi_, pi_, ALU.mult), reads=[b_ap], writes=[b_t2])
            S.op('dve', lambda e, i=i: e.tensor_tensor(ap_[:, :, 2 * i], t1[:], t2[:], ALU.subtract), reads=[b_t1, b_t2, b_ap], writes=[b_ap])
            S.op('dve', lambda e, pr=pr, pi_=pi_, i=i: e.scalar_tensor_tensor(ap_[:, :, 2 * i + 1], pr, 2.0, pi_, ALU.mult, ALU.mult),
                 reads=[b_ap], writes=[b_ap])
        S.dma(C.APOW[:, :, :], ap_[:], b_ap, reads=[b_ap])
        S.end()


def phase_ssm_scan(C, j, ni):
    nc, S = C.nc, C.S
    with ExitStack() as es:
        xn = _sb(nc, es, "xns", [P, DC, T], BF16)
        b_xn = [Buf(f"xns{t}") for t in range(5)]
        x32, b_x32 = load_xn_all(C, es, ni, xn, b_xn)
        x32f = x32[:].rearrange("p c t -> p (c t)")
        ident = _sb(nc, es, "idents", [P, P], F32)
        b_ident = Buf("idents")
        S.dma(ident[:], C.ident[:, :], b_ident, writes=[b_ident])
        rmask = _sb(nc, es, "rmask", [P, T], F32)
        b_rmask = Buf("rmask")
        S.dma(rmask[:], C.rmask[:, :], b_rmask, writes=[b_rmask])
        apw = _sb(nc, es, "apw", [P, 64, 10], F32)
        b_apw = Buf("apw")
        S.dma(apw[:], C.APOW[:, :, :], b_apw, writes=[b_apw])
        st0 = _sb(nc, es, "st0", [P, 2, 2, 64], F32)
        b_st0 = Buf("st0")
        S.dma(st0[:], C.ssms[j].rearrange("s a p k -> p s a k"), b_st0, writes=[b_st0])
        dv = _sb(nc, es, "dvs", [P, DC], F32)
        b_dv = Buf("dvs")
        S.dma(dv[:], C.ssmd[j], b_dv, writes=[b_dv])
        fin = _sb(nc, es, "fin", [P, 6, 64], F32)
        b_fin = Buf("fin")
        btk = [_sb(nc, es, f"sbt{i}", [P, 2, P], BF16) for i in range(2)]
        ctk = [_sb(nc, es, f"sct{i}", [P, 3, P], BF16) for i in range(2)]
        tbk = [_sb(nc, es, f"stb{i}", [P, 4, 64], F32) for i in range(2)]
        b_btk = [Buf(f"sbt{i}") for i in range(2)]
        b_ctk = [Buf(f"sct{i}") for i in range(2)]
        b_tbk = [Buf(f"stb{i}") for i in range(2)]
        bur = x32f[:, 0:T]
        bui = x32f[:, T:2 * T]
        b_bur = [Buf(f"bur{t}") for t in range(5)]
        b_bui = [Buf(f"bui{t}") for t in range(5)]
        wr = x32f[:, 2 * T:3 * T]
        wi = _sb(nc, es, "wi", [P, T], F32)
        b_wr, b_wi = Buf("wr"), Buf("wi")
        S.op('dve', lambda e: e.memset(wi[:, 0:1], 0.0), writes=[b_x32, b_wr, b_wi] + b_bur + b_bui)
        m1 = _sb(nc, es, "m1", [P, T], F32)
        m2 = _sb(nc, es, "m2", [P, T], F32)
        b_m1, b_m2 = Buf("m1"), Buf("m2")
        pr_ = [_sb(nc, es, f"prd{i}", [P, T], BF16) for i in range(4)]
        b_pr = [Buf(f"prd{i}") for i in range(4)]
        sm = _sb(nc, es, "sm", [P, 12, NCK], F32)
        b_sm = Buf("sm")
        psb = [_ps(nc, es, f"psb{i}") for i in range(2)]
        b_psb = [Buf(f"psb{i}") for i in range(2)]
        psy = [_ps(nc, es, f"psy{i}") for i in range(5)]
        b_psy = [Buf(f"psy{i}") for i in range(5)]
        xg = _sb(nc, es, "xg", [P, 512], F32)
        hx = _sb(nc, es, "hxg", [P, 512], F32)
        x2 = _sb(nc, es, "x2g", [P, 512], F32)
        b_xg, b_hx, b_x2 = Buf("xg"), Buf("hxg"), Buf("x2g")
        gb = [_sb(nc, es, f"gb{i}", [P, T], BF16) for i in range(2)]
        b_gb = [Buf(f"gb{i}") for i in range(2)]

        def v3(t_, lo, w):
            return t_[:, lo:lo + w].rearrange("p (n q) -> p n q", q=64)

        def tile_body(k):
            dc = k // 4
            a = k % 2
            S.dma(btk[a][:], C.BBT[k].rearrange("a r s -> r a s"), b_btk[a], writes=[b_btk[a]])
            S.dma(ctk[a][:], C.CTS[k].rearrange("a r s -> r a s"), b_ctk[a], writes=[b_ctk[a]])
            S.dma(tbk[a][:], C.TAB[:, k, :, :], b_tbk[a], writes=[b_tbk[a]])
            tpr, tpi, tnr, tni = (tbk[a][:, i, :] for i in range(4))
            for ti, (ts, w) in enumerate(TILES_ALL):
                for a_, (dst, bdst) in enumerate(((bur, b_bur), (bui, b_bui))):
                    pb, bpb = psb[a_], b_psb[a_]
                    S.op('pe', lambda e, pb=pb, a_=a_, ts=ts, w=w, a=a, dc=dc: e.matmul(
                        pb[:, :w], btk[a][:, a_, :], xn[:, dc, ts:ts + w], start=True, stop=True),
                        reads=[b_btk[a], b_xn[ti]], writes=[bpb])
                    S.op('act', lambda e, pb=pb, dst=dst, ts=ts, w=w: e.copy(dst[:, ts:ts + w], pb[:, :w]),
                         reads=[bpb], writes=[bdst[ti]])
            bc = lambda t_: t_.unsqueeze(1).to_broadcast([P, NCK, 64])
            A3 = lambda t_: t_[:].rearrange("p (n q) -> p n q", q=64)
            S.op('pool', lambda e: e.tensor_tensor(A3(m1), A3(bur), bc(tnr), ALU.mult), reads=b_bur + [b_tbk[a]], writes=[b_m1])
            S.op('pool', lambda e: e.tensor_tensor(A3(m2), A3(bui), bc(tni), ALU.mult), reads=b_bui + [b_tbk[a]], writes=[b_m2])
            S.op('dve', lambda e: e.tensor_tensor(wr[:], m1[:], m2[:], ALU.subtract), reads=[b_m1, b_m2], writes=[b_wr])
            S.op('pool', lambda e: e.tensor_tensor(A3(m1), A3(bur), bc(tni), ALU.mult), reads=b_bur + [b_tbk[a], b_wr], writes=[b_m1])
            S.op('pool', lambda e: e.tensor_tensor(A3(m2), A3(bui), bc(tnr), ALU.mult), reads=b_bui + [b_tbk[a], b_wr], writes=[b_m2])
            S.op('dve', lambda e: e.tensor_tensor(wi[:], m1[:], m2[:], ALU.add), reads=[b_m1, b_m2], writes=[b_wi])
            S.op('dve', lambda e: e.tensor_tensor_scan(wr[:], rmask[:], wr[:], 0.0, ALU.mult, ALU.add), reads=[b_rmask, b_wr], writes=[b_wr])
            S.op('dve', lambda e: e.tensor_tensor_scan(wi[:], rmask[:], wi[:], 0.0, ALU.mult, ALU.add), reads=[b_rmask, b_wi], writes=[b_wi])
            er = wr[:, 63:T:64]
            ei = wi[:, 63:T:64]
            Ar, Ai = apw[:, k, 0:1], apw[:, k, 1:2]
            X = lambda i, lo=0, hi=NCK: sm[:, i, lo:hi]
            S.op('dve', lambda e: e.tensor_scalar(X(2), ei, Ai, None, ALU.mult), reads=[b_wi, b_apw, b_fin], writes=[b_sm])
            S.op('dve', lambda e: e.scalar_tensor_tensor(X(0), er, Ar, X(2), ALU.mult, ALU.subtract), reads=[b_wr, b_apw, b_sm], writes=[b_sm])
            S.op('dve', lambda e: e.tensor_scalar(X(2), ei, Ar, None, ALU.mult), reads=[b_wi, b_apw, b_sm], writes=[b_sm])
            S.op('dve', lambda e: e.scalar_tensor_tensor(X(1), er, Ai, X(2), ALU.mult, ALU.add), reads=[b_wr, b_apw, b_sm], writes=[b_sm])
            cur = (0, 1)
            oth = (2, 3)
            sft = 1
            for i in range(5):
                Pr, Pi = apw[:, k, 2 * i:2 * i + 1], apw[:, k, 2 * i + 1:2 * i + 2]
                cr_, ci_ = cur
                or_, oi_ = oth
                n = 32
                S.op('dve', lambda e, cr_=cr_, or_=or_, sft=sft: e.tensor_copy(X(or_, 0, sft), X(cr_, 0, sft)), reads=[b_sm], writes=[b_sm])
                S.op('dve', lambda e, ci_=ci_, oi_=oi_, sft=sft: e.tensor_copy(X(oi_, 0, sft), X(ci_, 0, sft)), reads=[b_sm], writes=[b_sm])
                S.op('dve', lambda e, cr_=cr_, or_=or_, sft=sft, Pr=Pr: e.scalar_tensor_tensor(
                    X(or_, sft, n), X(cr_, 0, n - sft), Pr, X(cr_, sft, n), ALU.mult, ALU.add), reads=[b_sm, b_apw], writes=[b_sm])
                S.op('dve', lambda e, ci_=ci_, sft=sft, Pi=Pi: e.tensor_scalar(X(4, 0, n - sft), X(ci_, 0, n - sft), Pi, None, ALU.mult),
                     reads=[b_sm, b_apw], writes=[b_sm])
                S.op('dve', lambda e, or_=or_, sft=sft: e.tensor_tensor(X(or_, sft, n), X(or_, sft, n), X(4, 0, n - sft), ALU.subtract),
                     reads=[b_sm], writes=[b_sm])
                S.op('dve', lambda e, ci_=ci_, oi_=oi_, sft=sft, Pr=Pr: e.scalar_tensor_tensor(
                    X(oi_, sft, n), X(ci_, 0, n - sft), Pr, X(ci_, sft, n), ALU.mult, ALU.add), reads=[b_sm, b_apw], writes=[b_sm])
                S.op('dve', lambda e, cr_=cr_, oi_=oi_, sft=sft, Pi=Pi: e.scalar_tensor_tensor(
                    X(oi_, sft, n), X(cr_, 0, n - sft), Pi, X(oi_, sft, n), ALU.mult, ALU.add), reads=[b_sm, b_apw], writes=[b_sm])
                cur, oth = oth, cur
                sft *= 2
            sr, si = cur
            S.op('dve', lambda e: e.memset(sm[:, 6:8, 0:1], 0.0), reads=[b_sm], writes=[b_sm])
            S.op('dve', lambda e, sr=sr: e.tensor_copy(X(6, 1, 32), X(sr, 0, 31)), reads=[b_sm], writes=[b_sm])
            S.op('dve', lambda e, si=si: e.tensor_copy(X(7, 1, 32), X(si, 0, 31)), reads=[b_sm], writes=[b_sm])
            S.op('dve', lambda e, k=k: e.tensor_copy(sm[:, 6, 32:34], st0[:, :, 0, k]), reads=[b_sm, b_st0], writes=[b_sm])
            S.op('dve', lambda e, k=k: e.tensor_copy(sm[:, 7, 32:34], st0[:, :, 1, k]), reads=[b_sm, b_st0], writes=[b_sm])
            S.op('dve', lambda e, sr=sr, k=k: e.tensor_copy(fin[:, 0, k:k + 1], X(sr, 31, 32)), reads=[b_sm], writes=[b_fin])
            S.op('dve', lambda e, si=si, k=k: e.tensor_copy(fin[:, 1, k:k + 1], X(si, 31, 32)), reads=[b_sm, b_fin], writes=[b_fin])
            S.op('dve', lambda e: e.tensor_tensor(X(8, 32, 34), X(6, 32, 34), er[:, 32:34], ALU.add), reads=[b_sm, b_wr], writes=[b_sm])
            S.op('dve', lambda e: e.tensor_tensor(X(9, 32, 34), X(7, 32, 34), ei[:, 32:34], ALU.add), reads=[b_sm, b_wi], writes=[b_sm])
            S.op('dve', lambda e: e.tensor_scalar(X(10, 32, 34), X(9, 32, 34), Ai, None, ALU.mult), reads=[b_sm, b_apw], writes=[b_sm])
            S.op('dve', lambda e: e.scalar_tensor_tensor(X(11, 32, 34), X(8, 32, 34), Ar, X(10, 32, 34), ALU.mult, ALU.subtract),
                 reads=[b_sm, b_apw], writes=[b_sm])
            S.op('dve', lambda e, k=k: e.tensor_copy(fin[:, 2:6:2, k], X(11, 32, 34)), reads=[b_sm, b_fin], writes=[b_fin])
            S.op('dve', lambda e: e.tensor_scalar(X(10, 32, 34), X(9, 32, 34), Ar, None, ALU.mult), reads=[b_sm, b_apw, b_fin], writes=[b_sm])
            S.op('dve', lambda e: e.scalar_tensor_tensor(X(11, 32, 34), X(8, 32, 34), Ai, X(10, 32, 34), ALU.mult, ALU.add),
                 reads=[b_sm, b_apw], writes=[b_sm])
            S.op('dve', lambda e, k=k: e.tensor_copy(fin[:, 3:6:2, k], X(11, 32, 34)), reads=[b_sm, b_fin], writes=[b_fin])
            cb = lambda i: sm[:, i, :].unsqueeze(2).to_broadcast([P, NCK, 64])
            S.op('dve', lambda e: e.tensor_tensor(A3(wr), A3(wr), cb(6), ALU.add), reads=[b_wr, b_sm], writes=[b_wr])
            S.op('dve', lambda e: e.tensor_tensor(A3(wi), A3(wi), cb(7), ALU.add), reads=[b_wi, b_sm], writes=[b_wi])
            S.op('pool', lambda e: e.tensor_tensor(A3(pr_[0]), A3(wr), bc(tpr), ALU.mult), reads=[b_wr, b_tbk[a]], writes=[b_pr[0]])
            S.op('dve', lambda e: e.tensor_tensor(A3(pr_[1]), A3(wi), bc(tpi), ALU.mult), reads=[b_wi, b_tbk[a]], writes=[b_pr[1]])
            S.op('pool', lambda e: e.tensor_tensor(A3(pr_[2]), A3(wi), bc(tpr), ALU.mult), reads=[b_wi, b_tbk[a]], writes=[b_pr[2]])
            S.op('dve', lambda e: e.tensor_tensor(A3(pr_[3]), A3(wr), bc(tpi), ALU.mult), reads=[b_wr, b_tbk[a]], writes=[b_pr[3]])
            cmap = (0, 1, 2, 2)
            for ti, (ts, w) in enumerate(TILES_ALL):
                for q in range(4):
                    S.op('pe', lambda e, ti=ti, ts=ts, w=w, q=q, a=a, k=k: e.matmul(
                        psy[ti][:, :w], ctk[a][:, cmap[q], :], pr_[q][:, ts:ts + w],
                        start=(k % 4 == 0 and q == 0), stop=(k % 4 == 3 and q == 3)),
                        reads=[b_ctk[a], b_pr[q]], writes=[b_psy[ti]])
            if k % 4 != 3:
                return
            g_, bg_ = gb[dc % 2], b_gb[dc % 2]
            for ti, (ts, w) in enumerate(TILES_ALL):
                S.op('dve', lambda e, ti=ti, ts=ts, w=w, dc=dc: e.scalar_tensor_tensor(
                    xg[:, :w], xn[:, dc, ts:ts + w], dv[:, dc:dc + 1], psy[ti][:, :w], ALU.mult, ALU.add),
                    reads=[b_xn[ti], b_dv, b_psy[ti]], writes=[b_xg])
                S.op('act', lambda e, w=w: e.activation(x2[:, :w], xg[:, :w], AF.Square), reads=[b_xg], writes=[b_x2])
                S.op('act', lambda e, w=w: e.mul(hx[:, :w], xg[:, :w], 0.5), reads=[b_xg], writes=[b_hx])
                S.op('dve', lambda e, w=w: e.tensor_scalar(x2[:, :w], x2[:, :w], 0.044715, 1.0, ALU.mult, ALU.add), reads=[b_x2], writes=[b_x2])
                S.op('dve', lambda e, w=w: e.tensor_tensor(x2[:, :w], x2[:, :w], xg[:, :w], ALU.mult), reads=[b_x2, b_xg], writes=[b_x2])
                S.op('act', lambda e, w=w: e.activation(x2[:, :w], x2[:, :w], AF.Tanh, scale=0.7978845608028654), reads=[b_x2], writes=[b_x2])
                S.op('dve', lambda e, g_=g_, ts=ts, w=w: e.scalar_tensor_tensor(
                    g_[:, ts:ts + w], x2[:, :w], 1.0, hx[:, :w], ALU.add, ALU.mult), reads=[b_x2, b_hx], writes=[bg_])
            S.dma(C.GT[dc], g_[:], bg_, reads=[bg_])

        for k in range(64):
            tile_body(k)
        fo = _sb(nc, es, "fo", [P, 3, P], F32)
        b_fo = Buf("fo")
        for i in range(3):
            S.op('pe', lambda e, i=i: e.transpose(psb[i % 2][:, :P], fin[:, 2 * i:2 * i + 2, :].rearrange("p a k -> p (a k)"), ident[:]),
                 reads=[b_fin, b_ident], writes=[b_psb[i % 2]])
            S.op('dve', lambda e, i=i: e.tensor_copy(fo[:, i, :], psb[i % 2][:, :P]), reads=[b_psb[i % 2]], writes=[b_fo])
        S.dma(C.ssm_out[j].rearrange("i a k p -> (a k) i p"), fo[:], b_fo, reads=[b_fo])
        S.end()


def phase_ssm_glu(C, j):
    nc, S = C.nc, C.S
    with ExitStack() as es:
        g = _sb(nc, es, "gall", [P, DC, T], BF16)
        b_g = Buf("gall")
        S.dma(g[:], C.GT[:, :, :].rearrange("c p t -> p c t"), b_g, writes=[b_g])
        ws = WStream(C, es, [C.wglu[j, i // 2, i % 2] for i in range(32)], nstg=3, nwb=4, tag="wg")
        xr = [_sb(nc, es, f"xrg{i}", [P, 512], F32) for i in range(2)]
        b_xr = [Buf(f"xrg{i}") for i in range(2)]
        sgm = [_sb(nc, es, f"sgm{i}", [P, 512], F32) for i in range(2)]
        b_sgm = [Buf(f"sgm{i}") for i in range(2)]
        psv = [_ps(nc, es, f"psv{i}") for i in range(2)]
        psg = [_ps(nc, es, f"psgg{i}") for i in range(2)]
        b_psv = [Buf(f"psv{i}") for i in range(2)]
        b_psg = [Buf(f"psgg{i}") for i in range(2)]
        kk = 0
        for jc in range(DC):
            ws.fetch_upto(2 * jc + 4)
            wv, bwv = ws.wb[(2 * jc) % 4], ws.b_wb[(2 * jc) % 4]
            wg, bwg = ws.wb[(2 * jc + 1) % 4], ws.b_wb[(2 * jc + 1) % 4]
            for ti, (ts, w) in enumerate(TILES_ALL):
                pv, bpv, pg, bpg = psv[kk % 2], b_psv[kk % 2], psg[kk % 2], b_psg[kk % 2]
                x, bx = xr[kk % 2], b_xr[kk % 2]
                sg_, bsg_ = sgm[kk % 2], b_sgm[kk % 2]
                kk += 1
                S.dma(x[:, :w], C.XT[jc, :, ts:ts + w], bx, writes=[bx])
                for dc in range(DC):
                    S.op('pe', lambda e, pv=pv, wv=wv, dc=dc, ts=ts, w=w: e.matmul(
                        pv[:, :w], wv[:, dc * P:(dc + 1) * P], g[:, dc, ts:ts + w], start=(dc == 0), stop=(dc == DC - 1)),
                        reads=[bwv, b_g], writes=[bpv])
                for dc in range(DC):
                    S.op('pe', lambda e, pg=pg, wg=wg, dc=dc, ts=ts, w=w: e.matmul(
                        pg[:, :w], wg[:, dc * P:(dc + 1) * P], g[:, dc, ts:ts + w], start=(dc == 0), stop=(dc == DC - 1)),
                        reads=[bwg, b_g], writes=[bpg])
                S.op('act', lambda e, sg_=sg_, pg=pg, w=w: e.activation(sg_[:, :w], pg[:, :w], AF.Sigmoid), reads=[bpg], writes=[bsg_])
                S.op('dve', lambda e, sg_=sg_, pv=pv, w=w: e.tensor_tensor(sg_[:, :w], sg_[:, :w], pv[:, :w], ALU.mult),
                     reads=[bsg_, bpv], writes=[bsg_])
                S.op('dve', lambda e, sg_=sg_, x=x, w=w: e.tensor_tensor(x[:, :w], x[:, :w], sg_[:, :w], ALU.add),
                     reads=[bsg_, bx], writes=[bx])
                S.dma(C.XT[jc, :, ts:ts + w], x[:, :w], bx, reads=[bx])
        S.end()


def phase_band_attn(C):
    nc, S = C.nc, C.S
    with ExitStack() as es:
        OT = _sb(nc, es, "OT", [P, 16, T], BF16)
        b_OT = Buf("OT")
        ones_f = _sb(nc, es, "ones_fb", [P, P], F32)
        ones_bf = _sb(nc, es, "ones_bfb", [P, P], BF16)
        ident = _sb(nc, es, "identb", [P, P], F32)
        b_onesf, b_ones, b_ident = Buf("onesf"), Buf("ones"), Buf("ident")
        S.dma(ones_f[:], C.ones[:, :], b_onesf, writes=[b_onesf])
        S.dma(ident[:], C.ident[:, :], b_ident, writes=[b_ident])
        S.op('dve', lambda e: e.tensor_copy(ones_bf[:], ones_f[:]), reads=[b_onesf], writes=[b_ones])
        R = attn_resources(C, es, ones_bf, b_ones)
        qT = [_sb(nc, es, f"qT{i}", [P, T], BF16) for i in range(2)]
        kT = [_sb(nc, es, f"kT{i}", [P, T], BF16) for i in range(2)]
        vt = [_sb(nc, es, f"vt{i}", [P, NCH, P], BF16) for i in range(2)]
        bp = [_sb(nc, es, f"bp{i}", [P, 5, P], F32) for i in range(2)]
        bs = [_sb(nc, es, f"bs{i}", [P, 2, 5, 64], F32) for i in range(2)]
        kc = [_sb(nc, es, f"kc{i}", [P, 2, 4, P], F32) for i in range(2)]
        vc = [_sb(nc, es, f"vc{i}", [P, 2, 4, P], F32) for i in range(2)]
        kcT = [_sb(nc, es, f"kcT{i}", [P, 2, 512], BF16) for i in range(2)]
        vcb = [_sb(nc, es, f"vcb{i}", [P, 2, 4, P], BF16) for i in range(2)]
        nm = ['qT', 'kT', 'vt', 'bp', 'bs', 'kc', 'vc', 'kcT', 'vcb']
        B = {n: [Buf(f"{n}{i}") for i in range(2)] for n in nm}
        ptr = [_ps(nc, es, f"ptrb{i}") for i in range(2)]
        b_ptr = [Buf(f"ptrb{i}") for i in range(2)]
        ktr = 0
        for h in range(16):
            a = h % 2
            S.dma(qT[a][:], C.QT[h], B['qT'][a], writes=[B['qT'][a]])
            S.dma(kT[a][:], C.KT[h], B['kT'][a], writes=[B['kT'][a]])
            S.dma(vt[a][:], C.VTM[h].rearrange("c p d -> p c d"), B['vt'][a], writes=[B['vt'][a]])
            S.dma(bp[a][:], C.bbias_p[h].rearrange("t k q -> k t q"), B['bp'][a], writes=[B['bp'][a]])
            S.dma(bs[a][:], C.bbias_s[h].rearrange("s t k q -> k s t q"), B['bs'][a], writes=[B['bs'][a]])
            S.dma(kc[a][:], C.cbk[:, :, h, :].rearrange("s (c p) d -> p s c d", p=P), B['kc'][a], writes=[B['kc'][a]])
            S.dma(vc[a][:], C.cbv[:, :, h, :].rearrange("s (c p) d -> p s c d", p=P), B['vc'][a], writes=[B['vc'][a]])
            S.op('pool', lambda e, a=a: e.tensor_copy(vcb[a][:], vc[a][:]), reads=[B['vc'][a]], writes=[B['vcb'][a]])
            for s_ in range(2):
                for c in range(4):
                    pt, bpt = ptr[ktr % 2], b_ptr[ktr % 2]
                    ktr += 1
                    S.op('pe', lambda e, pt=pt, a=a, s_=s_, c=c: e.transpose(pt[:, :P], kc[a][:, s_, c, :], ident[:]),
                         reads=[B['kc'][a], b_ident], writes=[bpt])
                    S.op('act', lambda e, pt=pt, a=a, s_=s_, c=c: e.copy(kcT[a][:, s_, c * P:(c + 1) * P], pt[:, :P]),
                         reads=[bpt], writes=[B['kcT'][a]])
            for m in range(16):
                tl = [t for t in range(5) if m - 4 + t >= 0]
                for t in tl:
                    kt_ = m - 4 + t

                    def bias_fn(ps, bps, sb_, bsb, pT, bpT, a=a, t=t):
                        S.op('dve', lambda e: e.tensor_tensor(sb_[:], ps[:, :P], bp[a][:, t, :], ALU.add),
                             reads=[bps, B['bp'][a]], writes=[bsb])
                        S.op('act', lambda e: e.activation(pT[:], sb_[:], AF.Exp), reads=[bsb], writes=[bpT])

                    def out_fn(po, bpo, rc, brc, h=h, m=m):
                        S.op('dve', lambda e: e.tensor_tensor(OT[:, h, m * P:(m + 1) * P], po[:, :P], rc[:, :P], ALU.mult),
                             reads=[bpo, brc], writes=[b_OT])

                    attn_tile(C, kT[a][:, kt_ * P:(kt_ + 1) * P], B['kT'][a], qT[a][:, m * P:(m + 1) * P], B['qT'][a], P,
                              vt[a][:, kt_, :], B['vt'][a], bias_fn, R, t == tl[0], t == 4, out_fn)
            for s_ in range(2):
                q0 = 2048 + 64 * s_
                for t in range(5):
                    def bias_fn(ps, bps, sb_, bsb, pT, bpT, a=a, t=t, s_=s_):
                        S.op('dve', lambda e: e.tensor_tensor(sb_[:, :64], ps[:, :64], bs[a][:, s_, t, :], ALU.add),
                             reads=[bps, B['bs'][a]], writes=[bsb])
                        S.op('act', lambda e: e.activation(pT[:, :64], sb_[:, :64], AF.Exp), reads=[bsb], writes=[bpT])

                    def out_fn(po, bpo, rc, brc, h=h, q0=q0):
                        S.op('dve', lambda e: e.tensor_tensor(OT[:, h, q0:q0 + 64], po[:, :64], rc[:, :64], ALU.mult),
                             reads=[bpo, brc], writes=[b_OT])

                    if t < 4:
                        kk, bkk = kcT[a][:, s_, t * P:(t + 1) * P], B['kcT'][a]
                        vv, bvv = vcb[a][:, s_, t, :], B['vcb'][a]
                    else:
                        kk, bkk = kT[a][:, 2048:2176], B['kT'][a]
                        vv, bvv = vt[a][:, 16, :], B['vt'][a]
                    attn_tile(C, kk, bkk, qT[a][:, q0:q0 + 64], B['qT'][a], 64, vv, bvv, bias_fn, R, t == 0, t == 4, out_fn)
        out_proj(C, es, OT, b_OT, C.wo_band)
        S.end()


def phase_final(C, ni):
    nc, S = C.nc, C.S
    with ExitStack() as es:
        ident = _sb(nc, es, "identf", [P, P], F32)
        b_ident = Buf("identf")
        gv = _sb(nc, es, "gvf", [P, DC], F32)
        b_gv = Buf("gvf")
        ones_f = _sb(nc, es, "ones_ff", [P, P], F32)
        ones_bf = _sb(nc, es, "ones_bff", [P, P], BF16)
        b_onesf, b_ones = Buf("onesf"), Buf("ones")
        xt = [_sb(nc, es, f"fxt{i}", [P, DC, P], F32) for i in range(2)]
        b_xt = [Buf(f"fxt{i}") for i in range(2)]
        yn = [_sb(nc, es, f"fyn{i}", [P, DC, P], F32) for i in range(2)]
        b_yn = [Buf(f"fyn{i}") for i in range(2)]
        yo = [_sb(nc, es, f"fyo{i}", [P, D], F32) for i in range(2)]
        b_yo = [Buf(f"fyo{i}") for i in range(2)]
        sq = [_sb(nc, es, f"fsq{i}", [P, 512], BF16) for i in range(2)]
        b_sq = [Buf("fsq0"), Buf("fsq1")]
        tmp = _sb(nc, es, "ftmp", [P, 512], F32)
        b_tmp = Buf("ftmp")
        rstd = [_sb(nc, es, f"frstd{i}", [P, P], F32) for i in range(2)]
        b_rstd = [Buf("frstd0"), Buf("frstd1")]
        pss = _ps(nc, es, "fpss")
        b_pss = Buf("fpss")
        pst = [_ps(nc, es, f"fps{i}") for i in range(4)]
        b_pst = [Buf(f"fps{i}") for i in range(4)]
        S.dma(ident[:], C.ident[:, :], b_ident, writes=[b_ident])
        S.dma(gv[:], C.nrm[ni], b_gv, writes=[b_gv])
        S.dma(ones_f[:], C.ones[:, :], b_onesf, writes=[b_onesf])
        S.op('dve', lambda e: e.tensor_copy(ones_bf[:], ones_f[:]), reads=[b_onesf], writes=[b_ones])
        k = 0
        for ch in range(NCH):
            x, bx = xt[ch % 2], b_xt[ch % 2]
            y, by = yn[ch % 2], b_yn[ch % 2]
            o, bo = yo[ch % 2], b_yo[ch % 2]
            r, br = rstd[ch % 2], b_rstd[ch % 2]
            S.dma(x[:], C.XT[:, :, ch * P:(ch + 1) * P].rearrange("c p t -> p c t"), bx, writes=[bx])
            rms_tile(C, lambda dc, x=x: x[:, dc, :], bx, P, sq, b_sq, pss, b_pss, ones_bf, b_ones,
                     tmp, b_tmp, r[:], br)
            for dc in range(DC):
                S.op('dve', lambda e, y=y, x=x, r=r, dc=dc: e.scalar_tensor_tensor(
                    y[:, dc, :], x[:, dc, :], gv[:, dc:dc + 1], r[:], ALU.mult, ALU.mult),
                    reads=[bx, b_gv, br], writes=[by])
            for g in range(4):
                ps, bps = pst[k % 4], b_pst[k % 4]
                k += 1
                for j in range(4):
                    dc = g * 4 + j
                    S.op('pe', lambda e, ps=ps, j=j, y=y, dc=dc: e.transpose(
                        ps[:, j * P:(j + 1) * P], y[:, dc, :], ident[:]), reads=[by, b_ident], writes=[bps])
                if g % 2 == 0:
                    S.op('dve', lambda e, ps=ps, o=o, g=g: e.tensor_copy(o[:, g * 512:(g + 1) * 512], ps[:]),
                         reads=[bps], writes=[bo])
                else:
                    S.op('act', lambda e, ps=ps, o=o, g=g: e.copy(o[:, g * 512:(g + 1) * 512], ps[:]),
                         reads=[bps], writes=[bo])
            dst = C.yp[ch * P:(ch + 1) * P, :] if ch < 16 else C.ys[:, :]
            S.dma(dst, o[:], bo, reads=[bo])
        S.end()


def build_nc(cfg):
    nc = bass.Bass("TRN2", target_bir_lowering=False)
    C = Ctx()
    C.nc = nc
    C.xp = nc.dram_tensor("xp", [2048, D], F32, kind="ExternalInput").ap()
    C.xs = nc.dram_tensor("xs", [P, D], F32, kind="ExternalInput").ap()
    C.win = nc.dram_tensor("win", [2 * DEPTH, FC, 2, P, D], F32, kind="ExternalInput").ap()
    C.wout = nc.dram_tensor("wout", [2 * DEPTH, FC, P, D], F32, kind="ExternalInput").ap()
    C.nrm = nc.dram_tensor("nrm", [3 * DEPTH + 1, P, DC], F32, kind="ExternalInput").ap()
    C.ident = nc.dram_tensor("ident", [P, P], F32, kind="ExternalInput").ap()
    C.ones = nc.dram_tensor("ones", [P, P], F32, kind="ExternalInput").ap()
    C.yp = nc.dram_tensor("yp", [2048, D], F32, kind="ExternalOutput").ap()
    C.ys = nc.dram_tensor("ys", [P, D], F32, kind="ExternalOutput").ap()
    C.XT = nc.dram_tensor("XT", [DC, P, T], F32, kind="Internal").ap()
    C.QT = nc.dram_tensor("QT", [16, P, T], BF16, kind="Internal").ap()
    C.KT = nc.dram_tensor("KT", [16, P, T], BF16, kind="Internal").ap()
    C.VTM = nc.dram_tensor("VTM", [16, NCH, P, P], BF16, kind="Internal").ap()
    C.wqkv_band = nc.dram_tensor("wqkv_band", [48, P, D], F32, kind="ExternalInput").ap()
    C.wo_band = nc.dram_tensor("wo_band", [DC, P, D], F32, kind="ExternalInput").ap()
    C.bbias_p = nc.dram_tensor("bbias_p", [16, 5, P, P], F32, kind="ExternalInput").ap()
    C.bbias_s = nc.dram_tensor("bbias_s", [16, 2, 5, P, 64], F32, kind="ExternalInput").ap()
    C.cbk = nc.dram_tensor("cbk", [2, 512, 16, P], F32, kind="ExternalInput").ap()
    C.cbv = nc.dram_tensor("cbv", [2, 512, 16, P], F32, kind="ExternalInput").ap()
    C.wqkv_fox = nc.dram_tensor("wqkv_fox", [48, P, D], F32, kind="ExternalInput").ap()
    C.wo_fox = nc.dram_tensor("wo_fox", [DC, P, D], F32, kind="ExternalInput").ap()
    C.wf_fox = nc.dram_tensor("wf_fox", [P, DC, 16], F32, kind="ExternalInput").ap()
    C.bf_fox = nc.dram_tensor("bf_fox", [P, NCH, 16], F32, kind="ExternalInput").ap()
    C.utri = nc.dram_tensor("utri", [2, P, P], F32, kind="ExternalInput").ap()
    C.fmask = nc.dram_tensor("fmask", [P, 3, P], F32, kind="ExternalInput").ap()
    C.cfk = nc.dram_tensor("cfk", [2, 4096, 16, P], F32, kind="ExternalInput").ap()
    C.cfv = nc.dram_tensor("cfv", [2, 4096, 16, P], F32, kind="ExternalInput").ap()
    C.cfl = nc.dram_tensor("cfl", [2, 4096, 16], F32, kind="ExternalInput").ap()
    C.LFS = nc.dram_tensor("LFS", [P, NCH, 16], F32, kind="Internal").ap()
    C.pfk = nc.dram_tensor("pfk", [2048, 16, P], F32, kind="ExternalOutput").ap()
    C.pfv = nc.dram_tensor("pfv", [2048, 16, P], F32, kind="ExternalOutput").ap()
    C.pfl = nc.dram_tensor("pfl", [2048, 16], F32, kind="ExternalOutput").ap()
    C.sfk = nc.dram_tensor("sfk", [P, 16, P], F32, kind="ExternalOutput").ap()
    C.sfv = nc.dram_tensor("sfv", [P, 16, P], F32, kind="ExternalOutput").ap()
    C.sfl = nc.dram_tensor("sfl", [P, 16], F32, kind="ExternalOutput").ap()
    C.ssmp = nc.dram_tensor("ssmp", [2, 3, P, 64], F32, kind="ExternalInput").ap()
    C.ssmb = nc.dram_tensor("ssmb", [2, 2, P, 64, 16], F32, kind="ExternalInput").ap()
    C.ssmc = nc.dram_tensor("ssmc", [2, 2, P, 64, 16], F32, kind="ExternalInput").ap()
    C.ssmd = nc.dram_tensor("ssmd", [2, P, DC], F32, kind="ExternalInput").ap()
    C.ssms = nc.dram_tensor("ssms", [2, 2, 2, P, 64], F32, kind="ExternalInput").ap()
    C.wglu = nc.dram_tensor("wglu", [2, DC, 2, P, D], F32, kind="ExternalInput").ap()
    C.pos1 = nc.dram_tensor("pos1", [P, 64], F32, kind="ExternalInput").ap()
    C.rmask = nc.dram_tensor("rmask", [P, T], F32, kind="ExternalInput").ap()
    C.BBT = nc.dram_tensor("BBT", [64, 2, P, P], BF16, kind="Internal").ap()
    C.CTS = nc.dram_tensor("CTS", [64, 3, P, P], BF16, kind="Internal").ap()
    C.TAB = nc.dram_tensor("TAB", [P, 64, 4, 64], F32, kind="Internal").ap()
    C.APOW = nc.dram_tensor("APOW", [P, 64, 10], F32, kind="Internal").ap()
    C.GT = nc.dram_tensor("GT", [DC, P, T], BF16, kind="Internal").ap()
    C.ssm_out = nc.dram_tensor("ssm_out", [2, 3, 2, 64, P], F32, kind="ExternalOutput").ap()
    C.pbk = nc.dram_tensor("pbk", [512, 16, P], F32, kind="ExternalOutput").ap()
    C.pbv = nc.dram_tensor("pbv", [512, 16, P], F32, kind="ExternalOutput").ap()
    C.sbk = nc.dram_tensor("sbk", [P, 16, P], F32, kind="ExternalOutput").ap()
    C.sbv = nc.dram_tensor("sbv", [P, 16, P], F32, kind="ExternalOutput").ap()
    with ExitStack() as es:
        C.S = Sched(nc, es)
        phase_load(C)
        for i in range(cfg.get('depth', DEPTH)):
            if cfg.get('ffn', True):
                for half in range(2):
                    phase_ffn(C, 2 * i, 3 * i, half)
            if i % 3 == 0 and cfg.get('ssm', True):
                phase_ssm_prep(C, i // 3)
                if not cfg.get('prep_only', False):
                    phase_ssm_scan(C, i // 3, 3 * i + 1)
                    phase_ssm_glu(C, i // 3)
            if i % 3 == 1 and cfg.get('fox', True):
                phase_attn_proj(C, 3 * i + 1, C.wqkv_fox, C.pfk, C.pfv, C.sfk, C.sfv, 0, fox=True)
                if not cfg.get('proj_only', False):
                    phase_fox_attn(C)
            if i % 3 == 2 and cfg.get('band', True):
                phase_attn_proj(C, 3 * i + 1, C.wqkv_band, C.pbk, C.pbv, C.sbk, C.sbv, 1536)
                if not cfg.get('proj_only', False):
                    phase_band_attn(C)
            if cfg.get('ffn', True):
                for half in range(2):
                    phase_ffn(C, 2 * i + 1, 3 * i + 2, half)
        phase_final(C, 3 * DEPTH)
    return nc


def host_prep(inp):
    H = {}
    win = np.empty((2 * DEPTH, FC, 2, P, D), np.float32)
    wout = np.empty((2 * DEPTH, FC, P, D), np.float32)
    for i in range(DEPTH):
        for k, nm in enumerate(('ffn1', 'ffn2')):
            w = np.asarray(inp[nm + '_w_in'][i]).reshape(DC, P, 2, FC, P)
            win[2 * i + k] = w.transpose(3, 2, 1, 0, 4).reshape(FC, 2, P, D)
            wout[2 * i + k] = np.asarray(inp[nm + '_w_out'][i]).reshape(FC, P, D)
    H['win'] = win
    H['wout'] = wout
    nrm = np.empty((3 * DEPTH + 1, P, DC), np.float32)
    for i in range(DEPTH):
        nrm[3 * i] = np.asarray(inp['ffn1_norm'][i]).reshape(DC, P).T
        nrm[3 * i + 1] = np.asarray(inp['mix_norm'][i]).reshape(DC, P).T
        nrm[3 * i + 2] = np.asarray(inp['ffn2_norm'][i]).reshape(DC, P).T
    nrm[3 * DEPTH] = np.asarray(inp['final_norm']).reshape(DC, P).T
    H['nrm'] = nrm
    H['ident'] = np.eye(P, dtype=np.float32)
    wb = np.asarray(inp['band_w_in'][0]).reshape(DC, P, 48, P)
    H['wqkv_band'] = np.ascontiguousarray(wb.transpose(2, 1, 0, 3)).reshape(48, P, D)
    wo = np.asarray(inp['band_w_out'][0]).reshape(16, P, DC, P)
    H['wo_band'] = np.ascontiguousarray(wo.transpose(2, 1, 0, 3)).reshape(DC, P, D)
    def pk(a):
        return np.ascontiguousarray(np.asarray(a, np.float32).reshape(64, 2, 64).transpose(1, 2, 0)).reshape(P, 64)
    H['ssmp'] = np.stack([np.stack([pk(inp['ssm_a_re'][j]), pk(inp['ssm_a_im'][j]),
                                    pk(np.repeat(np.asarray(inp['ssm_log_step'][j])[:, None], 64, 1))], 0) for j in range(2)], 0)
    def pkb(b):
        return np.ascontiguousarray(np.asarray(b, np.float32).reshape(64, 2, 64, 16).transpose(1, 2, 0, 3)).reshape(P, 64, 16)
    def pkc(c_):
        return np.ascontiguousarray(np.asarray(c_, np.float32).reshape(64, 2, 16, 64).transpose(1, 3, 0, 2)).reshape(P, 64, 16)
    H['ssmb'] = np.stack([np.stack([pkb(inp['ssm_b_re'][j]), pkb(inp['ssm_b_im'][j])], 0) for j in range(2)], 0)
    H['ssmc'] = np.stack([np.stack([pkc(inp['ssm_c_re'][j]), pkc(inp['ssm_c_im'][j])], 0) for j in range(2)], 0)
    H['ssmd'] = np.stack([np.asarray(inp['ssm_d'][j], np.float32).reshape(DC, P).T for j in range(2)], 0).copy()
    wg = np.stack([np.asarray(inp['ssm_w_glu'][j]).reshape(DC, P, 2, DC, P).transpose(3, 2, 1, 0, 4).reshape(DC, 2, P, D)
                   for j in range(2)], 0)
    H['wglu'] = np.ascontiguousarray(wg)
    H['pos1'] = np.ascontiguousarray(np.broadcast_to(np.arange(1, 65, dtype=np.float32), (P, 64)))
    rm = np.ones((P, T), np.float32)
    rm[:, ::64] = 0.0
    H['rmask'] = rm
    H['_pk'] = pk
    wfx = np.asarray(inp['fox_w_in'][0])
    H['wqkv_fox'] = np.ascontiguousarray(wfx[:, :3 * D].reshape(DC, P, 48, P).transpose(2, 1, 0, 3)).reshape(48, P, D)
    H['wf_fox'] = np.ascontiguousarray(wfx[:, 3 * D:].reshape(DC, P, 16).transpose(1, 0, 2))
    wo = np.asarray(inp['fox_w_out'][0]).reshape(16, P, DC, P)
    H['wo_fox'] = np.ascontiguousarray(wo.transpose(2, 1, 0, 3)).reshape(DC, P, D)
    H['bf_fox'] = np.ascontiguousarray(np.broadcast_to(np.asarray(inp['fox_b_f'][0], np.float32), (P, NCH, 16)))
    ii = np.arange(P)
    u1 = (ii[:, None] <= ii[None, :]).astype(np.float32)
    u2 = u1 * ((ii[:, None] // 64) == (ii[None, :] // 64))
    H['utri'] = np.stack([u1, u2.astype(np.float32)], 0)
    fm = np.zeros((P, 3, P), np.float32)
    fm[:, 0, :] = u1
    for s_ in range(2):
        kl = ii[:, None] - 64 * s_
        fm[:, 1 + s_, :64] = ((kl >= 0) & (kl < 64) & (kl <= np.arange(64)[None, :])).astype(np.float32)
    H['fmask'] = fm
    rb = np.asarray(inp['band_rel_bias'][0])
    kk = np.arange(P)[:, None]
    bp = np.empty((16, 5, P, P), np.float32)
    for t in range(5):
        qq = np.arange(P)[None, :]
        idx = np.clip(512 - 128 * t + qq - kk, -256, 256) + 256
        v = rb[:, idx]
        if t == 0:
            v = np.where(((kk < 64) & (qq >= 64))[None], np.float32(NEG), v)
        if t == 4:
            v = np.where(((kk >= 64) & (qq < 64))[None], np.float32(NEG), v)
        bp[:, t] = v
    H['bbias_p'] = bp
    bs = np.empty((16, 2, 5, P, 64), np.float32)
    qq = np.arange(64)[None, :]
    for t in range(4):
        idx = np.clip(512 + qq - 128 * t - kk, -256, 256) + 256
        bs[:, 0, t] = rb[:, idx]
        bs[:, 1, t] = rb[:, idx]
    for s_ in range(2):
        kl = kk - 64 * s_
        idx = np.clip(qq - kl, -256, 256) + 256
        v = rb[:, idx]
        bs[:, s_, 4] = np.where(((kl < 0) | (kl >= 64))[None], np.float32(NEG), v)
    H['bbias_s'] = bs
    H['ones'] = np.ones((P, P), np.float32)
    return H


def core_inputs(inp, H, c):
    m = {k_: v for k_, v in H.items() if not k_.startswith('_')}
    pk = H['_pk']
    m['ssms'] = np.stack([np.stack([np.stack([pk(inp['state_ssm_re'][j, 2 * c + s_]), pk(inp['state_ssm_im'][j, 2 * c + s_])], 0)
                                    for s_ in range(2)], 0) for j in range(2)], 0)
    m['xp'] = np.ascontiguousarray(inp['x_prompt'][c])
    m['xs'] = np.ascontiguousarray(np.asarray(inp['x_sample'][2 * c:2 * c + 2]).reshape(P, D))
    m['cfk'] = np.ascontiguousarray(inp['cache_fox_k'][0, 2 * c:2 * c + 2])
    m['cfv'] = np.ascontiguousarray(inp['cache_fox_v'][0, 2 * c:2 * c + 2])
    m['cfl'] = np.ascontiguousarray(inp['cache_fox_logf'][0, 2 * c:2 * c + 2])
    m['cbk'] = np.ascontiguousarray(inp['cache_band_k'][0, 2 * c:2 * c + 2])
    m['cbv'] = np.ascontiguousarray(inp['cache_band_v'][0, 2 * c:2 * c + 2])
    return m


def kernel(**inp):
    nc = build_nc({})
    H = host_prep(inp)
    in_maps = [core_inputs(inp, H, c) for c in range(NCORES)]
    res = run_bass_kernel_spmd(nc, in_maps, core_ids=list(range(NCORES)))
    R_ = res.results
    f32 = np.float32
    yp = np.stack([r['yp'] for r in R_], 0).astype(f32, copy=False)
    ys = np.concatenate([r['ys'].reshape(2, 64, D) for r in R_], 0).astype(f32, copy=False)
    so = [np.asarray(r['ssm_out']) for r in R_]
    p_re = np.stack([np.stack([so[c][j, 0, 0].reshape(128, 64) for c in range(NCORES)], 0) for j in range(2)], 0)
    p_im = np.stack([np.stack([so[c][j, 0, 1].reshape(128, 64) for c in range(NCORES)], 0) for j in range(2)], 0)
    s_re = np.stack([np.stack([so[c][j, 1 + s_, 0].reshape(128, 64) for c in range(NCORES) for s_ in range(2)], 0) for j in range(2)], 0)
    s_im = np.stack([np.stack([so[c][j, 1 + s_, 1].reshape(128, 64) for c in range(NCORES) for s_ in range(2)], 0) for j in range(2)], 0)
    stk = lambda nm: np.stack([np.asarray(r[nm]) for r in R_], 0)[None]
    cat = lambda nm, tail: np.concatenate([np.asarray(r[nm]).reshape((2, 64) + tail) for r in R_], 0)[None]
    return (yp, ys, p_re.astype(f32), p_im.astype(f32),
            stk('pfk'), stk('pfv'), stk('pfl'), stk('pbk'), stk('pbv'),
            s_re.astype(f32), s_im.astype(f32),
            cat('sfk', (16, P)), cat('sfv', (16, P)), cat('sfl', (16,)),
            cat('sbk', (16, P)), cat('sbv', (16, P)))
```

```python
import os
import numpy as np
from contextlib import ExitStack
import concourse.bass as bass
import concourse.mybir as mybir
from concourse.bass_utils import run_bass_kernel_spmd

F32 = mybir.dt.float32
BF16 = mybir.dt.bfloat16
AF = mybir.ActivationFunctionType
ALU = mybir.AluOpType

P = 128
D = 2048
DC = 16
DFF = 5632
FC = 44
T = 2176
NCH = 17
DEPTH = 4
EPS = 1e-6
NCORES = 8
TILES_ALL = [(0, 512), (512, 512), (1024, 512), (1536, 512), (2048, 128)]
HALF_TILES = [[(0, 512), (512, 512), (1024, 64)], [(1088, 512), (1600, 512), (2112, 64)]]
HALF_W = 1088

ENG4 = ('pe', 'act', 'dve', 'pool')
NSETS = 5
SET_LIMIT = 20000
NDSEM = 64


class Buf:
    __slots__ = ('name', 'last_w', 'readers', 'dsem')

    def __init__(self, name):
        self.name = name
        self.last_w = None
        self.readers = []
        self.dsem = None


class Op:
    __slots__ = ('eng', 'fn', 'cdeps', 'dwaits', 'dma_sem', 'need_inc', 'inc_val', 'idx', 'is_dma')


class Sched:
    def __init__(self, nc, es):
        self.nc = nc
        self.esems = {e: [es.enter_context(nc.semaphore(f"es_{e}_{k}")) for k in range(NSETS)] for e in ENG4}
        self.eset = 0
        self.ecount = {e: 0 for e in ENG4}
        self.dpool = [[es.enter_context(nc.semaphore(f"ds_{k}")), 0] for k in range(NDSEM)]
        self.dnext = 0
        self.engobj = {'pe': nc.tensor, 'act': nc.scalar, 'dve': nc.vector, 'pool': nc.gpsimd, 'sp': nc.sync}
        self.begin()

    def begin(self):
        self.ops = []
        self.nops = {e: 0 for e in ENG4}
        self.lastop = {e: None for e in ENG4}
        self.waited = {e: {} for e in ('pe', 'act', 'dve', 'pool', 'sp')}
        self.used_dsems = []
        if max(self.ecount.values()) > SET_LIMIT:
            self.eset += 1
            self.ecount = {e: 0 for e in ENG4}

    def _deps(self, eng, reads, writes):
        prods = []
        for b in reads:
            if b.last_w is not None:
                prods.append(b.last_w)
        for b in writes:
            if b.last_w is not None:
                prods.append(b.last_w)
            prods.extend(b.readers)
        cbest = {}
        dwaits = {}
        w = self.waited[eng]
        for p in prods:
            if p.is_dma:
                ent = p.dma_sem
                val = ent[1] * 16
                if w.get(id(ent), 0) < val:
                    dwaits[id(ent)] = (ent, val)
            else:
                if p.eng == eng and eng == 'pe':
                    continue
                if p.eng == eng and eng == 'dve' and self.nops[eng] - p.idx >= 2:
                    continue
                if w.get(p.eng, -1) >= p.idx:
                    continue
                if p.eng not in cbest or cbest[p.eng].idx < p.idx:
                    cbest[p.eng] = p
        for e, p in cbest.items():
            w[e] = p.idx
        for k, (ent, val) in dwaits.items():
            w[k] = val
        return list(cbest.values()), list(dwaits.values())

    def _post(self, o, reads, writes):
        for b in reads:
            b.readers.append(o)
        for b in writes:
            b.last_w = o
            b.readers = []
        self.ops.append(o)

    def op(self, eng, fn, reads=(), writes=()):
        o = Op()
        o.eng = eng
        o.fn = fn
        o.is_dma = False
        o.dma_sem = None
        o.need_inc = False
        o.inc_val = 0
        o.cdeps, o.dwaits = self._deps(eng, reads, writes)
        o.idx = self.nops[eng]
        self.nops[eng] += 1
        self.lastop[eng] = o
        self._post(o, reads, writes)
        return o

    def dma(self, out, in_, sbuf, reads=(), writes=(), q='sp'):
        if sbuf.dsem is None:
            sbuf.dsem = self.dpool[self.dnext % NDSEM]
            self.dnext += 1
            self.used_dsems.append(sbuf.dsem)
        o = Op()
        o.eng = q
        o.fn = lambda e, out=out, in_=in_: e.dma_start(out=out, in_=in_)
        o.is_dma = True
        o.need_inc = False
        o.inc_val = 0
        o.cdeps, o.dwaits = self._deps(q, reads, writes)
        o.dma_sem = sbuf.dsem
        sbuf.dsem[1] += 1
        o.idx = -1
        self._post(o, reads, writes)
        return o

    def end(self):
        lasts = [o for o in self.lastop.values() if o is not None]
        dents = list({id(e): e for e in self.used_dsems}.values())
        for eng in ('pe', 'act', 'dve', 'pool', 'sp'):
            o = Op()
            o.eng = eng
            o.fn = None
            o.is_dma = False
            o.dma_sem = None
            o.need_inc = False
            o.inc_val = 0
            o.idx = -2
            w = self.waited[eng]
            o.cdeps = [p for p in lasts if p.eng != eng and w.get(p.eng, -1) < p.idx]
            o.dwaits = [(ent, ent[1] * 16) for ent in dents if w.get(id(ent), 0) < ent[1] * 16]
            self.ops.append(o)
        for o in self.ops:
            for p in o.cdeps:
                p.need_inc = True
        for o in self.ops:
            if o.need_inc:
                self.ecount[o.eng] += 1
                o.inc_val = self.ecount[o.eng]
        for o in self.ops:
            e = self.engobj[o.eng]
            for p in o.cdeps:
                e.wait_ge(self.esems[p.eng][self.eset], p.inc_val)
            for ent, val in o.dwaits:
                e.wait_ge(ent[0], val)
            if o.fn is None:
                continue
            ins = o.fn(e)
            if o.is_dma:
                ins.then_inc(o.dma_sem[0], 16)
            elif o.need_inc:
                ins.then_inc(self.esems[o.eng][self.eset], 1)
        self.begin()


class Ctx:
    pass


_UID = [0]


def _sb(nc, es, name, shape, dt):
    _UID[0] += 1
    return es.enter_context(nc.sbuf_tensor(f"s{_UID[0]}_{name}", shape, dt))


def _ps(nc, es, name, dt=F32, w=512):
    _UID[0] += 1
    return es.enter_context(nc.psum_tensor(f"p{_UID[0]}_{name}", [P, w], dt))


def phase_load(C):
    nc, S = C.nc, C.S
    with ExitStack() as es:
        ident = _sb(nc, es, "ident", [P, P], F32)
        b_ident = Buf("ident")
        xin = [_sb(nc, es, f"xin{i}", [P, D], F32) for i in range(2)]
        b_xin = [Buf(f"xin{i}") for i in range(2)]
        stg = [_sb(nc, es, f"lstg{i}", [P, DC, P], F32) for i in range(2)]
        b_stg = [Buf(f"lstg{i}") for i in range(2)]
        pst = [_ps(nc, es, f"lps{i}") for i in range(4)]
        b_pst = [Buf(f"lps{i}") for i in range(4)]
        S.dma(ident[:], C.ident[:, :], b_ident, writes=[b_ident])
        k = 0
        for ch in range(NCH):
            src = C.xp[ch * P:(ch + 1) * P, :] if ch < 16 else C.xs[:, :]
            xi, bxi = xin[ch % 2], b_xin[ch % 2]
            st, bst = stg[ch % 2], b_stg[ch % 2]
            S.dma(xi[:], src, bxi, writes=[bxi])
            for g in range(4):
                ps, bps = pst[k % 4], b_pst[k % 4]
                k += 1
                for j in range(4):
                    dc = g * 4 + j
                    S.op('pe', lambda e, ps=ps, j=j, xi=xi, dc=dc: e.transpose(
                        ps[:, j * P:(j + 1) * P], xi[:, dc * P:(dc + 1) * P], ident[:]),
                        reads=[bxi, b_ident], writes=[bps])
                eng = 'dve' if g % 2 == 0 else 'act'
                if eng == 'dve':
                    S.op('dve', lambda e, ps=ps, st=st, g=g: e.tensor_copy(
                        st[:, g * 4:(g + 1) * 4, :], ps[:].rearrange("p (a b) -> p a b", a=4)),
                        reads=[bps], writes=[bst])
                else:
                    S.op('act', lambda e, ps=ps, st=st, g=g: e.copy(
                        st[:, g * 4:(g + 1) * 4, :], ps[:].rearrange("p (a b) -> p a b", a=4)),
                        reads=[bps], writes=[bst])
            S.dma(C.XT[:, :, ch * P:(ch + 1) * P].rearrange("c p t -> p c t"), st[:], bst, reads=[bst])
        S.end()


def rms_tile(C, x_sl, b_x, w, sq, b_sq, ps_ss, b_ss, ones_bf, b_ones, tmp, b_tmp, rstd_sl, b_rstd):
    S = C.S
    for dc in range(DC):
        s, bs = sq[dc % 2], b_sq[dc % 2]
        S.op('act', lambda e, s=s, dc=dc: e.activation(s[:, :w], x_sl(dc), AF.Square),
             reads=[b_x(dc) if callable(b_x) else b_x], writes=[bs])
        S.op('pe', lambda e, s=s, dc=dc: e.matmul(ps_ss[:, :w], ones_bf[:], s[:, :w],
                                                 start=(dc == 0), stop=(dc == DC - 1)),
             reads=[bs, b_ones], writes=[b_ss])
    S.op('dve', lambda e: e.tensor_scalar(tmp[:, :w], ps_ss[:, :w], 1.0 / D, EPS, ALU.mult, ALU.add),
         reads=[b_ss], writes=[b_tmp])
    S.op('act', lambda e: e.activation(tmp[:, :w], tmp[:, :w], AF.Sqrt), reads=[b_tmp], writes=[b_tmp])
    S.op('dve', lambda e: e.reciprocal(rstd_sl, tmp[:, :w]), reads=[b_tmp], writes=[b_rstd])


def phase_ffn(C, fi, ni, half):
    nc, S = C.nc, C.S
    tiles = HALF_TILES[half]
    t0 = half * HALF_W
    FB = 2
    NB = FC // FB
    with ExitStack() as es:
        acc = _sb(nc, es, "acc", [P, DC, HALF_W], F32)
        xn = _sb(nc, es, "xn", [P, DC, HALF_W], BF16)
        rstd = _sb(nc, es, "rstd", [P, HALF_W], F32)
        gv = _sb(nc, es, "gv", [P, DC], F32)
        ones_bf = _sb(nc, es, "ones_bf", [P, P], BF16)
        ones_f = _sb(nc, es, "ones_f", [P, P], F32)
        sq = [_sb(nc, es, f"sq{i}", [P, 512], BF16) for i in range(2)]
        tmp = _sb(nc, es, "tmp", [P, 512], F32)
        sg = [_sb(nc, es, f"sg{i}", [P, 512], F32) for i in range(2)]
        NSTG = 3
        stg = [_sb(nc, es, f"stg{i}", [P, D], F32) for i in range(NSTG)]
        wgb = [_sb(nc, es, f"wgb{i}", [P, D], BF16) for i in range(3)]
        wub = [_sb(nc, es, f"wub{i}", [P, D], BF16) for i in range(3)]
        wob = [_sb(nc, es, f"wob{i}", [P, D], BF16) for i in range(2 * FB)]
        hb = _sb(nc, es, "hb", [P, 2, FB, HALF_W], BF16)
        psg = [_ps(nc, es, f"psg{i}") for i in range(2)]
        psu = [_ps(nc, es, f"psu{i}") for i in range(2)]
        pso = [_ps(nc, es, f"pso{i}") for i in range(3)]
        pss = _ps(nc, es, "pss")

        b_acc = [[Buf(f"acc{d}_{t}") for t in range(3)] for d in range(DC)]
        b_xn = [Buf(f"xn{t}") for t in range(3)]
        b_rstd = [Buf(f"rstd{t}") for t in range(3)]
        b_gv, b_ones, b_onesf, b_tmp, b_pss = Buf("gv"), Buf("ones"), Buf("onesf"), Buf("tmp"), Buf("pss")
        b_sq = [Buf("sq0"), Buf("sq1")]
        b_sg = [Buf("sg0"), Buf("sg1")]
        b_stg = [Buf(f"stg{i}") for i in range(NSTG)]
        b_wgb = [Buf(f"wgb{i}") for i in range(3)]
        b_wub = [Buf(f"wub{i}") for i in range(3)]
        b_wob = [Buf(f"wob{i}") for i in range(2 * FB)]
        b_h = [[[Buf(f"h{a}{c}{t}") for t in range(3)] for c in range(FB)] for a in range(2)]
        b_psg = [Buf("psg0"), Buf("psg1")]
        b_psu = [Buf("psu0"), Buf("psu1")]
        b_pso = [Buf("pso0"), Buf("pso1"), Buf("pso2")]
        b_accall = [b for row in b_acc for b in row]

        S.dma(gv[:], C.nrm[ni], b_gv, writes=[b_gv])
        S.dma(ones_f[:], C.ones[:, :], b_onesf, writes=[b_onesf])
        S.op('dve', lambda e: e.tensor_copy(ones_bf[:], ones_f[:]), reads=[b_onesf], writes=[b_ones])
        for dc in range(DC):
            S.dma(acc[:, dc, :], C.XT[dc, :, t0:t0 + HALF_W], b_acc[dc][0], writes=b_acc[dc])

        items = []
        for b in range(NB + 1):
            for cl in range(FB):
                if b < NB:
                    c = b * FB + cl
                    items.append(('g', c, C.win[fi, c, 0], wgb[c % 3], b_wgb[c % 3]))
                    items.append(('u', c, C.win[fi, c, 1], wub[c % 3], b_wub[c % 3]))
                if cl == 0 and b >= 1:
                    for cl2 in range(FB):
                        c2 = (b - 1) * FB + cl2
                        slot = ((b - 1) % 2) * FB + cl2
                        items.append(('o', c2, C.wout[fi, c2], wob[slot], b_wob[slot]))
        state = {'n': 0}

        def fetch_upto(n):
            while state['n'] < min(n, len(items)):
                i = state['n']
                _, _, src, dst, bdst = items[i]
                st, bst = stg[i % NSTG], b_stg[i % NSTG]
                S.dma(st[:], src, bst, writes=[bst])
                S.op('pool', lambda e, dst=dst, st=st: e.tensor_copy(dst[:], st[:]), reads=[bst], writes=[bdst])
                state['n'] += 1

        fetch_upto(NSTG)

        for ti, (ts, w) in enumerate(tiles):
            lo = ts - t0
            rms_tile(C, lambda dc, lo=lo, w=w: acc[:, dc, lo:lo + w], (lambda dc, ti=ti: b_acc[dc][ti]), w, sq, b_sq, pss, b_pss,
                     ones_bf, b_ones, tmp, b_tmp, rstd[:, lo:lo + w], b_rstd[ti])
            for dc in range(DC):
                S.op('dve', lambda e, dc=dc, lo=lo, w=w: e.scalar_tensor_tensor(
                    xn[:, dc, lo:lo + w], acc[:, dc, lo:lo + w], gv[:, dc:dc + 1], rstd[:, lo:lo + w],
                    ALU.mult, ALU.mult),
                    reads=[b_acc[dc][ti], b_gv, b_rstd[ti]], writes=[b_xn[ti]])

        evt = [_sb(nc, es, f"evt{i}", [P, 512], F32) for i in range(2)]
        b_evt = [Buf(f"evt{i}") for i in range(2)]
        cnt = {'k': 0, 'ko': 0, 'ev': 0}

        def glu_unit(b, cl, ti):
            par = b % 2
            c = b * FB + cl
            ts, w = tiles[ti]
            lo = ts - t0
            wg, bwg, wu, bwu = wgb[c % 3], b_wgb[c % 3], wub[c % 3], b_wub[c % 3]
            k = cnt['k']
            cnt['k'] += 1
            pg, bpg, pu, bpu = psg[k % 2], b_psg[k % 2], psu[k % 2], b_psu[k % 2]
            sgt, bsg = sg[k % 2], b_sg[k % 2]
            for dc in range(DC):
                S.op('pe', lambda e, dc=dc: e.matmul(pg[:, :w], wg[:, dc * P:(dc + 1) * P], xn[:, dc, lo:lo + w],
                                                     start=(dc == 0), stop=(dc == DC - 1)), reads=[bwg, b_xn[ti]], writes=[bpg])
            for dc in range(DC):
                S.op('pe', lambda e, dc=dc: e.matmul(pu[:, :w], wu[:, dc * P:(dc + 1) * P], xn[:, dc, lo:lo + w],
                                                     start=(dc == 0), stop=(dc == DC - 1)), reads=[bwu, b_xn[ti]], writes=[bpu])
            S.op('act', lambda e: e.activation(sgt[:, :w], pg[:, :w], AF.Silu), reads=[bpg], writes=[bsg])
            S.op('dve', lambda e: e.tensor_tensor(hb[:, par, cl, lo:lo + w], sgt[:, :w], pu[:, :w], ALU.mult),
                 reads=[bsg, bpu], writes=[b_h[par][cl][ti]])

        def out_group(b, d, ti):
            par = b % 2
            ts, w = tiles[ti]
            lo = ts - t0
            ko = cnt['ko']
            cnt['ko'] += 1
            po, bpo = pso[ko % 3], b_pso[ko % 3]
            for cl in range(FB):
                slot = par * FB + cl
                S.op('pe', lambda e, cl=cl, slot=slot: e.matmul(
                    po[:, :w], wob[slot][:, d * P:(d + 1) * P], hb[:, par, cl, lo:lo + w],
                    start=(cl == 0), stop=(cl == FB - 1)), reads=[b_wob[slot], b_h[par][cl][ti]], writes=[bpo])
            if ko % 3 == 2 and w == 512:
                ev = cnt['ev']
                cnt['ev'] += 1
                et, bet = evt[ev % 2], b_evt[ev % 2]
                S.op('act', lambda e: e.mul(et[:, :w], po[:, :w], 0.5), reads=[bpo], writes=[bet])
                S.op('pool', lambda e: e.tensor_tensor(acc[:, d, lo:lo + w], acc[:, d, lo:lo + w], et[:, :w], ALU.add),
                     reads=[bet, b_acc[d][ti]], writes=[b_acc[d][ti]])
            else:
                S.op('dve', lambda e: e.scalar_tensor_tensor(
                    acc[:, d, lo:lo + w], po[:, :w], 0.5, acc[:, d, lo:lo + w], ALU.mult, ALU.add),
                    reads=[bpo, b_acc[d][ti]], writes=[b_acc[d][ti]])

        for b in range(NB + 1):
            units = [(cl, ti) for cl in range(FB) for ti in range(3)] if b < NB else []
            groups = [(d, ti) for d in range(DC) for ti in range(3)] if b >= 1 else []
            nu = max(len(units), 1)
            per = (len(groups) + nu - 1) // nu
            gi = 0
            for ui in range(nu):
                if units:
                    cl, ti = units[ui]
                    if ti == 0:
                        state['need'] = state.get('need', 0) + 2 + (FB if (cl == 0 and b >= 1) else 0)
                        fetch_upto(state['need'] + 4)
                    glu_unit(b, cl, ti)
                elif ui == 0:
                    fetch_upto(len(items))
                for _ in range(per):
                    if gi < len(groups):
                        out_group(b - 1, groups[gi][0], groups[gi][1])
                        gi += 1
        for dc in range(DC):
            S.dma(C.XT[dc, :, t0:t0 + HALF_W], acc[:, dc, :], b_acc[dc][0], reads=b_acc[dc])
        S.end()


ATTN_SCALE = 128 ** -0.5
NEG = -30000.0


def load_xn_all(C, es, ni, xn, b_xn):
    nc, S = C.nc, C.S
    x32 = _sb(nc, es, "x32", [P, DC, 512], F32)
    b_x32 = Buf("x32")
    rstd = _sb(nc, es, "rstdm", [P, 512], F32)
    b_rstd = Buf("rstdm")
    gv = _sb(nc, es, "gvm", [P, DC], F32)
    b_gv = Buf("gvm")
    ones_f = _sb(nc, es, "ones_fm", [P, P], F32)
    ones_bf = _sb(nc, es, "ones_bfm", [P, P], BF16)
    b_onesf, b_ones = Buf("onesf"), Buf("ones")
    sq = [_sb(nc, es, f"sqm{i}", [P, 512], BF16) for i in range(2)]
    b_sq = [Buf("sqm0"), Buf("sqm1")]
    tmp = _sb(nc, es, "tmpm", [P, 512], F32)
    b_tmp = Buf("tmpm")
    pss = _ps(nc, es, "pssm")
    b_pss = Buf("pssm")
    S.dma(gv[:], C.nrm[ni], b_gv, writes=[b_gv])
    S.dma(ones_f[:], C.ones[:, :], b_onesf, writes=[b_onesf])
    S.op('dve', lambda e: e.tensor_copy(ones_bf[:], ones_f[:]), reads=[b_onesf], writes=[b_ones])
    for ti, (ts, w) in enumerate(TILES_ALL):
        S.dma(x32[:, :, :w], C.XT[:, :, ts:ts + w].rearrange("c p t -> p c t"), b_x32, writes=[b_x32])
        rms_tile(C, lambda dc, w=w: x32[:, dc, :w], b_x32, w, sq, b_sq, pss, b_pss, ones_bf, b_ones,
                 tmp, b_tmp, rstd[:, :w], b_rstd)
        for dc in range(DC):
            S.op('dve', lambda e, dc=dc, ts=ts, w=w: e.scalar_tensor_tensor(
                xn[:, dc, ts:ts + w], x32[:, dc, :w], gv[:, dc:dc + 1], rstd[:, :w], ALU.mult, ALU.mult),
                reads=[b_x32, b_gv, b_rstd], writes=[b_xn[ti]])
    return x32, b_x32


class WStream:
    def __init__(self, C, es, srcs, nstg=3, nwb=2, tag="ws"):
        nc = C.nc
        self.C = C
        self.srcs = srcs
        self.nstg, self.nwb = nstg, nwb
        self.stg = [_sb(nc, es, f"{tag}stg{i}", [P, D], F32) for i in range(nstg)]
        self.b_stg = [Buf(f"{tag}stg{i}") for i in range(nstg)]
        self.wb = [_sb(nc, es, f"{tag}wb{i}", [P, D], BF16) for i in range(nwb)]
        self.b_wb = [Buf(f"{tag}wb{i}") for i in range(nwb)]
        self.n = 0

    def fetch_upto(self, n):
        S = self.C.S
        while self.n < min(n, len(self.srcs)):
            i = self.n
            st, bst = self.stg[i % self.nstg], self.b_stg[i % self.nstg]
            dst, bdst = self.wb[i % self.nwb], self.b_wb[i % self.nwb]
            S.dma(st[:], self.srcs[i], bst, writes=[bst])
            S.op('pool', lambda e, dst=dst, st=st: e.tensor_copy(dst[:], st[:]), reads=[bst], writes=[bdst])
            self.n += 1

    def get(self, i):
        self.fetch_upto(i + self.nwb)
        return self.wb[i % self.nwb], self.b_wb[i % self.nwb]


def phase_attn_proj(C, ni, wqkv, kout, vout, skout, svout, prow0, fox=False):
    nc, S = C.nc, C.S
    with ExitStack() as es:
        xn = _sb(nc, es, "xna", [P, DC, T], BF16)
        b_xn = [Buf(f"xna{t}") for t in range(5)]
        load_xn_all(C, es, ni, xn, b_xn)
        ident = _sb(nc, es, "identa", [P, P], F32)
        b_ident = Buf("identa")
        S.dma(ident[:], C.ident[:, :], b_ident, writes=[b_ident])
        ws = WStream(C, es, [wqkv[i] for i in range(48)])
        rowf = [_sb(nc, es, f"rowf{i}", [P, T], F32) for i in range(2)]
        b_rowf = [Buf(f"rowf{i}") for i in range(2)]
        rowb = [_sb(nc, es, f"rowb{i}", [P, T], BF16) for i in range(2)]
        b_rowb = [Buf(f"rowb{i}") for i in range(2)]
        tm = [_sb(nc, es, f"tm{i}", [P, NCH, P], F32) for i in range(2)]
        b_tm = [Buf(f"tm{i}") for i in range(2)]
        tmb = [_sb(nc, es, f"tmb{i}", [P, NCH, P], BF16) for i in range(2)]
        b_tmb = [Buf(f"tmb{i}") for i in range(2)]
        psm = [_ps(nc, es, f"psm{i}") for i in range(2)]
        b_psm = [Buf(f"psm{i}") for i in range(2)]
        pst = [_ps(nc, es, f"psta{i}") for i in range(2)]
        b_pst = [Buf(f"psta{i}") for i in range(2)]
        if fox:
            fox_logf(C, es, xn, b_xn)
        ws.fetch_upto(2)
        km = 0
        kt = 0
        for cc in range(int(os.environ.get('NCC', 48))):
            which, h = cc // 16, cc % 16
            wb, bwb = ws.get(cc)
            rf, brf = rowf[cc % 2], b_rowf[cc % 2]
            rb, brb = rowb[cc % 2], b_rowb[cc % 2]
            for ti, (ts, w) in enumerate(TILES_ALL):
                pm, bpm = psm[km % 2], b_psm[km % 2]
                km += 1
                for dc in range(DC):
                    S.op('pe', lambda e, pm=pm, wb=wb, dc=dc, ts=ts, w=w: e.matmul(
                        pm[:, :w], wb[:, dc * P:(dc + 1) * P], xn[:, dc, ts:ts + w],
                        start=(dc == 0), stop=(dc == DC - 1)), reads=[bwb, b_xn[ti]], writes=[bpm])
                if which == 0:
                    S.op('act', lambda e, pm=pm, rb=rb, ts=ts, w=w: e.mul(rb[:, ts:ts + w], pm[:, :w], ATTN_SCALE),
                         reads=[bpm], writes=[brb])
                else:
                    S.op('act', lambda e, pm=pm, rf=rf, ts=ts, w=w: e.copy(rf[:, ts:ts + w], pm[:, :w]),
                         reads=[bpm], writes=[brf])
                    if which == 1:
                        S.op('dve', lambda e, rf=rf, rb=rb, ts=ts, w=w: e.tensor_copy(rb[:, ts:ts + w], rf[:, ts:ts + w]),
                             reads=[brf], writes=[brb])
            if which == 0:
                S.dma(C.QT[h], rb[:], brb, reads=[brb])
                continue
            if which == 1:
                S.dma(C.KT[h], rb[:], brb, reads=[brb])
            t_, bt_ = tm[cc % 2], b_tm[cc % 2]
            for ch in range(NCH if not os.environ.get('SKIP_TR') else 0):
                pt, bpt = pst[kt % 2], b_pst[kt % 2]
                kt += 1
                S.op('pe', lambda e, pt=pt, rf=rf, ch=ch: e.transpose(pt[:, :P], rf[:, ch * P:(ch + 1) * P], ident[:]),
                     reads=[brf, b_ident], writes=[bpt])
                if ch % 2 == 0:
                    S.op('dve', lambda e, pt=pt, t_=t_, ch=ch: e.tensor_copy(t_[:, ch, :], pt[:, :P]),
                         reads=[bpt], writes=[bt_])
                else:
                    S.op('act', lambda e, pt=pt, t_=t_, ch=ch: e.copy(t_[:, ch, :], pt[:, :P]),
                         reads=[bpt], writes=[bt_])
            dst_p, dst_s = (kout, skout) if which == 1 else (vout, svout)
            c0 = prow0 // P
            if not os.environ.get('SKIP_OUT'):
                S.dma(dst_p[:, h, :].rearrange("(c p) d -> p c d", p=P), t_[:, c0:16, :], bt_, reads=[bt_])
                S.dma(dst_s[:, h, :], t_[:, 16, :], bt_, reads=[bt_])
            if which == 2:
                tb, btb = tmb[cc % 2], b_tmb[cc % 2]
                S.op('pool', lambda e, tb=tb, t_=t_: e.tensor_copy(tb[:], t_[:]), reads=[bt_], writes=[btb])
                S.dma(C.VTM[h].rearrange("c p d -> p c d"), tb[:], btb, reads=[btb])
        S.end()


def attn_tile(C, kT, b_kT, qT, b_qT, nq, vt, b_vt, bias_fn, R, first, last, out_fn):
    S = C.S
    i = R['k']
    R['k'] += 1
    ps, bps = R['pss'][i % 2], R['b_pss'][i % 2]
    sb_, bsb = R['sb'][i % 2], R['b_sb'][i % 2]
    pT, bpT = R['pT'][i % 2], R['b_pT'][i % 2]
    S.op('pe', lambda e: e.matmul(ps[:, :nq], kT, qT, start=True, stop=True), reads=[b_kT, b_qT], writes=[bps])
    bias_fn(ps, bps, sb_, bsb, pT, bpT)
    po, bpo, pl, bpl = R['po'], R['b_po'], R['pl'], R['b_pl']
    S.op('pe', lambda e: e.matmul(po[:, :nq], vt, pT[:, :nq], start=first, stop=last), reads=[b_vt, bpT], writes=[bpo])
    S.op('pe', lambda e: e.matmul(pl[:, :nq], R['ones'][:], pT[:, :nq], start=first, stop=last),
         reads=[R['b_ones'], bpT], writes=[bpl])
    if last:
        rc, brc = R['rc'], R['b_rc']
        S.op('dve', lambda e: e.reciprocal(rc[:, :nq], pl[:, :nq]), reads=[bpl], writes=[brc])
        out_fn(po, bpo, rc, brc)


def attn_resources(C, es, ones_bf, b_ones):
    nc = C.nc
    R = {'k': 0}
    R['pss'] = [_ps(nc, es, f"pssc{i}") for i in range(2)]
    R['b_pss'] = [Buf(f"pssc{i}") for i in range(2)]
    R['sb'] = [_sb(nc, es, f"sbs{i}", [P, P], F32) for i in range(2)]
    R['b_sb'] = [Buf(f"sbs{i}") for i in range(2)]
    R['pT'] = [_sb(nc, es, f"pT{i}", [P, P], BF16) for i in range(2)]
    R['b_pT'] = [Buf(f"pT{i}") for i in range(2)]
    R['po'] = _ps(nc, es, "po")
    R['b_po'] = Buf("po")
    R['pl'] = _ps(nc, es, "pl")
    R['b_pl'] = Buf("pl")
    R['rc'] = _sb(nc, es, "rc", [P, P], F32)
    R['b_rc'] = Buf("rc")
    R['ones'] = ones_bf
    R['b_ones'] = b_ones
    return R


def out_proj(C, es, OT, b_OT, wo):
    nc, S = C.nc, C.S
    ws = WStream(C, es, [wo[i] for i in range(DC)], tag="wo")
    xr = [_sb(nc, es, f"xr{i}", [P, 512], F32) for i in range(2)]
    b_xr = [Buf(f"xr{i}") for i in range(2)]
    pso = [_ps(nc, es, f"psop{i}") for i in range(2)]
    b_pso = [Buf(f"psop{i}") for i in range(2)]
    ws.fetch_upto(2)
    k = 0
    for d in range(DC):
        wb, bwb = ws.get(d)
        for ti, (ts, w) in enumerate(TILES_ALL):
            po, bpo = pso[k % 2], b_pso[k % 2]
            x, bx = xr[k % 2], b_xr[k % 2]
            k += 1
            S.dma(x[:, :w], C.XT[d, :, ts:ts + w], bx, writes=[bx])
            for h in range(16):
                S.op('pe', lambda e, po=po, wb=wb, h=h, ts=ts, w=w: e.matmul(
                    po[:, :w], wb[:, h * P:(h + 1) * P], OT[:, h, ts:ts + w], start=(h == 0), stop=(h == 15)),
                    reads=[bwb, b_OT], writes=[bpo])
            S.op('dve', lambda e, po=po, x=x, w=w: e.tensor_tensor(x[:, :w], x[:, :w], po[:, :w], ALU.add),
                 reads=[bpo, bx], writes=[bx])
            S.dma(C.XT[d, :, ts:ts + w], x[:, :w], bx, reads=[bx])


def fox_logf(C, es, xn, b_xn):
    nc, S = C.nc, C.S
    wf32 = _sb(nc, es, "wf32", [P, DC, 16], F32)
    wfb = _sb(nc, es, "wfb", [P, DC, 16], BF16)
    bfb = _sb(nc, es, "bfb", [P, NCH, 16], F32)
    z = _sb(nc, es, "zlf", [P, NCH, 16], F32)
    lf = _sb(nc, es, "lf", [P, NCH, 16], F32)
    plg = _ps(nc, es, "plg")
    b_wf32, b_wfb, b_bfb, b_z, b_lf, b_plg = Buf("wf32"), Buf("wfb"), Buf("bfb"), Buf("zlf"), Buf("lf"), Buf("plg")
    S.dma(wf32[:], C.wf_fox[:, :, :], b_wf32, writes=[b_wf32])
    S.dma(bfb[:], C.bf_fox[:, :, :], b_bfb, writes=[b_bfb])
    S.op('dve', lambda e: e.tensor_copy(wfb[:], wf32[:]), reads=[b_wf32], writes=[b_wfb])
    for ch in range(NCH):
        ti = min(ch // 4, 4)
        for dc in range(DC):
            S.op('pe', lambda e, ch=ch, dc=dc: e.matmul(plg[:, ch * 16:(ch + 1) * 16], xn[:, dc, ch * P:(ch + 1) * P],
                                                       wfb[:, dc, :], start=(dc == 0), stop=(dc == DC - 1)),
                 reads=[b_xn[ti], b_wfb], writes=[b_plg])
    zf = z[:].rearrange("p c h -> p (c h)")
    S.op('dve', lambda e: e.tensor_tensor(zf, plg[:, :NCH * 16], bfb[:].rearrange("p c h -> p (c h)"), ALU.add),
         reads=[b_plg, b_bfb], writes=[b_z])
    S.op('act', lambda e: e.activation(zf, zf, AF.Exp, scale=-1.0), reads=[b_z], writes=[b_z])
    S.op('dve', lambda e: e.tensor_scalar(zf, zf, 1.0, None, ALU.add), reads=[b_z], writes=[b_z])
    S.op('act', lambda e: e.activation(zf, zf, AF.Ln), reads=[b_z], writes=[b_z])
    S.op('dve', lambda e: e.tensor_scalar(lf[:].rearrange("p c h -> p (c h)"), zf, -1.0, None, ALU.mult),
         reads=[b_z], writes=[b_lf])
    S.dma(C.pfl[:, :].rearrange("(c p) h -> p c h", p=P), lf[:, 0:16, :], b_lf, reads=[b_lf])
    S.dma(C.sfl[:, :], lf[:, 16, :], b_lf, reads=[b_lf])
    S.dma(C.LFS[:, :, :], lf[:], b_lf, reads=[b_lf])


def prefix_chunks(C, es, tot, b_tot, n, tag):
    nc, S = C.nc, C.S
    a = _sb(nc, es, f"pxa{tag}", [P, n, 16], F32)
    b = _sb(nc, es, f"pxb{tag}", [P, n, 16], F32)
    b_a, b_b = Buf(f"pxa{tag}"), Buf(f"pxb{tag}")
    S.op('dve', lambda e: e.memset(a[:, 0, :], 0.0), writes=[b_a])
    S.op('dve', lambda e: e.tensor_copy(a[:, 1:n, :], tot[:, 0:n - 1, :]), reads=[b_tot, b_a], writes=[b_a])
    cur, bcur, oth, both = a, b_a, b, b_b
    sft = 1
    while sft < n:
        S.op('dve', lambda e, cur=cur, oth=oth, sft=sft: e.tensor_copy(oth[:, 0:sft, :], cur[:, 0:sft, :]),
             reads=[bcur], writes=[both])
        S.op('dve', lambda e, cur=cur, oth=oth, sft=sft: e.tensor_tensor(
            oth[:, sft:n, :], cur[:, sft:n, :], cur[:, 0:n - sft, :], ALU.add), reads=[bcur, both], writes=[both])
        cur, bcur, oth, both = oth, both, cur, bcur
        sft *= 2
    return cur, bcur


def phase_fox_attn(C):
    nc, S = C.nc, C.S
    with ExitStack() as eo, ExitStack() as es:
        OT = _sb(nc, eo, "OTf", [P, 16, T], BF16)
        BP = _sb(nc, eo, "BP", [P, 16, 16, 16], F32)
        BN = _sb(nc, eo, "BN", [P, 16], F32)
        BCt = [_sb(nc, eo, f"BC{s_}", [P, 32, 16], F32) for s_ in range(2)]
        ones_f = _sb(nc, eo, "ones_ffx", [P, P], F32)
        ones_bf = _sb(nc, eo, "ones_bffx", [P, P], BF16)
        ident = _sb(nc, eo, "identfx", [P, P], F32)
        mkb = _sb(nc, eo, "mkb", [P, 3, P], BF16)
        b_OT = Buf("OTf")
        utri = _sb(nc, es, "utri", [P, P], F32)
        utri2 = _sb(nc, es, "utri2", [P, P], F32)
        mk32 = _sb(nc, es, "mk32", [P, 3, P], F32)
        b_onesf, b_ones, b_ident, b_utri, b_utri2, b_mk32, b_mkb = (Buf("onesf"), Buf("ones"), Buf("ident"),
                                                                    Buf("utri"), Buf("utri2"), Buf("mk32"), Buf("mkb"))
        S.dma(ones_f[:], C.ones[:, :], b_onesf, writes=[b_onesf])
        S.dma(ident[:], C.ident[:, :], b_ident, writes=[b_ident])
        S.dma(utri[:], C.utri[0], b_utri, writes=[b_utri])
        S.dma(utri2[:], C.utri[1], b_utri2, writes=[b_utri2])
        S.dma(mk32[:], C.fmask[:, :, :], b_mk32, writes=[b_mk32])
        S.op('dve', lambda e: e.tensor_copy(ones_bf[:], ones_f[:]), reads=[b_onesf], writes=[b_ones])
        S.op('dve', lambda e: e.tensor_copy(mkb[:], mk32[:]), reads=[b_mk32], writes=[b_mkb])
        pcs = _ps(nc, es, "pcs")
        b_pcs = Buf("pcs")

        lf = _sb(nc, es, "lfa", [P, NCH, 16], F32)
        b_lf = Buf("lfa")
        S.dma(lf[:], C.LFS[:, :, :], b_lf, writes=[b_lf])
        inc = _sb(nc, es, "inc", [P, NCH, 16], F32)
        tot = _sb(nc, es, "tot", [P, 16, 16], F32)
        b_inc, b_tot = Buf("inc"), Buf("tot")
        lf2 = lf[:, 0:16, :].rearrange("p c h -> p (c h)")
        S.op('pe', lambda e: e.matmul(pcs[:, :256], utri[:], lf2, start=True, stop=True), reads=[b_utri, b_lf], writes=[b_pcs])
        S.op('dve', lambda e: e.tensor_copy(inc[:, 0:16, :].rearrange("p c h -> p (c h)"), pcs[:, :256]),
             reads=[b_pcs], writes=[b_inc])
        S.op('pe', lambda e: e.matmul(pcs[:, :16], utri2[:], lf[:, 16, :], start=True, stop=True),
             reads=[b_utri2, b_lf, b_inc], writes=[b_pcs])
        S.op('dve', lambda e: e.tensor_copy(inc[:, 16, :], pcs[:, :16]), reads=[b_pcs], writes=[b_inc])
        S.op('pe', lambda e: e.matmul(pcs[:, :256], ones_f[:], lf2, start=True, stop=True),
             reads=[b_onesf, b_lf, b_inc], writes=[b_pcs])
        S.op('dve', lambda e: e.tensor_copy(tot[:].rearrange("p c h -> p (c h)"), pcs[:, :256]), reads=[b_pcs], writes=[b_tot])
        E, b_E = prefix_chunks(C, es, tot, b_tot, 16, "p")
        incp = _sb(nc, es, "incp", [P, 16, 16], F32)
        b_incp = Buf("incp")
        S.op('dve', lambda e: e.tensor_tensor(incp[:], inc[:, 0:16, :], E[:], ALU.add), reads=[b_inc, b_E], writes=[b_incp])
        b_BP = Buf("BP")
        for n in range(16):
            for j in range(n + 1):
                S.op('dve', lambda e, n=n, j=j: e.tensor_tensor(BP[:, n, j, :], E[:, n, :], incp[:, j, :], ALU.subtract),
                     reads=[b_E, b_incp], writes=[b_BP])
        b_BN = Buf("BN")
        S.op('dve', lambda e: e.tensor_scalar(BN[:], inc[:, 16, :], -1.0, None, ALU.mult), reads=[b_inc], writes=[b_BN])
        BC = []
        b_BC = []
        for s_ in range(2):
            lc = _sb(nc, es, f"lc{s_}", [P, 32, 16], F32)
            b_lc = Buf(f"lc{s_}")
            S.dma(lc[:], C.cfl[s_].rearrange("(c p) h -> p c h", p=P), b_lc, writes=[b_lc])
            incc = _sb(nc, es, f"incc{s_}", [P, 32, 16], F32)
            totc = _sb(nc, es, f"totc{s_}", [P, 32, 16], F32)
            b_incc, b_totc = Buf(f"incc{s_}"), Buf(f"totc{s_}")
            lcf = lc[:].rearrange("p c h -> p (c h)")
            S.op('pe', lambda e, lcf=lcf: e.matmul(pcs[:, :512], utri[:], lcf, start=True, stop=True),
                 reads=[b_utri, b_lc, b_tot, b_inc], writes=[b_pcs])
            S.op('dve', lambda e, incc=incc: e.tensor_copy(incc[:].rearrange("p c h -> p (c h)"), pcs[:, :512]),
                 reads=[b_pcs], writes=[b_incc])
            S.op('pe', lambda e, lcf=lcf: e.matmul(pcs[:, :512], ones_f[:], lcf, start=True, stop=True),
                 reads=[b_onesf, b_lc, b_incc], writes=[b_pcs])
            S.op('dve', lambda e, totc=totc: e.tensor_copy(totc[:].rearrange("p c h -> p (c h)"), pcs[:, :512]),
                 reads=[b_pcs], writes=[b_totc])
            EC, b_EC = prefix_chunks(C, es, totc, b_totc, 32, f"c{s_}")
            tt = _sb(nc, es, f"ttot{s_}", [P, 16], F32)
            b_tt = Buf(f"ttot{s_}")
            S.op('dve', lambda e, tt=tt, EC=EC, totc=totc: e.tensor_tensor(tt[:], EC[:, 31, :], totc[:, 31, :], ALU.add),
                 reads=[b_EC, b_totc], writes=[b_tt])
            bc = BCt[s_]
            b_bc = Buf(f"BC{s_}")
            S.op('dve', lambda e, incc=incc, EC=EC: e.tensor_tensor(incc[:], incc[:], EC[:], ALU.add),
                 reads=[b_incc, b_EC], writes=[b_incc])
            for j in range(32):
                S.op('dve', lambda e, bc=bc, tt=tt, incc=incc, j=j: e.tensor_tensor(bc[:, j, :], tt[:], incc[:, j, :], ALU.subtract),
                     reads=[b_tt, b_incc], writes=[b_bc])
            BC.append(bc)
            b_BC.append(b_bc)

        S.end()
        es.close()
        for bb in [b_OT, b_ones, b_onesf, b_ident, b_mkb, b_BP, b_BN] + b_BC:
            bb.last_w, bb.readers, bb.dsem = None, [], None
        R = attn_resources(C, es, ones_bf, b_ones)
        qT = [_sb(nc, es, f"fqT{i}", [P, T], BF16) for i in range(2)]
        kT = [_sb(nc, es, f"fkT{i}", [P, T], BF16) for i in range(2)]
        vt = [_sb(nc, es, f"fvt{i}", [P, NCH, P], BF16) for i in range(2)]
        kc = [_sb(nc, es, f"fkc{i}", [P, 16, P], F32) for i in range(2)]
        vc = [_sb(nc, es, f"fvc{i}", [P, 16, P], F32) for i in range(2)]
        kcT = [_sb(nc, es, f"fkcT{i}", [P, 4096], BF16) for i in range(2)]
        vcb = [_sb(nc, es, f"fvcb{i}", [P, 32, P], BF16) for i in range(2)]
        nm = ['qT', 'kT', 'vt', 'kc', 'vc']
        B = {n_: [Buf(f"f{n_}{i}") for i in range(2)] for n_ in nm}
        b_kcT = [[Buf(f"kcT{i}_{hf}") for hf in range(2)] for i in range(2)]
        b_vcb = [[Buf(f"vcb{i}_{hf}") for hf in range(2)] for i in range(2)]
        ptr = [_ps(nc, es, f"ptrf{i}") for i in range(2)]
        b_ptr = [Buf(f"ptrf{i}") for i in range(2)]
        ktr = 0
        kld = 0
        for h in range(16):
            a = h % 2
            S.dma(qT[a][:], C.QT[h], B['qT'][a], writes=[B['qT'][a]])
            S.dma(kT[a][:], C.KT[h], B['kT'][a], writes=[B['kT'][a]])
            S.dma(vt[a][:], C.VTM[h].rearrange("c p d -> p c d"), B['vt'][a], writes=[B['vt'][a]])
            for n in range(16):
                for j in range(n + 1):
                    def bias_fn(ps, bps, sb_, bsb, pT, bpT, n=n, j=j, h=h):
                        S.op('act', lambda e: e.activation(pT[:], ps[:, :P], AF.Exp, bias=BP[:, n, j, h:h + 1]),
                             reads=[bps, b_BP], writes=[bpT])
                        if j == n:
                            S.op('pool', lambda e: e.tensor_tensor(pT[:], pT[:], mkb[:, 0, :], ALU.mult),
                                 reads=[bpT, b_mkb], writes=[bpT])

                    def out_fn(po, bpo, rc, brc, h=h, n=n):
                        S.op('dve', lambda e: e.tensor_tensor(OT[:, h, n * P:(n + 1) * P], po[:, :P], rc[:, :P], ALU.mult),
                             reads=[bpo, brc], writes=[b_OT])

                    attn_tile(C, kT[a][:, j * P:(j + 1) * P], B['kT'][a], qT[a][:, n * P:(n + 1) * P], B['qT'][a], P,
                              vt[a][:, j, :], B['vt'][a], bias_fn, R, j == 0, j == n, out_fn)
            for s_ in range(2):
                q0 = 2048 + 64 * s_
                for hf in range(2):
                    u = kld % 2
                    kld += 1
                    S.dma(kc[u][:], C.cfk[s_, hf * 2048:(hf + 1) * 2048, h, :].rearrange("(c p) d -> p c d", p=P),
                          B['kc'][u], writes=[B['kc'][u]])
                    S.dma(vc[u][:], C.cfv[s_, hf * 2048:(hf + 1) * 2048, h, :].rearrange("(c p) d -> p c d", p=P),
                          B['vc'][u], writes=[B['vc'][u]])
                    S.op('pool', lambda e, u=u, s_=s_, hf=hf: e.tensor_copy(vcb[s_][:, hf * 16:(hf + 1) * 16, :], vc[u][:]),
                         reads=[B['vc'][u]], writes=[b_vcb[s_][hf]])
                    for c in range(16):
                        pt, bpt = ptr[ktr % 2], b_ptr[ktr % 2]
                        ktr += 1
                        S.op('pe', lambda e, pt=pt, u=u, c=c: e.transpose(pt[:, :P], kc[u][:, c, :], ident[:]),
                             reads=[B['kc'][u], b_ident], writes=[bpt])
                        cg = hf * 16 + c
                        if c % 2 == 0:
                            S.op('act', lambda e, pt=pt, s_=s_, cg=cg: e.copy(kcT[s_][:, cg * P:(cg + 1) * P], pt[:, :P]),
                                 reads=[bpt], writes=[b_kcT[s_][hf]])
                        else:
                            S.op('dve', lambda e, pt=pt, s_=s_, cg=cg: e.tensor_copy(kcT[s_][:, cg * P:(cg + 1) * P], pt[:, :P]),
                                 reads=[bpt], writes=[b_kcT[s_][hf]])
                for j in range(33):
                    def bias_fn(ps, bps, sb_, bsb, pT, bpT, j=j, h=h, s_=s_):
                        if j < 32:
                            S.op('act', lambda e: e.activation(pT[:, :64], ps[:, :64], AF.Exp, bias=BC[s_][:, j, h:h + 1]),
                                 reads=[bps, b_BC[s_]], writes=[bpT])
                        else:
                            S.op('act', lambda e: e.activation(pT[:, :64], ps[:, :64], AF.Exp, bias=BN[:, h:h + 1]),
                                 reads=[bps, b_BN], writes=[bpT])
                            S.op('pool', lambda e: e.tensor_tensor(pT[:, :64], pT[:, :64], mkb[:, 1 + s_, 0:64], ALU.mult),
                                 reads=[bpT, b_mkb], writes=[bpT])

                    def out_fn(po, bpo, rc, brc, h=h, q0=q0):
                        S.op('dve', lambda e: e.tensor_tensor(OT[:, h, q0:q0 + 64], po[:, :64], rc[:, :64], ALU.mult),
                             reads=[bpo, brc], writes=[b_OT])

                    if j < 32:
                        kk, bkk = kcT[s_][:, j * P:(j + 1) * P], b_kcT[s_][j // 16]
                        vv, bvv = vcb[s_][:, j, :], b_vcb[s_][j // 16]
                    else:
                        kk, bkk = kT[a][:, 2048:2176], B['kT'][a]
                        vv, bvv = vt[a][:, 16, :], B['vt'][a]
                    attn_tile(C, kk, bkk, qT[a][:, q0:q0 + 64], B['qT'][a], 64, vv, bvv, bias_fn, R, j == 0, j == 32, out_fn)
        S.end()
        es.close()
        b_OT.last_w, b_OT.readers, b_OT.dsem = None, [], None
        out_proj(C, es, OT, b_OT, C.wo_fox)
        S.end()


import math
PI = math.pi
NCK = 34


def _sincos(C, x, b_x, n, tmp, b_tmp, itmp, s_out, b_s, c_out, b_c):
    S = C.S
    for off, dst, bd in ((0.0, s_out, b_s), (0.25, c_out, b_c)):
        S.op('dve', lambda e, off=off: e.tensor_scalar(tmp, x, 1.0 / (2 * PI), off, ALU.mult, ALU.add), reads=[b_x], writes=[b_tmp])
        S.op('dve', lambda e: e.tensor_copy(itmp, tmp), reads=[b_tmp], writes=[b_tmp])
        S.op('dve', lambda e, dst=dst: e.tensor_copy(dst, itmp), reads=[b_tmp], writes=[bd])
        S.op('dve', lambda e, dst=dst: e.tensor_tensor(tmp, tmp, dst, ALU.subtract), reads=[b_tmp, bd], writes=[b_tmp])
        S.op('dve', lambda e, dst=dst: e.tensor_scalar(dst, tmp, 0.5, None, ALU.is_gt), reads=[b_tmp], writes=[bd])
        S.op('dve', lambda e, dst=dst: e.tensor_tensor(tmp, tmp, dst, ALU.subtract), reads=[b_tmp, bd], writes=[b_tmp])
        S.op('dve', lambda e, dst=dst: e.tensor_scalar(dst, tmp, -0.5, None, ALU.is_lt), reads=[b_tmp], writes=[bd])
        S.op('dve', lambda e, dst=dst: e.tensor_tensor(tmp, tmp, dst, ALU.add), reads=[b_tmp, bd], writes=[b_tmp])
        S.op('act', lambda e, dst=dst: e.activation(dst, tmp, AF.Sin, scale=2 * PI), reads=[b_tmp], writes=[bd])


def phase_ssm_prep(C, j):
    nc, S = C.nc, C.S
    with ExitStack() as es:
        def t64(nm):
            return _sb(nc, es, nm, [P, 64], F32), Buf(nm)
        are, b_are = t64("are")
        aim, b_aim = t64("aim")
        ls, b_ls = t64("ls")
        S.dma(are[:], C.ssmp[j, 0], b_are, writes=[b_are])
        S.dma(aim[:], C.ssmp[j, 1], b_aim, writes=[b_aim])
        S.dma(ls[:], C.ssmp[j, 2], b_ls, writes=[b_ls])
        pos1, b_pos1 = t64("pos1")
        S.dma(pos1[:], C.pos1[:, :], b_pos1, writes=[b_pos1])
        br = _sb(nc, es, "br", [P, 64, 16], F32)
        bi = _sb(nc, es, "bi", [P, 64, 16], F32)
        cr = _sb(nc, es, "cr", [P, 64, 16], F32)
        ci = _sb(nc, es, "ci", [P, 64, 16], F32)
        b_br, b_bi, b_cr, b_ci = Buf("br"), Buf("bi"), Buf("cr"), Buf("ci")
        S.dma(br[:], C.ssmb[j, 0], b_br, writes=[b_br])
        S.dma(bi[:], C.ssmb[j, 1], b_bi, writes=[b_bi])
        S.dma(cr[:], C.ssmc[j, 0], b_cr, writes=[b_cr])
        S.dma(ci[:], C.ssmc[j, 1], b_ci, writes=[b_ci])
        ident = _sb(nc, es, "identp", [P, P], F32)
        b_ident = Buf("identp")
        S.dma(ident[:], C.ident[:, :], b_ident, writes=[b_ident])
        dt, b_dt = t64("dt")
        ar, b_ar = t64("ar")
        th, b_th = t64("th")
        mag, b_mag = t64("mag")
        sn, b_sn = t64("sn")
        cs, b_cs = t64("cs")
        tm_, b_tm = t64("tmq")
        abr, b_abr = t64("abr")
        abi, b_abi = t64("abi")
        den, b_den = t64("den")
        t1, b_t1 = t64("t1")
        t2, b_t2 = t64("t2")
        zr, b_zr = t64("zr")
        zi, b_zi = t64("zi")
        S.op('act', lambda e: e.activation(dt[:], ls[:], AF.Exp), reads=[b_ls], writes=[b_dt])
        S.op('dve', lambda e: e.tensor_tensor(ar[:], are[:], dt[:], ALU.mult), reads=[b_are, b_dt], writes=[b_ar])
        S.op('dve', lambda e: e.tensor_tensor(th[:], aim[:], dt[:], ALU.mult), reads=[b_aim, b_dt], writes=[b_th])
        S.op('act', lambda e: e.activation(mag[:], ar[:], AF.Exp), reads=[b_ar], writes=[b_mag])
        it64 = _sb(nc, es, "it64", [P, 64], mybir.dt.int32)
        _sincos(C, th[:], b_th, 64, tm_[:], b_tm, it64[:], sn[:], b_sn, cs[:], b_cs)
        S.op('dve', lambda e: e.tensor_tensor(abr[:], mag[:], cs[:], ALU.mult), reads=[b_mag, b_cs], writes=[b_abr])
        S.op('dve', lambda e: e.tensor_tensor(abi[:], mag[:], sn[:], ALU.mult), reads=[b_mag, b_sn], writes=[b_abi])
        S.op('dve', lambda e: e.tensor_tensor(den[:], are[:], are[:], ALU.mult), reads=[b_are], writes=[b_den])
        S.op('dve', lambda e: e.tensor_tensor(t1[:], aim[:], aim[:], ALU.mult), reads=[b_aim], writes=[b_t1])
        S.op('dve', lambda e: e.tensor_tensor(den[:], den[:], t1[:], ALU.add), reads=[b_den, b_t1], writes=[b_den])
        S.op('dve', lambda e: e.reciprocal(den[:], den[:]), reads=[b_den], writes=[b_den])
        S.op('dve', lambda e: e.tensor_scalar(abr[:], abr[:], -1.0, None, ALU.add), reads=[b_abr], writes=[b_abr])
        S.op('dve', lambda e: e.tensor_tensor(t1[:], abr[:], are[:], ALU.mult), reads=[b_abr, b_are, b_den], writes=[b_t1])
        S.op('dve', lambda e: e.tensor_tensor(t2[:], abi[:], aim[:], ALU.mult), reads=[b_abi, b_aim], writes=[b_t2])
        S.op('dve', lambda e: e.tensor_tensor(t1[:], t1[:], t2[:], ALU.add), reads=[b_t1, b_t2], writes=[b_t1])
        S.op('dve', lambda e: e.tensor_tensor(zr[:], t1[:], den[:], ALU.mult), reads=[b_t1, b_den], writes=[b_zr])
        S.op('dve', lambda e: e.tensor_tensor(t1[:], abi[:], are[:], ALU.mult), reads=[b_abi, b_are, b_zr], writes=[b_t1])
        S.op('dve', lambda e: e.tensor_tensor(t2[:], abr[:], aim[:], ALU.mult), reads=[b_abr, b_aim], writes=[b_t2])
        S.op('dve', lambda e: e.tensor_tensor(t1[:], t1[:], t2[:], ALU.subtract), reads=[b_t1, b_t2], writes=[b_t1])
        S.op('dve', lambda e: e.tensor_tensor(zi[:], t1[:], den[:], ALU.mult), reads=[b_t1, b_den], writes=[b_zi])
        bbr = _sb(nc, es, "bbr", [P, 64, 16], F32)
        bbi = _sb(nc, es, "bbi", [P, 64, 16], F32)
        tq = _sb(nc, es, "tq", [P, 64, 16], F32)
        b_bbr, b_bbi, b_tq = Buf("bbr"), Buf("bbi"), Buf("tq")
        zrb = zr[:].unsqueeze(2).to_broadcast([P, 64, 16])
        zib = zi[:].unsqueeze(2).to_broadcast([P, 64, 16])
        S.op('dve', lambda e: e.tensor_tensor(bbr[:], br[:], zrb, ALU.mult), reads=[b_br, b_zr], writes=[b_bbr])
        S.op('dve', lambda e: e.tensor_tensor(tq[:], bi[:], zib, ALU.mult), reads=[b_bi, b_zi], writes=[b_tq])
        S.op('dve', lambda e: e.tensor_tensor(bbr[:], bbr[:], tq[:], ALU.subtract), reads=[b_bbr, b_tq], writes=[b_bbr])
        S.op('dve', lambda e: e.tensor_tensor(bbi[:], bi[:], zrb, ALU.mult), reads=[b_bi, b_zr], writes=[b_bbi])
        S.op('dve', lambda e: e.tensor_tensor(tq[:], br[:], zib, ALU.mult), reads=[b_br, b_zi, b_bbr], writes=[b_tq])
        S.op('dve', lambda e: e.tensor_tensor(bbi[:], bbi[:], tq[:], ALU.add), reads=[b_bbi, b_tq], writes=[b_bbi])
        xb = [_sb(nc, es, f"xb{i}", [P, P], F32) for i in range(2)]
        b_xb = [Buf(f"xb{i}") for i in range(2)]
        bt = [_sb(nc, es, f"bt{i}", [P, 2, P], BF16) for i in range(2)]
        b_bt = [Buf(f"bt{i}") for i in range(2)]
        ct = [_sb(nc, es, f"ct{i}", [P, 3, P], BF16) for i in range(2)]
        b_ct = [Buf(f"ct{i}") for i in range(2)]
        ptp = [_ps(nc, es, f"ptp{i}") for i in range(2)]
        b_ptp = [Buf(f"ptp{i}") for i in range(2)]
        q = 0
        for k in range(64):
            r0 = (k % 4) * 32
            btk, bbt_ = bt[k % 2], b_bt[k % 2]
            ctk, bct_ = ct[k % 2], b_ct[k % 2]
            for a_, (src, bsrc) in enumerate(((bbr, b_bbr), (bbi, b_bbi))):
                x_, bx_ = xb[q % 2], b_xb[q % 2]
                pp, bpp = ptp[q % 2], b_ptp[q % 2]
                q += 1
                S.op('pool', lambda e, x_=x_: e.memset(x_[:], 0.0), writes=[bx_])
                S.op('pool', lambda e, x_=x_, src=src, k=k, r0=r0: e.tensor_copy(x_[0:64, r0:r0 + 16], src[0:64, k, :]),
                     reads=[bsrc, bx_], writes=[bx_])
                S.op('pool', lambda e, x_=x_, src=src, k=k, r0=r0: e.tensor_copy(x_[64:128, r0 + 16:r0 + 32], src[64:128, k, :]),
                     reads=[bsrc, bx_], writes=[bx_])
                S.op('pe', lambda e, pp=pp, x_=x_: e.transpose(pp[:, :P], x_[:], ident[:]), reads=[bx_, b_ident], writes=[bpp])
                S.op('act', lambda e, pp=pp, btk=btk, a_=a_: e.copy(btk[:, a_, :], pp[:, :P]), reads=[bpp], writes=[bbt_])
            S.dma(C.BBT[k].rearrange("a r s -> r a s"), btk[:], bbt_, reads=[bbt_])
            S.op('dve', lambda e, ctk=ctk: e.memset(ctk[:], 0.0), writes=[bct_])
            for a_, (src, bsrc, sc) in enumerate(((cr, b_cr, 1.0), (cr, b_cr, -1.0), (ci, b_ci, -1.0))):
                for gl in range(2):
                    S.op('dve', lambda e, ctk=ctk, a_=a_, src=src, sc=sc, gl=gl, k=k, r0=r0: e.tensor_scalar(
                        ctk[gl * 64:(gl + 1) * 64, a_, r0 + gl * 16:r0 + gl * 16 + 16], src[gl * 64:(gl + 1) * 64, k, :],
                        sc, None, ALU.mult), reads=[bsrc, bct_], writes=[bct_])
            S.dma(C.CTS[k].rearrange("a r s -> r a s"), ctk[:], bct_, reads=[bct_])
        def big(nm):
            return _sb(nc, es, nm, [P, 64, 64], F32), Buf(nm)
        ang, b_ang = big("ang")
        ea, b_ea = big("ea")
        sN, b_sN = big("sN")
        cN, b_cN = big("cN")
        tb, b_tb = big("tbig")
        tab = _sb(nc, es, "tab", [P, 64, 4, 64], F32)
        b_tab = Buf("tab")
        for k in range(64):
            S.op('dve', lambda e, k=k: e.tensor_scalar(ang[:, k, :], pos1[:], th[:, k:k + 1], None, ALU.mult),
                 reads=[b_pos1, b_th], writes=[b_ang])
            S.op('pool', lambda e, k=k: e.tensor_scalar(ea[:, k, :], pos1[:], ar[:, k:k + 1], None, ALU.mult),
                 reads=[b_pos1, b_ar], writes=[b_ea])
        fl = lambda t_: t_[:].rearrange("p k q -> p (k q)")
        itb = _sb(nc, es, "itb", [P, 4096], mybir.dt.int32)
        _sincos(C, fl(ang), b_ang, 4096, fl(tb), b_tb, itb[:], fl(sN), b_sN, fl(cN), b_cN)
        S.op('act', lambda e: e.activation(fl(tb), fl(ea), AF.Exp), reads=[b_ea, b_sN, b_cN], writes=[b_tb])
        S.op('dve', lambda e: e.tensor_tensor(tab[:, :, 0, :], tb[:], cN[:], ALU.mult), reads=[b_tb, b_cN], writes=[b_tab])
        S.op('dve', lambda e: e.tensor_tensor(tab[:, :, 1, :], tb[:], sN[:], ALU.mult), reads=[b_tb, b_sN], writes=[b_tab])
        S.op('act', lambda e: e.activation(fl(tb), fl(ea), AF.Exp, scale=-1.0), reads=[b_ea, b_tab], writes=[b_tb])
        S.op('dve', lambda e: e.tensor_tensor(tab[:, :, 2, :], tb[:], cN[:], ALU.mult), reads=[b_tb, b_cN], writes=[b_tab])
        S.op('dve', lambda e: e.scalar_tensor_tensor(tab[:, :, 3, :], tb[:], -1.0, sN[:], ALU.mult, ALU.mult),
             reads=[b_tb, b_sN], writes=[b_tab])
        S.dma(C.TAB[:, :, :, :], tab[:], b_tab, reads=[b_tab])
        ap_ = _sb(nc, es, "apow", [P, 64, 10], F32)
        b_ap = Buf("apow")
        S.op('dve', lambda e: e.tensor_copy(ap_[:, :, 0], tab[:, :, 0, 63]), reads=[b_tab], writes=[b_ap])
        S.op('dve', lambda e: e.tensor_copy(ap_[:, :, 1], tab[:, :, 1, 63]), reads=[b_tab, b_ap], writes=[b_ap])
        for i in range(1, 5):
            pr, pi_ = ap_[:, :, 2 * i - 2], ap_[:, :, 2 * i - 1]
            S.op('dve', lambda e, pr=pr: e.tensor_tensor(t1[:], pr, pr, ALU.mult), reads=[b_ap, b_zi], writes=[b_t1])
            S.op('dve', lambda e, pi_=pi_: e.tensor_tensor(t2[:], pi_, pi_, ALU.mult), reads=[b_ap], writes=[b_t2])
            S.op('dve', lambda e, i=i: e.tensor_tensor(ap_[:, :, 2 * i], t1[:], t2[:], ALU.subtract), reads=[b_t1, b_t2, b_ap], writes=[b_ap])
            S.op('dve', lambda e, pr=pr, pi_=pi_, i=i: e.scalar_tensor_tensor(ap_[:, :, 2 * i + 1], pr, 2.0, pi_, ALU.mult, ALU.mult),
                 reads=[b_ap], writes=[b_ap])
        S.dma(C.APOW[:, :, :], ap_[:], b_ap, reads=[b_ap])
        S.end()


def phase_ssm_scan(C, j, ni):
    nc, S = C.nc, C.S
    with ExitStack() as es:
        rstd = _sb(nc, es, "rstds", [P, T], F32)
        b_rstd = [Buf(f"rstds{t}") for t in range(5)]
        gv = _sb(nc, es, "gvs", [P, DC], F32)
        b_gv = Buf("gvs")
        ones_f = _sb(nc, es, "ones_fs", [P, P], F32)
        ones_bf = _sb(nc, es, "ones_bfs", [P, P], BF16)
        b_onesf, b_ones = Buf("onesfs"), Buf("oness")
        sq = [_sb(nc, es, f"sqs{i}", [P, 512], BF16) for i in range(2)]
        b_sq = [Buf("sqs0"), Buf("sqs1")]
        xs2 = [_sb(nc, es, f"xs2{i}", [P, 512], F32) for i in range(2)]
        b_xs2 = [Buf("xs20"), Buf("xs21")]
        tmpq = _sb(nc, es, "tmps", [P, 512], F32)
        b_tmpq = Buf("tmps")
        pss = _ps(nc, es, "psss")
        b_pss = Buf("psss")
        S.dma(gv[:], C.nrm[ni], b_gv, writes=[b_gv])
        S.dma(ones_f[:], C.ones[:, :], b_onesf, writes=[b_onesf])
        S.op('dve', lambda e: e.tensor_copy(ones_bf[:], ones_f[:]), reads=[b_onesf], writes=[b_ones])
        q_ = 0
        for ti, (ts, w) in enumerate(TILES_ALL):
            for dc in range(DC):
                x_, bx_ = xs2[q_ % 2], b_xs2[q_ % 2]
                s_, bs_ = sq[q_ % 2], b_sq[q_ % 2]
                q_ += 1
                S.dma(x_[:, :w], C.XT[dc, :, ts:ts + w], bx_, writes=[bx_])
                S.op('act', lambda e, s_=s_, x_=x_, w=w: e.activation(s_[:, :w], x_[:, :w], AF.Square), reads=[bx_], writes=[bs_])
                S.op('pe', lambda e, s_=s_, dc=dc, w=w: e.matmul(pss[:, :w], ones_bf[:], s_[:, :w], start=(dc == 0), stop=(dc == DC - 1)),
                     reads=[bs_, b_ones], writes=[b_pss])
            S.op('dve', lambda e, w=w: e.tensor_scalar(tmpq[:, :w], pss[:, :w], 1.0 / D, EPS, ALU.mult, ALU.add), reads=[b_pss], writes=[b_tmpq])
            S.op('act', lambda e, w=w: e.activation(tmpq[:, :w], tmpq[:, :w], AF.Sqrt), reads=[b_tmpq], writes=[b_tmpq])
            S.op('dve', lambda e, ts=ts, w=w: e.reciprocal(rstd[:, ts:ts + w], tmpq[:, :w]), reads=[b_tmpq], writes=[b_rstd[ti]])
        xdc = [_sb(nc, es, f"xdc{i}", [P, T], F32) for i in range(2)]
        b_xdc = [Buf("xdc0"), Buf("xdc1")]
        xnd = [_sb(nc, es, f"xnd{i}", [P, T], BF16) for i in range(2)]
        b_xnd = [Buf("xnd0"), Buf("xnd1")]

        def do_dc(dc):
            if dc >= DC:
                return
            u = dc % 2
            S.dma(xdc[u][:], C.XT[dc, :, :], b_xdc[u], writes=[b_xdc[u]])
            S.op('dve', lambda e: e.scalar_tensor_tensor(xnd[u][:], xdc[u][:], gv[:, dc:dc + 1], rstd[:], ALU.mult, ALU.mult),
                 reads=[b_xdc[u], b_gv] + b_rstd, writes=[b_xnd[u]])

        do_dc(0)
        ident = _sb(nc, es, "idents", [P, P], F32)
        b_ident = Buf("idents")
        S.dma(ident[:], C.ident[:, :], b_ident, writes=[b_ident])
        rmask = _sb(nc, es, "rmask", [P, T], F32)
        b_rmask = Buf("rmask")
        S.dma(rmask[:], C.rmask[:, :], b_rmask, writes=[b_rmask])
        apw = _sb(nc, es, "apw", [P, 64, 10], F32)
        b_apw = Buf("apw")
        S.dma(apw[:], C.APOW[:, :, :], b_apw, writes=[b_apw])
        st0 = _sb(nc, es, "st0", [P, 2, 2, 64], F32)
        b_st0 = Buf("st0")
        S.dma(st0[:], C.ssms[j].rearrange("s a p k -> p s a k"), b_st0, writes=[b_st0])
        dv = _sb(nc, es, "dvs", [P, DC], F32)
        b_dv = Buf("dvs")
        S.dma(dv[:], C.ssmd[j], b_dv, writes=[b_dv])
        fin = _sb(nc, es, "fin", [P, 6, 64], F32)
        b_fin = Buf("fin")
        btk = [_sb(nc, es, f"sbt{i}", [P, 2, P], BF16) for i in range(2)]
        ctk = [_sb(nc, es, f"sct{i}", [P, 3, P], BF16) for i in range(2)]
        tbk = [_sb(nc, es, f"stb{i}", [P, 4, 64], F32) for i in range(2)]
        b_btk = [Buf(f"sbt{i}") for i in range(2)]
        b_ctk = [Buf(f"sct{i}") for i in range(2)]
        b_tbk = [Buf(f"stb{i}") for i in range(2)]
        burs = [_sb(nc, es, f"bur{i}", [P, T], F32) for i in range(2)]
        buis = [_sb(nc, es, f"bui{i}", [P, T], F32) for i in range(2)]
        b_burs = [[Buf(f"bur{i}_{t}") for t in range(5)] for i in range(2)]
        b_buis = [[Buf(f"bui{i}_{t}") for t in range(5)] for i in range(2)]
        wrs = [_sb(nc, es, f"wr{i}", [P, T], F32) for i in range(2)]
        wis = [_sb(nc, es, f"wi{i}", [P, T], F32) for i in range(2)]
        b_wrs = [Buf("wr0"), Buf("wr1")]
        b_wis = [Buf("wi0"), Buf("wi1")]
        m1 = _sb(nc, es, "m1", [P, T], F32)
        m2 = _sb(nc, es, "m2", [P, T], F32)
        m3 = _sb(nc, es, "m3", [P, T], F32)
        m4 = _sb(nc, es, "m4", [P, T], F32)
        b_m1, b_m2, b_m3, b_m4 = Buf("m1"), Buf("m2"), Buf("m3"), Buf("m4")
        pr_ = [_sb(nc, es, f"prd{i}", [P, T], BF16) for i in range(4)]
        b_pr = [Buf(f"prd{i}") for i in range(4)]
        sm = _sb(nc, es, "sm", [P, 12, NCK], F32)
        b_smr = [Buf(f"sm{i}") for i in range(12)]
        psb = [_ps(nc, es, f"psb{i}") for i in range(2)]
        b_psb = [Buf(f"psb{i}") for i in range(2)]
        psy = [_ps(nc, es, f"psy{i}") for i in range(5)]
        b_psy = [Buf(f"psy{i}") for i in range(5)]
        xg = _sb(nc, es, "xg", [P, 512], F32)
        hx = _sb(nc, es, "hxg", [P, 512], F32)
        x2 = _sb(nc, es, "x2g", [P, 512], F32)
        b_xg, b_hx, b_x2 = Buf("xg"), Buf("hxg"), Buf("x2g")
        gb = [_sb(nc, es, f"gb{i}", [P, T], BF16) for i in range(2)]
        b_gb = [Buf(f"gb{i}") for i in range(2)]

        def v3(t_, lo, w):
            return t_[:, lo:lo + w].rearrange("p (n q) -> p n q", q=64)

        def stage_a(k):
            dc = k // 4
            a = k % 2
            if k % 4 == 1:
                do_dc(dc + 1)
            xn_d, b_xn_d = xnd[dc % 2], b_xnd[dc % 2]
            bur, bui, b_bur, b_bui = burs[a], buis[a], b_burs[a], b_buis[a]
            wr, wi, b_wr, b_wi = wrs[a], wis[a], b_wrs[a], b_wis[a]
            S.dma(btk[a][:], C.BBT[k].rearrange("a r s -> r a s"), b_btk[a], writes=[b_btk[a]])
            S.dma(ctk[a][:], C.CTS[k].rearrange("a r s -> r a s"), b_ctk[a], writes=[b_ctk[a]])
            S.dma(tbk[a][:], C.TAB[:, k, :, :], b_tbk[a], writes=[b_tbk[a]])
            tpr, tpi, tnr, tni = (tbk[a][:, i, :] for i in range(4))
            for ti, (ts, w) in enumerate(TILES_ALL):
                for a_, (dst, bdst) in enumerate(((bur, b_bur), (bui, b_bui))):
                    pb, bpb = psb[a_], b_psb[a_]
                    S.op('pe', lambda e, pb=pb, a_=a_, ts=ts, w=w, a=a, dc=dc: e.matmul(
                        pb[:, :w], btk[a][:, a_, :], xn_d[:, ts:ts + w], start=True, stop=True),
                        reads=[b_btk[a], b_xn_d], writes=[bpb])
                    S.op('act', lambda e, pb=pb, dst=dst, ts=ts, w=w: e.copy(dst[:, ts:ts + w], pb[:, :w]),
                         reads=[bpb], writes=[bdst[ti]])
            bc = lambda t_: t_.unsqueeze(1).to_broadcast([P, NCK, 64])
            A3 = lambda t_: t_[:].rearrange("p (n q) -> p n q", q=64)
            S.op('pool', lambda e: e.tensor_tensor(A3(m1), A3(bur), bc(tnr), ALU.mult), reads=b_bur + [b_tbk[a]], writes=[b_m1])
            S.op('pool', lambda e: e.tensor_tensor(A3(m2), A3(bui), bc(tni), ALU.mult), reads=b_bui + [b_tbk[a]], writes=[b_m2])
            S.op('pool', lambda e: e.tensor_tensor(A3(m3), A3(bur), bc(tni), ALU.mult), reads=b_bur + [b_tbk[a]], writes=[b_m3])
            S.op('pool', lambda e: e.tensor_tensor(A3(m4), A3(bui), bc(tnr), ALU.mult), reads=b_bui + [b_tbk[a]], writes=[b_m4])
            S.op('dve', lambda e: e.tensor_tensor(wr[:], m1[:], m2[:], ALU.subtract), reads=[b_m1, b_m2], writes=[b_wr])
            S.op('dve', lambda e: e.tensor_tensor(wi[:], m3[:], m4[:], ALU.add), reads=[b_m3, b_m4], writes=[b_wi])

        def stage_b1(k):
            dc = k // 4
            a = k % 2
            xn_d, b_xn_d = xnd[dc % 2], b_xnd[dc % 2]
            bur, bui, b_bur, b_bui = burs[a], buis[a], b_burs[a], b_buis[a]
            wr, wi, b_wr, b_wi = wrs[a], wis[a], b_wrs[a], b_wis[a]
            tpr, tpi, tnr, tni = (tbk[a][:, i, :] for i in range(4))

            bc = lambda t_: t_.unsqueeze(1).to_broadcast([P, NCK, 64])
            A3 = lambda t_: t_[:].rearrange("p (n q) -> p n q", q=64)
            S.op('dve', lambda e: e.tensor_tensor_scan(wr[:], rmask[:], wr[:], 0.0, ALU.mult, ALU.add), reads=[b_rmask, b_wr], writes=[b_wr])
            S.op('dve', lambda e: e.tensor_tensor_scan(wi[:], rmask[:], wi[:], 0.0, ALU.mult, ALU.add), reads=[b_rmask, b_wi], writes=[b_wi])
            er = wr[:, 63:T:64]
            ei = wi[:, 63:T:64]
            Ar, Ai = apw[:, k, 0:1], apw[:, k, 1:2]
            X = lambda i, lo=0, hi=NCK: sm[:, i, lo:hi]
            R_ = lambda *rows: [b_smr[i] for i in rows]
            S.op('dve', lambda e: e.tensor_scalar(X(2), ei, Ai, None, ALU.mult), reads=[b_wi, b_apw], writes=R_(2))
            S.op('dve', lambda e: e.tensor_scalar(X(3), ei, Ar, None, ALU.mult), reads=[b_wi, b_apw], writes=R_(3))
            S.op('dve', lambda e: e.scalar_tensor_tensor(X(0), er, Ar, X(2), ALU.mult, ALU.subtract), reads=[b_wr, b_apw] + R_(2), writes=R_(0))
            S.op('dve', lambda e: e.scalar_tensor_tensor(X(1), er, Ai, X(3), ALU.mult, ALU.add), reads=[b_wr, b_apw] + R_(3), writes=R_(1))
            cur = (0, 1)
            oth = (2, 3)
            sft = 1
            n = 32
            for i in range(5):
                Pr, Pi = apw[:, k, 2 * i:2 * i + 1], apw[:, k, 2 * i + 1:2 * i + 2]
                cr_, ci_ = cur
                or_, oi_ = oth
                S.op('dve', lambda e, cr_=cr_, or_=or_, sft=sft, Pr=Pr: e.scalar_tensor_tensor(
                    X(or_, sft, n), X(cr_, 0, n - sft), Pr, X(cr_, sft, n), ALU.mult, ALU.add), reads=R_(cr_) + [b_apw], writes=R_(or_))
                S.op('dve', lambda e, ci_=ci_, sft=sft, Pi=Pi: e.tensor_scalar(X(4, 0, n - sft), X(ci_, 0, n - sft), Pi, None, ALU.mult),
                     reads=R_(ci_) + [b_apw], writes=R_(4))
                S.op('dve', lambda e, ci_=ci_, oi_=oi_, sft=sft, Pr=Pr: e.scalar_tensor_tensor(
                    X(oi_, sft, n), X(ci_, 0, n - sft), Pr, X(ci_, sft, n), ALU.mult, ALU.add), reads=R_(ci_) + [b_apw], writes=R_(oi_))
                S.op('dve', lambda e, cr_=cr_, or_=or_, sft=sft: e.tensor_copy(X(or_, 0, sft), X(cr_, 0, sft)), reads=R_(cr_), writes=R_(or_))
                S.op('dve', lambda e, ci_=ci_, oi_=oi_, sft=sft: e.tensor_copy(X(oi_, 0, sft), X(ci_, 0, sft)), reads=R_(ci_), writes=R_(oi_))
                S.op('dve', lambda e, or_=or_, sft=sft: e.tensor_tensor(X(or_, sft, n), X(or_, sft, n), X(4, 0, n - sft), ALU.subtract),
                     reads=R_(or_, 4), writes=R_(or_))
                S.op('dve', lambda e, cr_=cr_, oi_=oi_, sft=sft, Pi=Pi: e.scalar_tensor_tensor(
                    X(oi_, sft, n), X(cr_, 0, n - sft), Pi, X(oi_, sft, n), ALU.mult, ALU.add), reads=R_(cr_, oi_) + [b_apw], writes=R_(oi_))
                cur, oth = oth, cur
                sft *= 2
            sr, si = cur
            S.op('dve', lambda e: e.memset(sm[:, 6:8, 0:1], 0.0), writes=R_(6, 7))
            S.op('dve', lambda e, k=k: e.tensor_copy(sm[:, 6, 32:34], st0[:, :, 0, k]), reads=[b_st0], writes=R_(6))
            S.op('dve', lambda e, k=k: e.tensor_copy(sm[:, 7, 32:34], st0[:, :, 1, k]), reads=[b_st0], writes=R_(7))
            S.op('dve', lambda e, sr=sr: e.tensor_copy(X(6, 1, 32), X(sr, 0, 31)), reads=R_(sr), writes=R_(6))
            S.op('dve', lambda e, si=si: e.tensor_copy(X(7, 1, 32), X(si, 0, 31)), reads=R_(si), writes=R_(7))
            S.op('dve', lambda e, sr=sr, k=k: e.tensor_copy(fin[:, 0, k:k + 1], X(sr, 31, 32)), reads=R_(sr), writes=[b_fin])
            S.op('dve', lambda e, si=si, k=k: e.tensor_copy(fin[:, 1, k:k + 1], X(si, 31, 32)), reads=R_(si), writes=[b_fin])
            S.op('dve', lambda e: e.tensor_tensor(X(8, 32, 34), X(6, 32, 34), er[:, 32:34], ALU.add), reads=R_(6) + [b_wr], writes=R_(8))
            S.op('dve', lambda e: e.tensor_tensor(X(9, 32, 34), X(7, 32, 34), ei[:, 32:34], ALU.add), reads=R_(7) + [b_wi], writes=R_(9))
            S.op('dve', lambda e: e.tensor_scalar(X(10, 32, 34), X(9, 32, 34), Ai, None, ALU.mult), reads=R_(9) + [b_apw], writes=R_(10))
            S.op('dve', lambda e: e.tensor_scalar(X(5, 32, 34), X(9, 32, 34), Ar, None, ALU.mult), reads=R_(9) + [b_apw], writes=R_(5))
            S.op('dve', lambda e: e.scalar_tensor_tensor(X(11, 32, 34), X(8, 32, 34), Ar, X(10, 32, 34), ALU.mult, ALU.subtract),
                 reads=R_(8, 10) + [b_apw], writes=R_(11))
            S.op('dve', lambda e: e.scalar_tensor_tensor(X(4, 32, 34), X(8, 32, 34), Ai, X(5, 32, 34), ALU.mult, ALU.add),
                 reads=R_(8, 5) + [b_apw], writes=R_(4))
            S.op('dve', lambda e, k=k: e.tensor_copy(fin[:, 2:6:2, k], X(11, 32, 34)), reads=R_(11), writes=[b_fin])
            S.op('dve', lambda e, k=k: e.tensor_copy(fin[:, 3:6:2, k], X(4, 32, 34)), reads=R_(4), writes=[b_fin])
            cb = lambda i: sm[:, i, :].unsqueeze(2).to_broadcast([P, NCK, 64])
            S.op('dve', lambda e: e.tensor_tensor(A3(wr), A3(wr), cb(6), ALU.add), reads=[b_wr, b_smr[6]], writes=[b_wr])
            S.op('dve', lambda e: e.tensor_tensor(A3(wi), A3(wi), cb(7), ALU.add), reads=[b_wi, b_smr[7]], writes=[b_wi])

        def stage_b2(k):
            dc = k // 4
            a = k % 2
            xn_d, b_xn_d = xnd[dc % 2], b_xnd[dc % 2]
            bur, bui, b_bur, b_bui = burs[a], buis[a], b_burs[a], b_buis[a]
            wr, wi, b_wr, b_wi = wrs[a], wis[a], b_wrs[a], b_wis[a]
            tpr, tpi, tnr, tni = (tbk[a][:, i, :] for i in range(4))

            bc = lambda t_: t_.unsqueeze(1).to_broadcast([P, NCK, 64])
            A3 = lambda t_: t_[:].rearrange("p (n q) -> p n q", q=64)
            S.op('pool', lambda e: e.tensor_tensor(A3(pr_[0]), A3(wr), bc(tpr), ALU.mult), reads=[b_wr, b_tbk[a]], writes=[b_pr[0]])
            S.op('dve', lambda e: e.tensor_tensor(A3(pr_[1]), A3(wi), bc(tpi), ALU.mult), reads=[b_wi, b_tbk[a]], writes=[b_pr[1]])
            S.op('pool', lambda e: e.tensor_tensor(A3(pr_[2]), A3(wi), bc(tpr), ALU.mult), reads=[b_wi, b_tbk[a]], writes=[b_pr[2]])
            S.op('dve', lambda e: e.tensor_tensor(A3(pr_[3]), A3(wr), bc(tpi), ALU.mult), reads=[b_wr, b_tbk[a]], writes=[b_pr[3]])
            cmap = (0, 1, 2, 2)
            for ti, (ts, w) in enumerate(TILES_ALL):
                for q in range(4):
                    S.op('pe', lambda e, ti=ti, ts=ts, w=w, q=q, a=a, k=k: e.matmul(
                        psy[ti][:, :w], ctk[a][:, cmap[q], :], pr_[q][:, ts:ts + w],
                        start=(k % 4 == 0 and q == 0), stop=(k % 4 == 3 and q == 3)),
                        reads=[b_ctk[a], b_pr[q]], writes=[b_psy[ti]])
            if k % 4 != 3:
                return
            g_, bg_ = gb[dc % 2], b_gb[dc % 2]
            for ti, (ts, w) in enumerate(TILES_ALL):
                S.op('dve', lambda e, ti=ti, ts=ts, w=w, dc=dc: e.scalar_tensor_tensor(
                    xg[:, :w], xn_d[:, ts:ts + w], dv[:, dc:dc + 1], psy[ti][:, :w], ALU.mult, ALU.add),
                    reads=[b_xn_d, b_dv, b_psy[ti]], writes=[b_xg])
                S.op('act', lambda e, w=w: e.activation(x2[:, :w], xg[:, :w], AF.Square), reads=[b_xg], writes=[b_x2])
                S.op('act', lambda e, w=w: e.mul(hx[:, :w], xg[:, :w], 0.5), reads=[b_xg], writes=[b_hx])
                S.op('dve', lambda e, w=w: e.tensor_scalar(x2[:, :w], x2[:, :w], 0.044715, 1.0, ALU.mult, ALU.add), reads=[b_x2], writes=[b_x2])
                S.op('dve', lambda e, w=w: e.tensor_tensor(x2[:, :w], x2[:, :w], xg[:, :w], ALU.mult), reads=[b_x2, b_xg], writes=[b_x2])
                S.op('act', lambda e, w=w: e.activation(x2[:, :w], x2[:, :w], AF.Tanh, scale=0.7978845608028654), reads=[b_x2], writes=[b_x2])
                S.op('dve', lambda e, g_=g_, ts=ts, w=w: e.scalar_tensor_tensor(
                    g_[:, ts:ts + w], x2[:, :w], 1.0, hx[:, :w], ALU.add, ALU.mult), reads=[b_x2, b_hx], writes=[bg_])
            S.dma(C.GT[dc], g_[:], bg_, reads=[bg_])


        stage_a(0)
        stage_b1(0)
        for k in range(64):
            if k + 1 < 64:
                stage_a(k + 1)
            stage_b2(k)
            if k + 1 < 64:
                stage_b1(k + 1)
        fo = _sb(nc, es, "fo", [P, 3, P], F32)
        b_fo = Buf("fo")
        for i in range(3):
            S.op('pe', lambda e, i=i: e.transpose(psb[i % 2][:, :P], fin[:, 2 * i:2 * i + 2, :].rearrange("p a k -> p (a k)"), ident[:]),
                 reads=[b_fin, b_ident], writes=[b_psb[i % 2]])
            S.op('dve', lambda e, i=i: e.tensor_copy(fo[:, i, :], psb[i % 2][:, :P]), reads=[b_psb[i % 2]], writes=[b_fo])
        S.dma(C.ssm_out[j].rearrange("i a k p -> (a k) i p"), fo[:], b_fo, reads=[b_fo])
        S.end()


def phase_ssm_glu(C, j):
    nc, S = C.nc, C.S
    with ExitStack() as es:
        g = _sb(nc, es, "gall", [P, DC, T], BF16)
        b_g = Buf("gall")
        S.dma(g[:], C.GT[:, :, :].rearrange("c p t -> p c t"), b_g, writes=[b_g])
        ws = WStream(C, es, [C.wglu[j, i // 2, i % 2] for i in range(32)], nstg=3, nwb=4, tag="wg")
        xr = [_sb(nc, es, f"xrg{i}", [P, 512], F32) for i in range(2)]
        b_xr = [Buf(f"xrg{i}") for i in range(2)]
        sgm = [_sb(nc, es, f"sgm{i}", [P, 512], F32) for i in range(2)]
        b_sgm = [Buf(f"sgm{i}") for i in range(2)]
        psv = [_ps(nc, es, f"psv{i}") for i in range(2)]
        psg = [_ps(nc, es, f"psgg{i}") for i in range(2)]
        b_psv = [Buf(f"psv{i}") for i in range(2)]
        b_psg = [Buf(f"psgg{i}") for i in range(2)]
        kk = 0
        for jc in range(DC):
            ws.fetch_upto(2 * jc + 4)
            wv, bwv = ws.wb[(2 * jc) % 4], ws.b_wb[(2 * jc) % 4]
            wg, bwg = ws.wb[(2 * jc + 1) % 4], ws.b_wb[(2 * jc + 1) % 4]
            for ti, (ts, w) in enumerate(TILES_ALL):
                pv, bpv, pg, bpg = psv[kk % 2], b_psv[kk % 2], psg[kk % 2], b_psg[kk % 2]
                x, bx = xr[kk % 2], b_xr[kk % 2]
                sg_, bsg_ = sgm[kk % 2], b_sgm[kk % 2]
                kk += 1
                S.dma(x[:, :w], C.XT[jc, :, ts:ts + w], bx, writes=[bx])
                for dc in range(DC):
                    S.op('pe', lambda e, pv=pv, wv=wv, dc=dc, ts=ts, w=w: e.matmul(
                        pv[:, :w], wv[:, dc * P:(dc + 1) * P], g[:, dc, ts:ts + w], start=(dc == 0), stop=(dc == DC - 1)),
                        reads=[bwv, b_g], writes=[bpv])
                for dc in range(DC):
                    S.op('pe', lambda e, pg=pg, wg=wg, dc=dc, ts=ts, w=w: e.matmul(
                        pg[:, :w], wg[:, dc * P:(dc + 1) * P], g[:, dc, ts:ts + w], start=(dc == 0), stop=(dc == DC - 1)),
                        reads=[bwg, b_g], writes=[bpg])
                S.op('act', lambda e, sg_=sg_, pg=pg, w=w: e.activation(sg_[:, :w], pg[:, :w], AF.Sigmoid), reads=[bpg], writes=[bsg_])
                S.op('dve', lambda e, sg_=sg_, pv=pv, w=w: e.tensor_tensor(sg_[:, :w], sg_[:, :w], pv[:, :w], ALU.mult),
                     reads=[bsg_, bpv], writes=[bsg_])
                S.op('dve', lambda e, sg_=sg_, x=x, w=w: e.tensor_tensor(x[:, :w], x[:, :w], sg_[:, :w], ALU.add),
                     reads=[bsg_, bx], writes=[bx])
                S.dma(C.XT[jc, :, ts:ts + w], x[:, :w], bx, reads=[bx])
        S.end()


def phase_band_attn(C):
    nc, S = C.nc, C.S
    with ExitStack() as es:
        OT = _sb(nc, es, "OT", [P, 16, T], BF16)
        b_OT = Buf("OT")
        ones_f = _sb(nc, es, "ones_fb", [P, P], F32)
        ones_bf = _sb(nc, es, "ones_bfb", [P, P], BF16)
        ident = _sb(nc, es, "identb", [P, P], F32)
        b_onesf, b_ones, b_ident = Buf("onesf"), Buf("ones"), Buf("ident")
        S.dma(ones_f[:], C.ones[:, :], b_onesf, writes=[b_onesf])
        S.dma(ident[:], C.ident[:, :], b_ident, writes=[b_ident])
        S.op('dve', lambda e: e.tensor_copy(ones_bf[:], ones_f[:]), reads=[b_onesf], writes=[b_ones])
        R = attn_resources(C, es, ones_bf, b_ones)
        qT = [_sb(nc, es, f"qT{i}", [P, T], BF16) for i in range(2)]
        kT = [_sb(nc, es, f"kT{i}", [P, T], BF16) for i in range(2)]
        vt = [_sb(nc, es, f"vt{i}", [P, NCH, P], BF16) for i in range(2)]
        bp = [_sb(nc, es, f"bp{i}", [P, 5, P], F32) for i in range(2)]
        bs = [_sb(nc, es, f"bs{i}", [P, 2, 5, 64], F32) for i in range(2)]
        kc = [_sb(nc, es, f"kc{i}", [P, 2, 4, P], F32) for i in range(2)]
        vc = [_sb(nc, es, f"vc{i}", [P, 2, 4, P], F32) for i in range(2)]
        kcT = [_sb(nc, es, f"kcT{i}", [P, 2, 512], BF16) for i in range(2)]
        vcb = [_sb(nc, es, f"vcb{i}", [P, 2, 4, P], BF16) for i in range(2)]
        nm = ['qT', 'kT', 'vt', 'bp', 'bs', 'kc', 'vc', 'kcT', 'vcb']
        B = {n: [Buf(f"{n}{i}") for i in range(2)] for n in nm}
        ptr = [_ps(nc, es, f"ptrb{i}") for i in range(2)]
        b_ptr = [Buf(f"ptrb{i}") for i in range(2)]
        ktr = 0
        for h in range(16):
            a = h % 2
            S.dma(qT[a][:], C.QT[h], B['qT'][a], writes=[B['qT'][a]])
            S.dma(kT[a][:], C.KT[h], B['kT'][a], writes=[B['kT'][a]])
            S.dma(vt[a][:], C.VTM[h].rearrange("c p d -> p c d"), B['vt'][a], writes=[B['vt'][a]])
            S.dma(bp[a][:], C.bbias_p[h].rearrange("t k q -> k t q"), B['bp'][a], writes=[B['bp'][a]])
            S.dma(bs[a][:], C.bbias_s[h].rearrange("s t k q -> k s t q"), B['bs'][a], writes=[B['bs'][a]])
            S.dma(kc[a][:], C.cbk[:, :, h, :].rearrange("s (c p) d -> p s c d", p=P), B['kc'][a], writes=[B['kc'][a]])
            S.dma(vc[a][:], C.cbv[:, :, h, :].rearrange("s (c p) d -> p s c d", p=P), B['vc'][a], writes=[B['vc'][a]])
            S.op('pool', lambda e, a=a: e.tensor_copy(vcb[a][:], vc[a][:]), reads=[B['vc'][a]], writes=[B['vcb'][a]])
            for s_ in range(2):
                for c in range(4):
                    pt, bpt = ptr[ktr % 2], b_ptr[ktr % 2]
                    ktr += 1
                    S.op('pe', lambda e, pt=pt, a=a, s_=s_, c=c: e.transpose(pt[:, :P], kc[a][:, s_, c, :], ident[:]),
                         reads=[B['kc'][a], b_ident], writes=[bpt])
                    S.op('act', lambda e, pt=pt, a=a, s_=s_, c=c: e.copy(kcT[a][:, s_, c * P:(c + 1) * P], pt[:, :P]),
                         reads=[bpt], writes=[B['kcT'][a]])
            for m in range(16):
                tl = [t for t in range(5) if m - 4 + t >= 0]
                for t in tl:
                    kt_ = m - 4 + t

                    def bias_fn(ps, bps, sb_, bsb, pT, bpT, a=a, t=t):
                        S.op('dve', lambda e: e.tensor_tensor(sb_[:], ps[:, :P], bp[a][:, t, :], ALU.add),
                             reads=[bps, B['bp'][a]], writes=[bsb])
                        S.op('act', lambda e: e.activation(pT[:], sb_[:], AF.Exp), reads=[bsb], writes=[bpT])

                    def out_fn(po, bpo, rc, brc, h=h, m=m):
                        S.op('dve', lambda e: e.tensor_tensor(OT[:, h, m * P:(m + 1) * P], po[:, :P], rc[:, :P], ALU.mult),
                             reads=[bpo, brc], writes=[b_OT])

                    attn_tile(C, kT[a][:, kt_ * P:(kt_ + 1) * P], B['kT'][a], qT[a][:, m * P:(m + 1) * P], B['qT'][a], P,
                              vt[a][:, kt_, :], B['vt'][a], bias_fn, R, t == tl[0], t == 4, out_fn)
            for s_ in range(2):
                q0 = 2048 + 64 * s_
                for t in range(5):
                    def bias_fn(ps, bps, sb_, bsb, pT, bpT, a=a, t=t, s_=s_):
                        S.op('dve', lambda e: e.tensor_tensor(sb_[:, :64], ps[:, :64], bs[a][:, s_, t, :], ALU.add),
                             reads=[bps, B['bs'][a]], writes=[bsb])
                        S.op('act', lambda e: e.activation(pT[:, :64], sb_[:, :64], AF.Exp), reads=[bsb], writes=[bpT])

                    def out_fn(po, bpo, rc, brc, h=h, q0=q0):
                        S.op('dve', lambda e: e.tensor_tensor(OT[:, h, q0:q0 + 64], po[:, :64], rc[:, :64], ALU.mult),
                             reads=[bpo, brc], writes=[b_OT])

                    if t < 4:
                        kk, bkk = kcT[a][:, s_, t * P:(t + 1) * P], B['kcT'][a]
                        vv, bvv = vcb[a][:, s_, t, :], B['vcb'][a]
                    else:
                        kk, bkk = kT[a][:, 2048:2176], B['kT'][a]
                        vv, bvv = vt[a][:, 16, :], B['vt'][a]
                    attn_tile(C, kk, bkk, qT[a][:, q0:q0 + 64], B['qT'][a], 64, vv, bvv, bias_fn, R, t == 0, t == 4, out_fn)
        out_proj(C, es, OT, b_OT, C.wo_band)
        S.end()


def phase_final(C, ni):
    nc, S = C.nc, C.S
    with ExitStack() as es:
        ident = _sb(nc, es, "identf", [P, P], F32)
        b_ident = Buf("identf")
        gv = _sb(nc, es, "gvf", [P, DC], F32)
        b_gv = Buf("gvf")
        ones_f = _sb(nc, es, "ones_ff", [P, P], F32)
        ones_bf = _sb(nc, es, "ones_bff", [P, P], BF16)
        b_onesf, b_ones = Buf("onesf"), Buf("ones")
        xt = [_sb(nc, es, f"fxt{i}", [P, DC, P], F32) for i in range(2)]
        b_xt = [Buf(f"fxt{i}") for i in range(2)]
        yn = [_sb(nc, es, f"fyn{i}", [P, DC, P], F32) for i in range(2)]
        b_yn = [Buf(f"fyn{i}") for i in range(2)]
        yo = [_sb(nc, es, f"fyo{i}", [P, D], F32) for i in range(2)]
        b_yo = [Buf(f"fyo{i}") for i in range(2)]
        sq = [_sb(nc, es, f"fsq{i}", [P, 512], BF16) for i in range(2)]
        b_sq = [Buf("fsq0"), Buf("fsq1")]
        tmp = _sb(nc, es, "ftmp", [P, 512], F32)
        b_tmp = Buf("ftmp")
        rstd = [_sb(nc, es, f"frstd{i}", [P, P], F32) for i in range(2)]
        b_rstd = [Buf("frstd0"), Buf("frstd1")]
        pss = _ps(nc, es, "fpss")
        b_pss = Buf("fpss")
        pst = [_ps(nc, es, f"fps{i}") for i in range(4)]
        b_pst = [Buf(f"fps{i}") for i in range(4)]
        S.dma(ident[:], C.ident[:, :], b_ident, writes=[b_ident])
        S.dma(gv[:], C.nrm[ni], b_gv, writes=[b_gv])
        S.dma(ones_f[:], C.ones[:, :], b_onesf, writes=[b_onesf])
        S.op('dve', lambda e: e.tensor_copy(ones_bf[:], ones_f[:]), reads=[b_onesf], writes=[b_ones])
        k = 0
        for ch in range(NCH):
            x, bx = xt[ch % 2], b_xt[ch % 2]
            y, by = yn[ch % 2], b_yn[ch % 2]
            o, bo = yo[ch % 2], b_yo[ch % 2]
            r, br = rstd[ch % 2], b_rstd[ch % 2]
            S.dma(x[:], C.XT[:, :, ch * P:(ch + 1) * P].rearrange("c p t -> p c t"), bx, writes=[bx])
            rms_tile(C, lambda dc, x=x: x[:, dc, :], bx, P, sq, b_sq, pss, b_pss, ones_bf, b_ones,
                     tmp, b_tmp, r[:], br)
            for dc in range(DC):
                S.op('dve', lambda e, y=y, x=x, r=r, dc=dc: e.scalar_tensor_tensor(
                    y[:, dc, :], x[:, dc, :], gv[:, dc:dc + 1], r[:], ALU.mult, ALU.mult),
                    reads=[bx, b_gv, br], writes=[by])
            for g in range(4):
                ps, bps = pst[k % 4], b_pst[k % 4]
                k += 1
                for j in range(4):
                    dc = g * 4 + j
                    S.op('pe', lambda e, ps=ps, j=j, y=y, dc=dc: e.transpose(
                        ps[:, j * P:(j + 1) * P], y[:, dc, :], ident[:]), reads=[by, b_ident], writes=[bps])
                if g % 2 == 0:
                    S.op('dve', lambda e, ps=ps, o=o, g=g: e.tensor_copy(o[:, g * 512:(g + 1) * 512], ps[:]),
                         reads=[bps], writes=[bo])
                else:
                    S.op('act', lambda e, ps=ps, o=o, g=g: e.copy(o[:, g * 512:(g + 1) * 512], ps[:]),
                         reads=[bps], writes=[bo])
            dst = C.yp[ch * P:(ch + 1) * P, :] if ch < 16 else C.ys[:, :]
            S.dma(dst, o[:], bo, reads=[bo])
        S.end()


def build_nc(cfg):
    nc = bass.Bass("TRN2", target_bir_lowering=False)
    C = Ctx()
    C.nc = nc
    C.xp = nc.dram_tensor("xp", [2048, D], F32, kind="ExternalInput").ap()
    C.xs = nc.dram_tensor("xs", [P, D], F32, kind="ExternalInput").ap()
    C.win = nc.dram_tensor("win", [2 * DEPTH, FC, 2, P, D], F32, kind="ExternalInput").ap()
    C.wout = nc.dram_tensor("wout", [2 * DEPTH, FC, P, D], F32, kind="ExternalInput").ap()
    C.nrm = nc.dram_tensor("nrm", [3 * DEPTH + 1, P, DC], F32, kind="ExternalInput").ap()
    C.ident = nc.dram_tensor("ident", [P, P], F32, kind="ExternalInput").ap()
    C.ones = nc.dram_tensor("ones", [P, P], F32, kind="ExternalInput").ap()
    C.yp = nc.dram_tensor("yp", [2048, D], F32, kind="ExternalOutput").ap()
    C.ys = nc.dram_tensor("ys", [P, D], F32, kind="ExternalOutput").ap()
    C.XT = nc.dram_tensor("XT", [DC, P, T], F32, kind="Internal").ap()
    C.QT = nc.dram_tensor("QT", [16, P, T], BF16, kind="Internal").ap()
    C.KT = nc.dram_tensor("KT", [16, P, T], BF16, kind="Internal").ap()
    C.VTM = nc.dram_tensor("VTM", [16, NCH, P, P], BF16, kind="Internal").ap()
    C.wqkv_band = nc.dram_tensor("wqkv_band", [48, P, D], F32, kind="ExternalInput").ap()
    C.wo_band = nc.dram_tensor("wo_band", [DC, P, D], F32, kind="ExternalInput").ap()
    C.bbias_p = nc.dram_tensor("bbias_p", [16, 5, P, P], F32, kind="ExternalInput").ap()
    C.bbias_s = nc.dram_tensor("bbias_s", [16, 2, 5, P, 64], F32, kind="ExternalInput").ap()
    C.cbk = nc.dram_tensor("cbk", [2, 512, 16, P], F32, kind="ExternalInput").ap()
    C.cbv = nc.dram_tensor("cbv", [2, 512, 16, P], F32, kind="ExternalInput").ap()
    C.wqkv_fox = nc.dram_tensor("wqkv_fox", [48, P, D], F32, kind="ExternalInput").ap()
    C.wo_fox = nc.dram_tensor("wo_fox", [DC, P, D], F32, kind="ExternalInput").ap()
    C.wf_fox = nc.dram_tensor("wf_fox", [P, DC, 16], F32, kind="ExternalInput").ap()
    C.bf_fox = nc.dram_tensor("bf_fox", [P, NCH, 16], F32, kind="ExternalInput").ap()
    C.utri = nc.dram_tensor("utri", [2, P, P], F32, kind="ExternalInput").ap()
    C.fmask = nc.dram_tensor("fmask", [P, 3, P], F32, kind="ExternalInput").ap()
    C.cfk = nc.dram_tensor("cfk", [2, 4096, 16, P], F32, kind="ExternalInput").ap()
    C.cfv = nc.dram_tensor("cfv", [2, 4096, 16, P], F32, kind="ExternalInput").ap()
    C.cfl = nc.dram_tensor("cfl", [2, 4096, 16], F32, kind="ExternalInput").ap()
    C.LFS = nc.dram_tensor("LFS", [P, NCH, 16], F32, kind="Internal").ap()
    C.pfk = nc.dram_tensor("pfk", [2048, 16, P], F32, kind="ExternalOutput").ap()
    C.pfv = nc.dram_tensor("pfv", [2048, 16, P], F32, kind="ExternalOutput").ap()
    C.pfl = nc.dram_tensor("pfl", [2048, 16], F32, kind="ExternalOutput").ap()
    C.sfk = nc.dram_tensor("sfk", [P, 16, P], F32, kind="ExternalOutput").ap()
    C.sfv = nc.dram_tensor("sfv", [P, 16, P], F32, kind="ExternalOutput").ap()
    C.sfl = nc.dram_tensor("sfl", [P, 16], F32, kind="ExternalOutput").ap()
    C.ssmp = nc.dram_tensor("ssmp", [2, 3, P, 64], F32, kind="ExternalInput").ap()
    C.ssmb = nc.dram_tensor("ssmb", [2, 2, P, 64, 16], F32, kind="ExternalInput").ap()
    C.ssmc = nc.dram_tensor("ssmc", [2, 2, P, 64, 16], F32, kind="ExternalInput").ap()
    C.ssmd = nc.dram_tensor("ssmd", [2, P, DC], F32, kind="ExternalInput").ap()
    C.ssms = nc.dram_tensor("ssms", [2, 2, 2, P, 64], F32, kind="ExternalInput").ap()
    C.wglu = nc.dram_tensor("wglu", [2, DC, 2, P, D], F32, kind="ExternalInput").ap()
    C.pos1 = nc.dram_tensor("pos1", [P, 64], F32, kind="ExternalInput").ap()
    C.rmask = nc.dram_tensor("rmask", [P, T], F32, kind="ExternalInput").ap()
    C.BBT = nc.dram_tensor("BBT", [64, 2, P, P], BF16, kind="Internal").ap()
    C.CTS = nc.dram_tensor("CTS", [64, 3, P, P], BF16, kind="Internal").ap()
    C.TAB = nc.dram_tensor("TAB", [P, 64, 4, 64], F32, kind="Internal").ap()
    C.APOW = nc.dram_tensor("APOW", [P, 64, 10], F32, kind="Internal").ap()
    C.GT = nc.dram_tensor("GT", [DC, P, T], BF16, kind="Internal").ap()
    C.ssm_out = nc.dram_tensor("ssm_out", [2, 3, 2, 64, P], F32, kind="ExternalOutput").ap()
    C.pbk = nc.dram_tensor("pbk", [512, 16, P], F32, kind="ExternalOutput").ap()
    C.pbv = nc.dram_tensor("pbv", [512, 16, P], F32, kind="ExternalOutput").ap()
    C.sbk = nc.dram_tensor("sbk", [P, 16, P], F32, kind="ExternalOutput").ap()
    C.sbv = nc.dram_tensor("sbv", [P, 16, P], F32, kind="ExternalOutput").ap()
    with ExitStack() as es:
        C.S = Sched(nc, es)
        phase_load(C)
        for i in range(cfg.get('depth', DEPTH)):
            if cfg.get('ffn', True):
                for half in range(2):
                    phase_ffn(C, 2 * i, 3 * i, half)
            if i % 3 == 0 and cfg.get('ssm', True):
                phase_ssm_prep(C, i // 3)
                if not cfg.get('prep_only', False):
                    phase_ssm_scan(C, i // 3, 3 * i + 1)
                    phase_ssm_glu(C, i // 3)
            if i % 3 == 1 and cfg.get('fox', True):
                phase_attn_proj(C, 3 * i + 1, C.wqkv_fox, C.pfk, C.pfv, C.sfk, C.sfv, 0, fox=True)
                if not cfg.get('proj_only', False):
                    phase_fox_attn(C)
            if i % 3 == 2 and cfg.get('band', True):
                phase_attn_proj(C, 3 * i + 1, C.wqkv_band, C.pbk, C.pbv, C.sbk, C.sbv, 1536)
                if not cfg.get('proj_only', False):
                    phase_band_attn(C)
            if cfg.get('ffn', True):
                for half in range(2):
                    phase_ffn(C, 2 * i + 1, 3 * i + 2, half)
        phase_final(C, 3 * DEPTH)
    return nc


def host_prep(inp):
    H = {}
    win = np.empty((2 * DEPTH, FC, 2, P, D), np.float32)
    wout = np.empty((2 * DEPTH, FC, P, D), np.float32)
    for i in range(DEPTH):
        for k, nm in enumerate(('ffn1', 'ffn2')):
            w = np.asarray(inp[nm + '_w_in'][i]).reshape(DC, P, 2, FC, P)
            win[2 * i + k] = w.transpose(3, 2, 1, 0, 4).reshape(FC, 2, P, D)
            wout[2 * i + k] = np.asarray(inp[nm + '_w_out'][i]).reshape(FC, P, D)
    H['win'] = win
    H['wout'] = wout
    nrm = np.empty((3 * DEPTH + 1, P, DC), np.float32)
    for i in range(DEPTH):
        nrm[3 * i] = np.asarray(inp['ffn1_norm'][i]).reshape(DC, P).T
        nrm[3 * i + 1] = np.asarray(inp['mix_norm'][i]).reshape(DC, P).T
        nrm[3 * i + 2] = np.asarray(inp['ffn2_norm'][i]).reshape(DC, P).T
    nrm[3 * DEPTH] = np.asarray(inp['final_norm']).reshape(DC, P).T
    H['nrm'] = nrm
    H['ident'] = np.eye(P, dtype=np.float32)
    wb = np.asarray(inp['band_w_in'][0]).reshape(DC, P, 48, P)
    H['wqkv_band'] = np.ascontiguousarray(wb.transpose(2, 1, 0, 3)).reshape(48, P, D)
    wo = np.asarray(inp['band_w_out'][0]).reshape(16, P, DC, P)
    H['wo_band'] = np.ascontiguousarray(wo.transpose(2, 1, 0, 3)).reshape(DC, P, D)
    def pk(a):
        return np.ascontiguousarray(np.asarray(a, np.float32).reshape(64, 2, 64).transpose(1, 2, 0)).reshape(P, 64)
    H['ssmp'] = np.stack([np.stack([pk(inp['ssm_a_re'][j]), pk(inp['ssm_a_im'][j]),
                                    pk(np.repeat(np.asarray(inp['ssm_log_step'][j])[:, None], 64, 1))], 0) for j in range(2)], 0)
    def pkb(b):
        return np.ascontiguousarray(np.asarray(b, np.float32).reshape(64, 2, 64, 16).transpose(1, 2, 0, 3)).reshape(P, 64, 16)
    def pkc(c_):
        return np.ascontiguousarray(np.asarray(c_, np.float32).reshape(64, 2, 16, 64).transpose(1, 3, 0, 2)).reshape(P, 64, 16)
    H['ssmb'] = np.stack([np.stack([pkb(inp['ssm_b_re'][j]), pkb(inp['ssm_b_im'][j])], 0) for j in range(2)], 0)
    H['ssmc'] = np.stack([np.stack([pkc(inp['ssm_c_re'][j]), pkc(inp['ssm_c_im'][j])], 0) for j in range(2)], 0)
    H['ssmd'] = np.stack([np.asarray(inp['ssm_d'][j], np.float32).reshape(DC, P).T for j in range(2)], 0).copy()
    wg = np.stack([np.asarray(inp['ssm_w_glu'][j]).reshape(DC, P, 2, DC, P).transpose(3, 2, 1, 0, 4).reshape(DC, 2, P, D)
                   for j in range(2)], 0)
    H['wglu'] = np.ascontiguousarray(wg)
    H['pos1'] = np.ascontiguousarray(np.broadcast_to(np.arange(1, 65, dtype=np.float32), (P, 64)))
    rm = np.ones((P, T), np.float32)
    rm[:, ::64] = 0.0
    H['rmask'] = rm
    H['_pk'] = pk
    wfx = np.asarray(inp['fox_w_in'][0])
    H['wqkv_fox'] = np.ascontiguousarray(wfx[:, :3 * D].reshape(DC, P, 48, P).transpose(2, 1, 0, 3)).reshape(48, P, D)
    H['wf_fox'] = np.ascontiguousarray(wfx[:, 3 * D:].reshape(DC, P, 16).transpose(1, 0, 2))
    wo = np.asarray(inp['fox_w_out'][0]).reshape(16, P, DC, P)
    H['wo_fox'] = np.ascontiguousarray(wo.transpose(2, 1, 0, 3)).reshape(DC, P, D)
    H['bf_fox'] = np.ascontiguousarray(np.broadcast_to(np.asarray(inp['fox_b_f'][0], np.float32), (P, NCH, 16)))
    ii = np.arange(P)
    u1 = (ii[:, None] <= ii[None, :]).astype(np.float32)
    u2 = u1 * ((ii[:, None] // 64) == (ii[None, :] // 64))
    H['utri'] = np.stack([u1, u2.astype(np.float32)], 0)
    fm = np.zeros((P, 3, P), np.float32)
    fm[:, 0, :] = u1
    for s_ in range(2):
        kl = ii[:, None] - 64 * s_
        fm[:, 1 + s_, :64] = ((kl >= 0) & (kl < 64) & (kl <= np.arange(64)[None, :])).astype(np.float32)
    H['fmask'] = fm
    rb = np.asarray(inp['band_rel_bias'][0])
    kk = np.arange(P)[:, None]
    bp = np.empty((16, 5, P, P), np.float32)
    for t in range(5):
        qq = np.arange(P)[None, :]
        idx = np.clip(512 - 128 * t + qq - kk, -256, 256) + 256
        v = rb[:, idx]
        if t == 0:
            v = np.where(((kk < 64) & (qq >= 64))[None], np.float32(NEG), v)
        if t == 4:
            v = np.where(((kk >= 64) & (qq < 64))[None], np.float32(NEG), v)
        bp[:, t] = v
    H['bbias_p'] = bp
    bs = np.empty((16, 2, 5, P, 64), np.float32)
    qq = np.arange(64)[None, :]
    for t in range(4):
        idx = np.clip(512 + qq - 128 * t - kk, -256, 256) + 256
        bs[:, 0, t] = rb[:, idx]
        bs[:, 1, t] = rb[:, idx]
    for s_ in range(2):
        kl = kk - 64 * s_
        idx = np.clip(qq - kl, -256, 256) + 256
        v = rb[:, idx]
        bs[:, s_, 4] = np.where(((kl < 0) | (kl >= 64))[None], np.float32(NEG), v)
    H['bbias_s'] = bs
    H['ones'] = np.ones((P, P), np.float32)
    return H


def core_inputs(inp, H, c):
    m = {k_: v for k_, v in H.items() if not k_.startswith('_')}
    pk = H['_pk']
    m['ssms'] = np.stack([np.stack([np.stack([pk(inp['state_ssm_re'][j, 2 * c + s_]), pk(inp['state_ssm_im'][j, 2 * c + s_])], 0)
                                    for s_ in range(2)], 0) for j in range(2)], 0)
    m['xp'] = np.ascontiguousarray(inp['x_prompt'][c])
    m['xs'] = np.ascontiguousarray(np.asarray(inp['x_sample'][2 * c:2 * c + 2]).reshape(P, D))
    m['cfk'] = np.ascontiguousarray(inp['cache_fox_k'][0, 2 * c:2 * c + 2])
    m['cfv'] = np.ascontiguousarray(inp['cache_fox_v'][0, 2 * c:2 * c + 2])
    m['cfl'] = np.ascontiguousarray(inp['cache_fox_logf'][0, 2 * c:2 * c + 2])
    m['cbk'] = np.ascontiguousarray(inp['cache_band_k'][0, 2 * c:2 * c + 2])
    m['cbv'] = np.ascontiguousarray(inp['cache_band_v'][0, 2 * c:2 * c + 2])
    return m


def kernel(**inp):
    nc = build_nc({})
    H = host_prep(inp)
    in_maps = [core_inputs(inp, H, c) for c in range(NCORES)]
    res = run_bass_kernel_spmd(nc, in_maps, core_ids=list(range(NCORES)))
    R_ = res.results
    f32 = np.float32
    yp = np.stack([r['yp'] for r in R_], 0).astype(f32, copy=False)
    ys = np.concatenate([r['ys'].reshape(2, 64, D) for r in R_], 0).astype(f32, copy=False)
    so = [np.asarray(r['ssm_out']) for r in R_]
    p_re = np.stack([np.stack([so[c][j, 0, 0].reshape(128, 64) for c in range(NCORES)], 0) for j in range(2)], 0)
    p_im = np.stack([np.stack([so[c][j, 0, 1].reshape(128, 64) for c in range(NCORES)], 0) for j in range(2)], 0)
    s_re = np.stack([np.stack([so[c][j, 1 + s_, 0].reshape(128, 64) for c in range(NCORES) for s_ in range(2)], 0) for j in range(2)], 0)
    s_im = np.stack([np.stack([so[c][j, 1 + s_, 1].reshape(128, 64) for c in range(NCORES) for s_ in range(2)], 0) for j in range(2)], 0)
    stk = lambda nm: np.stack([np.asarray(r[nm]) for r in R_], 0)[None]
    cat = lambda nm, tail: np.concatenate([np.asarray(r[nm]).reshape((2, 64) + tail) for r in R_], 0)[None]
    return (yp, ys, p_re.astype(f32), p_im.astype(f32),
            stk('pfk'), stk('pfv'), stk('pfl'), stk('pbk'), stk('pbv'),
            s_re.astype(f32), s_im.astype(f32),
            cat('sfk', (16, P)), cat('sfv', (16, P)), cat('sfl', (16,)),
            cat('sbk', (16, P)), cat('sbv', (16, P)))
```
